# Optimizing a Trainium2 kernel written in Bass

```python
import math
import jax, jax.numpy as jnp
from jax import lax
import numpy as np

D_MODEL = 1024
BATCH = 8
SEQ = 4096
DEPTH = 1
DEC_BATCH = 2
DEC_SEQ = 8192
PAST_LEN = 128

D_MIX = 2 * D_MODEL
D_FOURIER = D_MIX // 4
N_FGROUPS = 8
F_GROUP = D_FOURIER // N_FGROUPS
D_SSM = D_MIX - D_FOURIER
HEAD_DIM = 64
N_HEADS = D_SSM // HEAD_DIM
N_BC_GROUPS = 4
HEADS_PER_GROUP = N_HEADS // N_BC_GROUPS
D_STATE = 128
D_CONV = 5
CONV_PAD = D_CONV // 2
CONV_DIM = D_SSM + 2 * N_BC_GROUPS * D_STATE
CHUNK = 128
D_PLE = 256
D_IN_PROJ = 2 * D_FOURIER + D_SSM + CONV_DIM + 2 * N_HEADS
EPS = 1e-6

kernel_name = 'hybrid_fnet_ssd_bidir_encoder'


def rms_norm(x, w):
    xf = x.astype(jnp.float32)
    y = xf * lax.rsqrt(jnp.mean(xf * xf, axis=-1, keepdims=True) + EPS)
    return (y * w.astype(jnp.float32)).astype(x.dtype)


def ssd_chunked(x, dt, A, Bm, Cm):
    b, l, g, r, p = x.shape
    n = Bm.shape[-1]
    c = l // CHUNK
    X = (x * dt[..., None]).reshape(b, c, CHUNK, g, r, p)
    a = (dt * A).reshape(b, c, CHUNK, g, r)
    Bc = Bm.reshape(b, c, CHUNK, g, n)
    Cc = Cm.reshape(b, c, CHUNK, g, n)
    a_cs = jnp.cumsum(a, axis=2)
    seg = a_cs[:, :, :, None] - a_cs[:, :, None, :]
    mask = jnp.tril(jnp.ones((CHUNK, CHUNK), dtype=bool))[:, :, None, None]
    decay = jnp.exp(jnp.where(mask, seg, -jnp.inf))
    cb = jnp.einsum('bclgn,bcsgn->bclsg', Cc, Bc)
    y_diag = jnp.einsum('bclsg,bclsgr,bcsgrp->bclgrp', cb, decay, X)
    decay_st = jnp.exp(a_cs[:, :, -1:] - a_cs)
    states = jnp.einsum('bcsgn,bcsgr,bcsgrp->bcgrpn', Bc, decay_st, X)
    chunk_decay = jnp.exp(a_cs[:, :, -1])

    def step(h, inp):
        s, d = inp
        return h * d[..., None, None] + s, h

    h0 = jnp.zeros((b, g, r, p, n), dtype=jnp.float32)
    _, prev = lax.scan(step, h0, (jnp.moveaxis(states, 1, 0), jnp.moveaxis(chunk_decay, 1, 0)))
    prev = jnp.moveaxis(prev, 0, 1)
    y_off = jnp.einsum('bclgn,bcgrpn,bclgr->bclgrp', Cc, prev, jnp.exp(a_cs))
    return (y_diag + y_off).reshape(b, l, g, r, p)


def mixer_layer(h, p, norm_w, w_in, w_fmix, conv_w, conv_b, a_log_f, a_log_b,
                dt_bias_f, dt_bias_b, d_skip, ssd_norm_w, w_out, w_ple_in, w_ple_gate):
    b, s, _ = h.shape
    f32 = jnp.float32
    u = rms_norm(h, norm_w)
    proj = u @ w_in
    z_f, u_f, z_s, xbc, dt_raw = jnp.split(
        proj, [D_FOURIER, 2 * D_FOURIER, 2 * D_FOURIER + D_SSM,
               2 * D_FOURIER + D_SSM + CONV_DIM], axis=-1)

    uf = u_f.astype(f32).reshape(b, s, N_FGROUPS, F_GROUP)
    four = jnp.fft.fft2(uf, axes=(1, 3), norm='ortho').real
    y_f = jnp.einsum('bsgc,gcd->bsgd', four, w_fmix.astype(f32)).reshape(b, s, D_FOURIER)
    y_f = (y_f * jax.nn.silu(z_f.astype(f32))).astype(h.dtype)

    xbc = lax.conv_general_dilated(
        xbc, conv_w[:, None, :].astype(xbc.dtype), window_strides=(1,),
        padding=[(CONV_PAD, CONV_PAD)], dimension_numbers=('NWC', 'WIO', 'NWC'),
        feature_group_count=CONV_DIM)
    xbc = jax.nn.silu(xbc.astype(f32) + conv_b.astype(f32))
    xs, Bm, Cm = jnp.split(xbc, [D_SSM, D_SSM + N_BC_GROUPS * D_STATE], axis=-1)
    xs = xs.reshape(b, s, N_BC_GROUPS, HEADS_PER_GROUP, HEAD_DIM)
    Bm = Bm.reshape(b, s, N_BC_GROUPS, D_STATE)
    Cm = Cm.reshape(b, s, N_BC_GROUPS, D_STATE)
    dt_f_raw, dt_b_raw = jnp.split(dt_raw.astype(f32), [N_HEADS], axis=-1)
    dt_f = jax.nn.softplus(dt_f_raw + dt_bias_f.astype(f32)).reshape(b, s, N_BC_GROUPS, HEADS_PER_GROUP)
    dt_b = jax.nn.softplus(dt_b_raw + dt_bias_b.astype(f32)).reshape(b, s, N_BC_GROUPS, HEADS_PER_GROUP)
    A_f = -jnp.exp(a_log_f.astype(f32)).reshape(N_BC_GROUPS, HEADS_PER_GROUP)
    A_b = -jnp.exp(a_log_b.astype(f32)).reshape(N_BC_GROUPS, HEADS_PER_GROUP)
    flip = lambda t: jnp.flip(t, axis=1)
    y_fwd = ssd_chunked(xs, dt_f, A_f, Bm, Cm)
    y_bwd = flip(ssd_chunked(flip(xs), flip(dt_b), A_b, flip(Bm), flip(Cm)))
    D = d_skip.astype(f32).reshape(N_BC_GROUPS, HEADS_PER_GROUP)[..., None]
    y_s = (y_fwd + y_bwd + D * xs).reshape(b, s, D_SSM)
    y_s = y_s * jax.nn.silu(z_s.astype(f32))
    yg = y_s.reshape(b, s, N_BC_GROUPS, D_SSM // N_BC_GROUPS)
    yg = yg * lax.rsqrt(jnp.mean(yg * yg, axis=-1, keepdims=True) + EPS)
    y_s = (yg.reshape(b, s, D_SSM) * ssd_norm_w.astype(f32)).astype(h.dtype)

    h = h + jnp.concatenate([y_f, y_s], axis=-1) @ w_out

    gate = jax.nn.sigmoid((h @ w_ple_gate).astype(f32))
    h = h + ((p @ w_ple_in).astype(f32) * gate).astype(h.dtype)
    return h


def encoder_trunk(x, p, norm_w, w_in, w_fmix, conv_w, conv_b, a_log_f, a_log_b,
                  dt_bias_f, dt_bias_b, d_skip, ssd_norm_w, w_out, w_ple_in, w_ple_gate,
                  final_norm_w):
    h = x
    for i in range(DEPTH):
        h = mixer_layer(h, p[i], norm_w[i], w_in[i], w_fmix[i], conv_w[i], conv_b[i],
                        a_log_f[i], a_log_b[i], dt_bias_f[i], dt_bias_b[i], d_skip[i],
                        ssd_norm_w[i], w_out[i], w_ple_in[i], w_ple_gate[i])
    return rms_norm(h, final_norm_w)


def _dt_bias(k):
    dt = jnp.exp(jax.random.uniform(k, (DEPTH, N_HEADS)) * (math.log(0.1) - math.log(0.001)) + math.log(0.001))
    return dt + jnp.log(-jnp.expm1(-dt))


def setup_inputs(seed: int = 0) -> dict:
    key = jax.random.key(seed)
    ks = jax.random.split(key, 20)
    nrm = jax.random.normal
    return {
        'x_prompt': nrm(ks[0], (BATCH, SEQ, D_MODEL), jnp.float32),
        'x_sample': nrm(ks[1], (DEC_BATCH, DEC_SEQ, D_MODEL), jnp.float32),
        'p_prompt': nrm(ks[2], (DEPTH, BATCH, SEQ, D_PLE), jnp.float32),
        'p_sample': nrm(ks[3], (DEPTH, DEC_BATCH, DEC_SEQ, D_PLE), jnp.float32),
        'norm_w': 1.0 + 0.02 * nrm(ks[4], (DEPTH, D_MODEL), jnp.float32),
        'w_in': nrm(ks[5], (DEPTH, D_MODEL, D_IN_PROJ), jnp.float32) * D_MODEL ** -0.5,
        'w_fmix': nrm(ks[6], (DEPTH, N_FGROUPS, F_GROUP, F_GROUP), jnp.float32) * F_GROUP ** -0.5,
        'conv_w': nrm(ks[7], (DEPTH, D_CONV, CONV_DIM), jnp.float32) * D_CONV ** -0.5,
        'conv_b': 0.02 * nrm(ks[8], (DEPTH, CONV_DIM), jnp.float32),
        'a_log_f': jnp.log(jax.random.uniform(ks[9], (DEPTH, N_HEADS), jnp.float32, 1.0, 16.0)),
        'a_log_b': jnp.log(jax.random.uniform(ks[10], (DEPTH, N_HEADS), jnp.float32, 1.0, 16.0)),
        'dt_bias_f': _dt_bias(ks[11]),
        'dt_bias_b': _dt_bias(ks[12]),
        'd_skip': 1.0 + 0.1 * nrm(ks[13], (DEPTH, N_HEADS), jnp.float32),
        'ssd_norm_w': 1.0 + 0.02 * nrm(ks[14], (DEPTH, D_SSM), jnp.float32),
        'w_out': nrm(ks[15], (DEPTH, D_MIX, D_MODEL), jnp.float32) * D_MIX ** -0.5,
        'w_ple_in': nrm(ks[16], (DEPTH, D_PLE, D_MODEL), jnp.float32) * D_PLE ** -0.5,
        'w_ple_gate': nrm(ks[17], (DEPTH, D_MODEL, D_MODEL), jnp.float32) * D_MODEL ** -0.5,
        'final_norm_w': 1.0 + 0.02 * nrm(ks[18], (D_MODEL,), jnp.float32),
    }


def reference(x_prompt, x_sample, p_prompt, p_sample, norm_w, w_in, w_fmix, conv_w, conv_b,
              a_log_f, a_log_b, dt_bias_f, dt_bias_b, d_skip, ssd_norm_w, w_out,
              w_ple_in, w_ple_gate, final_norm_w):
    y_prompt = encoder_trunk(x_prompt, p_prompt, norm_w, w_in, w_fmix, conv_w, conv_b,
                             a_log_f, a_log_b, dt_bias_f, dt_bias_b, d_skip, ssd_norm_w,
                             w_out, w_ple_in, w_ple_gate, final_norm_w)
    y_sample = encoder_trunk(x_sample, p_sample, norm_w, w_in, w_fmix, conv_w, conv_b,
                             a_log_f, a_log_b, dt_bias_f, dt_bias_b, d_skip, ssd_norm_w,
                             w_out, w_ple_in, w_ple_gate, final_norm_w)
    return (y_prompt, y_sample)
```

```python
import math
from contextlib import ExitStack
import numpy as np
import ml_dtypes
import concourse.bass as bass
import concourse.mybir as mybir
from concourse.bass_utils import run_bass_kernel_spmd

F32 = mybir.dt.float32
BF16 = mybir.dt.bfloat16
ALU = mybir.AluOpType
AF = mybir.ActivationFunctionType

D = 1024
NG = 4
GC = 1292
COL_Z, COL_UF, COL_XS, COL_B, COL_C, COL_DT = 0, 512, 640, 1024, 1152, 1280
EPS = 1e-6
SAME_ENGINE_RAW_SYNC = True
LIST_SCHEDULE = True
PRIO_CRITICAL = True
FILL_MIN_GAP = 700.0
FILL_SLACK = 250.0
FILL_COST = 240.0


class Op:
    __slots__ = ("eng", "fn", "deps", "odeps", "inc", "mile", "sem", "is_dma", "idx", "cost", "users", "npend", "ready", "fin", "blev")

    def __init__(self, eng, fn, is_dma=False, sem=None, cost=100.0):
        self.eng, self.fn, self.is_dma, self.sem, self.cost = eng, fn, is_dma, sem, cost
        self.deps = set()
        self.odeps = set()
        self.inc = False
        self.mile = None


class Sched:
    ENGS = ("sync", "scalar", "vector", "gpsimd", "tensor")

    def __init__(self, nc):
        self.nc = nc
        self.final = []
        self.seg = []
        self.last_w = {}
        self.readers = {}
        self.last_on = {}
        self.dmas_since = []
        self.excl_last = {}
        self.nops = 0
        self.kmap = {}
        self.filler = None
        self.nfill = 0

    def _xk(self, k):
        n = k if isinstance(k, str) else k[0]
        sfx = self.kmap.get(n)
        return k if sfx is None else (k, sfx)

    def _edge(self, op, d, raw):
        if d is op:
            return
        if d.is_dma or op.is_dma or d.eng != op.eng:
            op.deps.add(d)
        elif d.eng != "tensor" and SAME_ENGINE_RAW_SYNC:
            op.deps.add(d)
        else:
            op.odeps.add(d)

    def add(self, eng, fn, reads=(), writes=(), dma_sem=None, extra_deps=(), excl=(), cost=100.0):
        op = Op(eng, fn, dma_sem is not None, dma_sem, cost)
        op.idx = self.nops
        self.nops += 1
        if self.kmap:
            reads = [self._xk(k) for k in reads]
            writes = [self._xk(k) for k in writes]
        for k in excl:
            d = self.excl_last.setdefault(k, {})
            for e2, o2 in d.items():
                self._edge(op, o2, False)
            d[eng] = op
        for b in reads:
            w = self.last_w.get(b)
            if w is not None:
                self._edge(op, w, True)
        for b in writes:
            w = self.last_w.get(b)
            if w is not None:
                self._edge(op, w, False)
            for r in self.readers.get(b, ()):
                self._edge(op, r, False)
        for d in extra_deps:
            op.deps.add(d)
        for b in reads:
            self.readers.setdefault(b, []).append(op)
        for b in writes:
            self.last_w[b] = op
            self.readers[b] = []
        self.seg.append(op)
        if op.is_dma:
            self.dmas_since.append(op)
        else:
            self.last_on[eng] = op
        return op

    def _schedule_segment(self):
        import heapq
        seg = self.seg
        self.seg = []
        if not LIST_SCHEDULE:
            self.final.extend(seg)
            return
        inseg = set(id(o) for o in seg)
        for o in seg:
            o.users = []
            o.npend = 0
            o.ready = 0.0
        for o in seg:
            for d in list(o.deps) + list(o.odeps):
                if id(d) in inseg:
                    d.users.append(o)
                    o.npend += 1
        for o in reversed(seg):
            bl = 0.0
            for u in o.users:
                if u.blev > bl:
                    bl = u.blev
            o.blev = bl + o.cost
        if PRIO_CRITICAL:
            for o in seg:
                o.idx = (-o.blev, o.idx)
        future = {e: [] for e in self.ENGS}
        avail = {e: [] for e in self.ENGS}
        free = {e: 0.0 for e in self.ENGS}
        for o in seg:
            if o.npend == 0:
                heapq.heappush(future[o.eng], (0.0, o.idx, o))
        out = []
        n = len(seg)
        while len(out) < n:
            best = None
            for e in self.ENGS:
                fq, aq = future[e], avail[e]
                t0 = free[e]
                while fq and fq[0][0] <= t0:
                    r_, i_, o_ = heapq.heappop(fq)
                    heapq.heappush(aq, (i_, r_, o_))
                if aq:
                    st, key = t0, aq[0][0]
                elif fq:
                    st, key = fq[0][0], fq[0][1]
                else:
                    continue
                if best is None or (st, key) < (best[0], best[1]):
                    best = (st, key, e)
            st, key, e = best
            if avail[e]:
                o = heapq.heappop(avail[e])[2]
            else:
                o = heapq.heappop(future[e])[2]
            if e == "tensor" and self.filler is not None and free[e] > 0.0:
                gap = st - free[e]
                if gap > FILL_MIN_GAP:
                    for _ in range(min(int((gap - FILL_SLACK) / FILL_COST), 24)):
                        fo = Op("tensor", self.filler, False, None, FILL_COST)
                        fo.idx = -1
                        out.append(fo)
                        n += 1
                        self.nfill += 1
            if o.is_dma:
                free[e] = st + 60.0
                o.fin = st + o.cost
            else:
                free[e] = st + o.cost
                o.fin = free[e] + 40.0
            out.append(o)
            for u in o.users:
                u.npend -= 1
                if o.fin > u.ready:
                    u.ready = o.fin
                if u.npend == 0:
                    heapq.heappush(future[u.eng], (u.ready, u.idx, u))
        self.final.extend(out)

    def barrier(self):
        dmas = list(self.dmas_since)
        self.dmas_since = []
        self._schedule_segment()
        self.filler = None
        last = {}
        for o in self.final[::-1]:
            if not o.is_dma and o.eng not in last:
                last[o.eng] = o
                if len(last) == 4:
                    break
        deps = list(last.values()) + dmas
        for e in self.ENGS:
            self.add(e, lambda eng: eng.nop(), extra_deps=deps, cost=30.0)
        self._schedule_segment()
        self.last_w.clear()
        self.readers.clear()
        self.excl_last.clear()

    def emit(self, eng_sems, block):
        self._schedule_segment()
        ops = self.final
        for op in ops:
            for d in op.deps:
                if not d.is_dma:
                    d.inc = True
        cnt = {e: 0 for e in self.ENGS}
        dcnt = {}
        for op in ops:
            if op.is_dma:
                dcnt[op.sem] = dcnt.get(op.sem, 0) + 16
                op.mile = dcnt[op.sem]
            elif op.inc:
                cnt[op.eng] += 1
                op.mile = cnt[op.eng]
        by_eng = {e: [o for o in ops if o.eng == e] for e in self.ENGS}

        def run(eng_name, eng):
            waited = {}
            for op in by_eng[eng_name]:
                need = {}
                for d in op.deps:
                    s = d.sem if d.is_dma else eng_sems[d.eng]
                    if need.get(s, 0) < d.mile:
                        need[s] = d.mile
                for s, v in need.items():
                    if waited.get(s, 0) < v:
                        eng.wait_ge(s, v)
                        waited[s] = v
                ins = op.fn(eng)
                if op.is_dma:
                    ins.then_inc(op.sem, 16)
                elif op.inc:
                    ins.then_inc(eng_sems[eng_name], 1)

        @block.sync
        def _(e):
            run("sync", e)

        @block.scalar
        def _(e):
            run("scalar", e)

        @block.vector
        def _(e):
            run("vector", e)

        @block.gpsimd
        def _(e):
            run("gpsimd", e)

        @block.tensor
        def _(e):
            run("tensor", e)


def _consts(L, dual):
    C = L // 128
    Ch = C // 2
    j = np.arange(C)[:, None].astype(np.float64)
    k2 = np.arange(C)[None, :].astype(np.float64)
    if not dual:
        ang = 2 * np.pi * j * k2 / C
        cs = np.concatenate([np.cos(ang), np.sin(ang)], axis=1)
    else:
        cs = np.zeros((C, 2 * C))
        jj = np.arange(Ch)[:, None].astype(np.float64)
        rr = np.arange(Ch)[None, :].astype(np.float64)
        ang = 2 * np.pi * jj * rr / Ch
        cs[0:Ch, 0:Ch] = np.cos(ang)
        cs[0:Ch, C:C + Ch] = np.sin(ang)
        cs[Ch:C, Ch:C] = np.cos(ang)
        cs[Ch:C, C + Ch:2 * C] = np.sin(ang)
    p = np.arange(128)[:, None, None].astype(np.float64)
    kk2 = np.arange(C)[None, :, None].astype(np.float64)
    k1 = np.arange(128)[None, None, :].astype(np.float64)
    t2 = np.zeros((128, C, 2, 4, 128))
    if not dual:
        a2 = 2 * np.pi * ((p * (C * k1 + kk2)) % L) / L
        mc, ms = np.cos(a2), np.sin(a2)
        t2[:, :, 0] = np.stack([mc, ms, -ms, mc], axis=2)
    else:
        Lh = L // 2
        k1a = np.arange(128)[None, None, :]
        isA = (k1a < 64)
        freq = np.where(isA, C * k1 + kk2, C * (k1 - 64) + kk2)
        a2 = 2 * np.pi * ((p * freq) % Lh) / Lh
        mc, ms = np.cos(a2), np.sin(a2)
        full = np.stack([mc, ms, -ms, mc], axis=2) * math.sqrt(2.0)
        slotA = (np.arange(C) < Ch)[None, :, None, None]
        colA = isA[:, :, None, :] if isA.ndim == 3 else isA
        colA = np.broadcast_to((np.arange(128) < 64)[None, None, None, :], full.shape)
        own_mask = np.where(slotA, colA, ~colA)
        t2[:, :, 0] = np.where(own_mask, full, 0.0)
        t2[:, :, 1] = np.where(own_mask, 0.0, full)
    return cs.astype(ml_dtypes.bfloat16), np.ascontiguousarray(t2.reshape(128, C, 1024)).astype(ml_dtypes.bfloat16)


def _masks():
    k = np.arange(128)[:, None]
    s = np.arange(128)[None, :]
    m = np.stack([(k <= s), (k >= s), (k < s), (k > s), np.ones((128, 128), bool), np.eye(128, dtype=bool)], 0)
    c = np.arange(64)[:, None] * np.arange(64)[None, :]
    c64 = np.cos(2 * np.pi * c / 64)
    s64 = np.sin(2 * np.pi * c / 64)
    z = np.zeros((64, 64))
    cb = np.block([[c64, z], [z, c64]])
    sb = np.block([[s64, z], [z, s64]])
    out = []
    for t in (cb, sb):
        r = t.astype(np.float32)
        for _ in range(3):
            h = r.astype(ml_dtypes.bfloat16)
            out.append(h)
            r = (r - h.astype(np.float32)).astype(np.float32)
    return m.astype(ml_dtypes.bfloat16), np.stack(out, 0)


def build(Ls):
    nc = bass.Bass("TRN2", target_bir_lowering=False)
    LMAX = Ls
    seqs = [("s", Ls)]

    def din(name, shape, dt=F32):
        return nc.dram_tensor(name, shape, dt, kind="ExternalInput").ap()

    x_d = {"s": din("x_s", [Ls, D])}
    pl_d = {"s": din("pl_s", [Ls, 256])}
    y_d = {"s": nc.dram_tensor("y_s", [Ls, D], F32, kind="ExternalOutput").ap()}
    flag_d = din("flag", [1])
    win_d = din("w_in_g", [NG, D, GC])
    convw_d = din("convw_g", [NG, 128, 25])
    convb_d = din("convb_g", [NG, 128, 5])
    cbrow_d = din("cbrow_g", [NG, 640])
    dtb_d = din("dtb_g", [NG, 12])
    alog_d = din("alog_g", [NG, 12])
    dsk_d = din("dsk_g", [NG, 6])
    snw_d = din("snw_g", [NG, 384])
    wfm_d = din("wfm", [8, 64, 64])
    wout_d = din("w_out", [2048, D])
    wpi_d = din("w_ple_in", [256, D])
    wpg_d = din("w_ple_gate", [D, D])
    fnw_d = din("fnw", [D])
    normw_d = din("normw", [D])
    masks_d = din("masks", [6, 128, 128], BF16)
    c64_d = din("c64", [6, 128, 128], BF16)
    cs_d = {"s": din("cs_s", [Ls // 128, 2 * (Ls // 128)], BF16)}
    t2_d = {"s": din("t2_s", [128, Ls // 128, 1024], BF16)}
    UT = nc.dram_tensor("UT", [8, 128, LMAX], BF16).ap()
    YCAT = nc.dram_tensor("YCAT", [LMAX, 2048], BF16).ap()
    YF = nc.dram_tensor("YFs", [LMAX, 128], F32).ap()
    XSBs = nc.dram_tensor("XSBs", [LMAX, 512], BF16).ap()
    BCTs = nc.dram_tensor("BCTs", [2, 128, LMAX], BF16).ap()
    DTPs = nc.dram_tensor("DTPs", [LMAX // 512, 128, 96], F32).ap()

    es = ExitStack()
    with es:
        S = Sched(nc)
        eng_sems = {e: es.enter_context(nc.semaphore("s_" + e)) for e in Sched.ENGS}
        dsems = {}

        def dsem(name):
            if name not in dsems:
                dsems[name] = es.enter_context(nc.semaphore("d_" + name))
            return dsems[name]

        def banks(*aps):
            out = set()
            for a in aps:
                try:
                    nm = a.tensor.name
                except Exception:
                    continue
                if nm.startswith("pb"):
                    out.add(nm)
            return out

        def fsz(ap):
            n = 1
            for d_ in ap.shape[1:]:
                n *= d_
            return n

        def ecost(eng, ap, *ins):
            n = fsz(ap)
            slow = 1.0
            for a_ in ins:
                try:
                    if a_.ap[-1][0] == 0 and a_.ap[-1][1] > 1:
                        slow = 1.0
                except Exception:
                    pass
            if eng == "scalar":
                return 220.0 + n / 1.4
            if eng == "vector":
                return 70.0 + slow * n / 0.96
            return 120.0 + slow * n / 0.7

        def V(fn, r=(), w=()):
            return S.add("vector", fn, r, w, cost=100.0)

        def G(fn, r=(), w=()):
            return S.add("gpsimd", fn, r, w, cost=150.0)

        def DMA(out, in_, r, w, sem, **kw):
            nbytes = out.shape[0] * fsz(out) * (4 if out.dtype == F32 else 2)
            return S.add("sync", lambda e: e.dma_start(out=out, in_=in_, **kw), r, w, dma_sem=dsem(sem),
                         cost=2500.0 + nbytes / 60.0)

        def mm(out, lhsT, rhs, start, stop, r, w):
            bk = banks(out)
            return S.add("tensor", lambda e: e.matmul(out, lhsT=lhsT, rhs=rhs, start=start, stop=stop), r,
                         list(w) + [("accb", b_) for b_ in bk], excl=bk, cost=28.0 + fsz(rhs) * 0.45)

        def tr(out, in_, ident, r, w):
            bk = banks(out)
            return S.add("tensor", lambda e: e.transpose(out=out, in_=in_, identity=ident), r,
                         list(w) + [("accb", b_) for b_ in bk], excl=bk, cost=110.0)

        def act(eng, out, in_, func, r, w, **kw):
            return S.add(eng, lambda e: e.activation(out=out, in_=in_, func=func, **kw), r, w, excl=banks(out, in_),
                         cost=ecost(eng, out))

        def cp(eng, out, in_, r, w):
            if eng == "scalar":
                return S.add(eng, lambda e: e.activation(out=out, in_=in_, func=AF.Copy), r, w, excl=banks(out, in_),
                             cost=ecost(eng, out))
            return S.add(eng, lambda e: e.tensor_copy(out=out, in_=in_), r, w, excl=banks(out, in_), cost=ecost(eng, out))

        def tt(eng, out, in0, in1, op, r, w):
            return S.add(eng, lambda e: e.tensor_tensor(out=out, in0=in0, in1=in1, op=op), r, w, excl=banks(out, in0, in1),
                         cost=ecost(eng, out, in0, in1))

        def ts(eng, out, in0, s1, s2, op0, op1, r, w, **kw):
            if s2 is None:
                return S.add(eng, lambda e: e.tensor_scalar(out=out, in0=in0, scalar1=s1, scalar2=None, op0=op0, **kw), r, w,
                             excl=banks(out, in0), cost=ecost(eng, out))
            return S.add(eng, lambda e: e.tensor_scalar(out=out, in0=in0, scalar1=s1, scalar2=s2, op0=op0, op1=op1, **kw), r, w,
                         excl=banks(out, in0), cost=ecost(eng, out))

        def split3(eng, src, dst3, tmp, r, w):
            cp(eng, dst3[0], src, r, [w + "0"])
            tt(eng, tmp, src, dst3[0], ALU.subtract, list(r) + [w + "0"], [w + "t"])
            cp(eng, dst3[1], tmp, [w + "t"], [w + "1"])
            tt(eng, tmp, tmp, dst3[1], ALU.subtract, [w + "t", w + "1"], [w + "t"])
            cp(eng, dst3[2], tmp, [w + "t"], [w + "2"])

        def stt(eng, out, in0, scalar, in1, op0, op1, r, w):
            return S.add(eng, lambda e: e.scalar_tensor_tensor(out=out, in0=in0, scalar=scalar, in1=in1, op0=op0, op1=op1), r, w,
                         excl=banks(out, in0, in1), cost=ecost(eng, out))

        ARENA = 207 * 1024
        arena = es.enter_context(nc.sbuf_tensor("arena", [128, ARENA // 2], BF16))

        class Alloc:
            def __init__(self, base=0):
                self.off = base

            def get(self, shape, dt):
                n = 1
                for s_ in shape[1:]:
                    n *= s_
                nb = n * (4 if dt == F32 else 2)
                nb = (nb + 63) // 64 * 64
                assert self.off + nb <= ARENA, (self.off, nb)
                v = arena[:, self.off // 2:(self.off + nb) // 2]
                self.off += nb
                if dt == F32:
                    v = v.bitcast(F32)
                v = v[:, 0:n]
                if len(shape) == 3:
                    v = v.rearrange("p (a b) -> p a b", a=shape[1])
                elif len(shape) == 4:
                    v = v.rearrange("p (a b c) -> p a b c", a=shape[1], b=shape[2])
                if shape[0] < 128:
                    v = v[0:shape[0]]
                return v

        pers = Alloc(0)
        masksb = pers.get([128, 6, 128], BF16)
        bLE, bGE, bLT, bGT, bONE, identb = (masksb[:, i, :] for i in range(6))
        gtb, ltb = bGT, bLT
        c64 = pers.get([128, 6, 128], BF16)
        wf3 = pers.get([128, 3, 128], BF16)
        wftmp = pers.get([128, 128], F32)
        normw_col = pers.get([128, 8], F32)
        flag_col = pers.get([128, 1], F32)
        Wg = pers.get([128, 8, GC], BF16)
        convw = pers.get([128, 25], F32)
        convb = pers.get([128, 5], F32)
        Dg = pers.get([128, 25, 128], BF16)
        DgD = pers.get([128, 6, 128], BF16)
        convwh = pers.get([128, 25], F32)
        cbh = pers.get([128, 5], F32)
        cbrow = pers.get([1, 640], BF16)
        cbrow_f = pers.get([1, 640], F32)
        ones_row = pers.get([1, 512], BF16)
        dtb_bc = pers.get([128, 12], F32)
        A_bc = pers.get([128, 12], F32)
        dsk_bc = pers.get([128, 6], F32)
        snw_bc = pers.get([128, 384], F32)
        wfblk = pers.get([128, 128], F32)
        W1b = pers.get([128, 128], BF16)
        W2nb = pers.get([128, 128], BF16)
        HBASE = pers.off
        Hst = pers.get([128, LMAX // 128, 384], BF16)
        PBASE = pers.off

        pb = [es.enter_context(nc.psum_tensor("pb%d" % i, [128, 512], F32)) for i in range(8)]

        def pbf(i, a, b):
            return pb[i][:, a:b]

        def pbb(i, a, b):
            return pb[i][:, a // 2:b // 2].bitcast(BF16)

        block = es.enter_context(nc.Block())

        DMA(masksb, masks_d.rearrange("m k s -> k m s"), [], ["identb", "gtb", "ltb", "masksb", "masks"], "c0")
        DMA(c64, c64_d.rearrange("m k s -> k m s"), [], ["c64"], "c1")
        DMA(normw_col, normw_d.rearrange("(kt k) -> k kt", k=128), [], ["normw_col"], "c2", allow_slow_non_contiguous=True)
        DMA(flag_col, flag_d.partition_broadcast(128), [], ["flag"], "c3")
        V(lambda e: e.memset(ones_row, 1.0), [], ["ones_row"])
        S.barrier()

        def load_uT(seq, L, sc, buf, key, sem):
            t0 = 512 * sc - 2
            lo = max(t0, 0)
            hi = min(t0 + 516, L)
            src = UT[:, :, lo:hi].rearrange("kt k t -> k kt t")
            return DMA(buf[:, :, lo - t0:hi - t0], src, [("UT", sc - 1), ("UT", sc), ("UT", sc + 1)], [key], sem)

        def pass0(seq, L):
            al = Alloc(PBASE)
            xt = [al.get([128, D], F32) for _ in range(2)]
            junk = al.get([128, D], BF16)
            ub = [al.get([128, D], BF16) for _ in range(2)]
            ss = [al.get([128, 1], F32) for _ in range(2)]
            rstd = [al.get([128, 1], F32) for _ in range(2)]
            uts = [al.get([128, 8, 512], BF16) for _ in range(2)]
            NSC = L // 512
            for sc in range(NSC):
                for c in range(4):
                    j = 4 * sc + c
                    b = j % 2
                    DMA(xt[b], x_d[seq][128 * j:128 * (j + 1), :], [], [("xt", b)], "xt%d" % b)
                    act("scalar", junk, xt[b], AF.Square, [("xt", b)], ["junk", ("ss", b)], accum_out=ss[b])
                    ts("vector", rstd[b], ss[b], 1.0 / D, EPS, ALU.mult, ALU.add, [("ss", b)], [("rstd", b)])
                    V(lambda e, b=b: e.reciprocal(out=rstd[b], in_=rstd[b]), [("rstd", b)], [("rstd", b)])
                    act("scalar", rstd[b], rstd[b], AF.Sqrt, [("rstd", b)], [("rstd", b)])
                    act("scalar", ub[b], xt[b], AF.Copy, [("xt", b), ("rstd", b)], [("ub", b)], scale=rstd[b])
                    bank = j % 2
                    for kt in range(8):
                        tr(pbb(bank, 128 * kt, 128 * (kt + 1)), ub[b][:, 128 * kt:128 * (kt + 1)], identb,
                           [("ub", b), "identb"], [("pT", bank)])
                    cp("vector", uts[sc % 2][:, :, 128 * c:128 * (c + 1)],
                       pbb(bank, 0, 1024).rearrange("p (a b) -> p a b", a=8), [("pT", bank)], [("uts", sc % 2, c)])
                DMA(UT[:, :, 512 * sc:512 * (sc + 1)].rearrange("kt k t -> k kt t"), uts[sc % 2],
                    [("uts", sc % 2, c) for c in range(4)], [("UT", sc)], "uts%d" % (sc % 2))

        def load_group(g, L):
            import os
            dbg = os.environ.get("KDEBUG", "")
            al = Alloc(PBASE)
            wst = [al.get([128, GC], F32) for _ in range(2)]
            tmp12 = al.get([128, 12], F32)
            if not dbg or "w" in dbg:
              for kt in range(8):
                b = kt % 2
                DMA(wst[b], win_d[g, 128 * kt:128 * (kt + 1), :], [], [("wst", b)], "wst%d" % b)
                eng_ = "vector" if kt % 2 == 0 else "gpsimd"
                ts(eng_, Wg[:, kt, 512:GC], wst[b][:, 512:GC], normw_col[:, kt:kt + 1], None, ALU.mult, None,
                   [("wst", b), "normw_col"], [("Wg", kt)])
                ts(eng_, Wg[:, kt, 0:512], wst[b][:, 0:512], normw_col[:, kt:kt + 1], 0.5, ALU.mult, ALU.mult,
                   [("wst", b), "normw_col"], [("Wg", kt)])
            if not dbg or "c" in dbg:
              DMA(convw, convw_d[g], [], ["convw"], "c0")
              DMA(convb, convb_d[g], [], ["convb"], "c1")
              for i25 in range(25):
                  ts("vector" if i25 % 2 == 0 else "gpsimd", Dg[:, i25, :], identb, convw[:, i25:i25 + 1], 0.5, ALU.mult, ALU.mult,
                     ["identb", "convw"], [("Dg", i25)])
              ts("vector", convwh, convw, 0.5, None, ALU.mult, None, ["convw"], ["convwh"])
              ts("vector", cbh, convb, 0.5, None, ALU.mult, None, ["convb"], ["cbh"])
              DMA(cbrow_f, cbrow_d[g:g + 1, :], [], ["cbrow_f"], "c8")
              ts("vector", cbrow, cbrow_f, 0.5, None, ALU.mult, None, ["cbrow_f"], ["cbrow"])
              DMA(dtb_bc, dtb_d[g].partition_broadcast(128), [], ["dtb"], "c2")
              DMA(tmp12, alog_d[g].partition_broadcast(128), [], ["tmp12"], "c3")
              DMA(dsk_bc, dsk_d[g].partition_broadcast(128), [], ["dsk"], "c4")
              DMA(snw_bc, snw_d[g].partition_broadcast(128), [], ["snw"], "c5")
              for h_ in range(6):
                  ts("gpsimd", DgD[:, h_, :], identb, dsk_bc[:, h_:h_ + 1], None, ALU.mult, None, ["identb", "dsk"], [("DgD", h_)])
              act("scalar", A_bc, tmp12, AF.Exp, ["tmp12"], ["A_bc"])
              ts("vector", A_bc, A_bc, -1.0, None, ALU.mult, None, ["A_bc"], ["A_bc"])
            if not dbg or "f" in dbg:
              V(lambda e: e.memset(wfblk, 0.0), [], ["wfblk"])
              DMA(wfblk[0:64, 0:64], wfm_d[2 * g], [], ["wfblk"], "c6")
              DMA(wfblk[64:128, 64:128], wfm_d[2 * g + 1], [], ["wfblk"], "c7")
              sc_ = 1.0 / math.sqrt(64.0 * L)
              split3("vector", wfblk, [wf3[:, i, :] for i in range(3)], wftmp, ["wfblk"], "wf3")
              pairs = [(i, j_) for i in range(3) for j_ in range(3) if i + j_ <= 2]
              for m_ in range(2):
                  for n_, (i, j_) in enumerate(pairs):
                      mm(pbf(0, 128 * m_, 128 * (m_ + 1)), c64[:, 3 * m_ + i, :], wf3[:, j_, :], n_ == 0, n_ == len(pairs) - 1,
                         ["c64", "wf30", "wf31", "wf32"], ["w%dp" % (m_ + 1)])
              ts("vector", W1b, pbf(0, 0, 128), sc_, None, ALU.mult, None, ["w1p"], ["W1b"])
              ts("vector", W2nb, pbf(0, 128, 256), -sc_, None, ALU.mult, None, ["w2p"], ["W2nb"])
            S.barrier()

        def passF(seq, L):
            C = L // 128
            NSC = L // 512
            al = Alloc(PBASE)
            UF = al.get([128, C, 128], BF16)
            UTf = al.get([128, 128, 128], BF16)
            Asb = al.get([128, 2, C, 128], BF16)
            csb = al.get([128, 2 * C], BF16)
            uts = [al.get([128, 8, 516], BF16) for _ in range(2)]
            t2b = [al.get([128, 4, 1024], BF16) for _ in range(2)]
            pq = [al.get([128, 256], BF16) for _ in range(2)]
            yst = [al.get([128, 4, 128], F32) for _ in range(2)]
            DMA(csb[0:C, :], cs_d[seq], [], ["csb"], "c0")
            load_uT(seq, L, 0, uts[0], ("uts", 0), "uts0")
            for sc in range(NSC):
                if sc + 1 < NSC:
                    load_uT(seq, L, sc + 1, uts[(sc + 1) % 2], ("uts", (sc + 1) % 2), "uts%d" % ((sc + 1) % 2))
                u = uts[sc % 2]
                for c in range(4):
                    j = 4 * sc + c
                    bank = j % 2
                    for kt in range(8):
                        mm(pbf(bank, 0, 128), u[:, kt, 2 + 128 * c:2 + 128 * (c + 1)], Wg[:, kt, COL_UF:COL_UF + 128],
                           kt == 0, kt == 7, [("uts", sc % 2), ("Wg", kt)], [("pu", bank)])
                    cp("scalar" if j % 2 == 0 else "vector", UF[:, j, :], pbf(bank, 0, 128), [("pu", bank)], ["UF"])
            for cb in range(16):
                bank = 2 + cb % 2
                for i in range(8):
                    ch = 8 * cb + i
                    tr(pbb(bank, 128 * i, 128 * (i + 1))[0:C], UF[:, :, ch], identb, ["UF", "identb"], [("pt", bank)])
                cp("vector" if cb % 2 == 0 else "scalar", UTf[0:C, 8 * cb:8 * cb + 8, :],
                   pbb(bank, 0, 1024)[0:C].rearrange("p (a b) -> p a b", a=8), [("pt", bank)], [("UTf", cb)])
            nper = min(512 // (2 * C), 128)
            nb_ = 128 // nper
            for bi in range(nb_):
                bank = 4 + bi % 2
                for i in range(nper):
                    ch = bi * nper + i
                    mm(pbf(bank, 2 * C * i, 2 * C * (i + 1)), UTf[0:C, ch, :], csb[0:C, :], True, True,
                       [("UTf", ch // 8), "csb"], [("pa", bank)])
                cp("vector" if bi % 2 == 0 else "scalar", Asb[:, :, :, bi * nper:(bi + 1) * nper],
                   pbf(bank, 0, 2 * C * nper).rearrange("p (c r k) -> p r k c", c=nper, r=2),
                   [("pa", bank)], [("Asb", bi)])
            areads = [("Asb", bi) for bi in range(nb_)]
            NP = C // 4
            Ch = C // 2
            DMA(t2b[0], t2_d[seq][:, 0:4, :], [], [("t2b", 0)], "t2b0")
            for pc in range(NP):
                if pc + 1 < NP:
                    DMA(t2b[(pc + 1) % 2], t2_d[seq][:, 4 * (pc + 1):4 * (pc + 2), :], [], [("t2b", (pc + 1) % 2)],
                        "t2b%d" % ((pc + 1) % 2))
                tb = t2b[pc % 2]
                ybank = pc % 2
                for kk in range(4):
                    k2 = 4 * pc + kk
                    kp = (k2 + Ch) % C
                    sl = k2 % 2
                    o = pbf(6 + sl, 0, 256)
                    srcs = ((0, k2, 0), (1, k2, 256), (0, kp, 512), (1, kp, 768))
                    for n_, (ri, kq, off) in enumerate(srcs):
                        mm(o, Asb[:, ri, kq, :], tb[:, kk, off:off + 256], n_ == 0, n_ == 3,
                           areads + [("t2b", pc % 2)], [("ppq", sl)])
                    cp("scalar" if sl == 0 else "vector", pq[sl], o, [("ppq", sl)], [("pq", sl)])
                    yo = pbf(ybank, 128 * kk, 128 * (kk + 1))
                    mm(yo, pq[sl][:, 0:128], W1b, True, False, [("pq", sl), "W1b"], [("py", ybank, kk)])
                    mm(yo, pq[sl][:, 128:256], W2nb, False, True, [("pq", sl), "W2nb"], [("py", ybank, kk)])
                cp("vector" if pc % 2 == 0 else "scalar", yst[pc % 2],
                   pbf(ybank, 0, 512).rearrange("p (a b) -> p a b", a=4),
                   [("py", ybank, qq) for qq in range(4)], [("yst", pc % 2)])
                DMA(YF[0:L].rearrange("(k1 k2) d -> k1 k2 d", k2=C)[:, 4 * pc:4 * pc + 4, :], yst[pc % 2],
                    [("yst", pc % 2)], [("YF", pc)], "yst%d" % (pc % 2))
            S.barrier()

        def ssd_front(u, ukey, WIN, ACC, CV, cvk, sc, NSC, tiles, cbanks, ACCD):
            for ti, t in enumerate(tiles):
                bank = ti % 2
                col = COL_XS + 128 * t
                for kt in range(8):
                    mm(pbf(bank, 0, 512), Wg[:, kt, col:col + 128], u[:, kt, 2:514], kt == 0, kt == 7,
                       [ukey, ("Wg", kt)], [("pfm", bank)])
                cp("scalar", WIN[:, t, 2:514], pbf(bank, 0, 512), [("pfm", bank)], [("WIN", t)])
                for side, (a, b_) in enumerate(((0, 2), (514, 516))):
                    edge = (sc == 0 and side == 0) or (sc == NSC - 1 and side == 1)
                    if edge:
                        G(lambda e, t=t, a=a, b_=b_: e.memset(WIN[:, t, a:b_], 0.0), [], [("WIN", t)])
                    else:
                        hb = pbf(2, 4 * ti + 2 * side, 4 * ti + 2 * side + 2)
                        for kt in range(8):
                            mm(hb, Wg[:, kt, col:col + 128], u[:, kt, a:b_], kt == 0, kt == 7,
                               [ukey, ("Wg", kt)], [("phalo", ti, side)])
                        if (side == 0 and sc == NSC // 2) or (side == 1 and sc == NSC // 2 - 1):
                            ts("vector", WIN[:, t, a:b_], hb, flag_col[:, 0:1], None, ALU.mult, None,
                               [("phalo", ti, side), "flag"], [("WIN", t)])
                        else:
                            cp("vector", WIN[:, t, a:b_], hb, [("phalo", ti, side)], [("WIN", t)])
                if ti in (1, 3):
                    ab = ACCD[(ti // 2) % 2]
                    ak = ("ACCD", (ti // 2) % 2)
                    ts("vector", ab, WIN[:, t, 0:512], convwh[:, 5 * t:5 * t + 1], cbh[:, t:t + 1], ALU.mult, ALU.add,
                       [("WIN", t), "convwh", "cbh"], [ak])
                    for k in range(1, 5):
                        stt("vector", ab, WIN[:, t, k:k + 512], convwh[:, 5 * t + k:5 * t + k + 1], ab, ALU.mult, ALU.add,
                            [("WIN", t), "convwh", ak], [ak])
                    act("scalar", ACC[ti % 2], ab, AF.Tanh, [ak], [("TT", ti % 2)])
                    stt("vector", CV[:, t, :], ACC[ti % 2], 1.0, ab, ALU.add, ALU.mult, [("TT", ti % 2), ak], [(cvk, t)])
                    continue
                cbk = cbanks[(ti // 2) % 2]
                for k in range(5):
                    mm(pbf(cbk, 0, 512), Dg[:, 5 * t + k, :], WIN[:, t, k:k + 512], k == 0, False,
                       [("WIN", t), ("Dg", 5 * t + k)], [("pcv", cbk)])
                mm(pbf(cbk, 0, 512), cbrow[0:1, 128 * t:128 * (t + 1)], ones_row[0:1, 0:512], False, True,
                   ["cbrow", "ones_row"], [("pcv", cbk)])
                act("scalar", ACC[ti % 2], pbf(cbk, 0, 512), AF.Tanh, [("pcv", cbk)], [("TT", ti % 2)])
                stt("vector", CV[:, t, :], ACC[ti % 2], 1.0, pbf(cbk, 0, 512), ALU.add, ALU.mult,
                    [("TT", ti % 2), ("pcv", cbk)], [(cvk, t)])

        def dt_block(u, ukey, DTR, DTV, AV, dk, A3, ATMP):
            for c in range(4):
                for kt in range(8):
                    mm(pbf(2, 20 + 12 * c, 20 + 12 * (c + 1)), u[:, kt, 2 + 128 * c:2 + 128 * (c + 1)],
                       Wg[:, kt, COL_DT:COL_DT + 12], kt == 0, kt == 7, [ukey, ("Wg", kt)], [("pdt", c)])
            tt("vector", DTR, pbf(2, 20, 68).rearrange("p (a b) -> p a b", a=4),
               dtb_bc.unsqueeze(1).broadcast_to([128, 4, 12]), ALU.add, [("pdt", c) for c in range(4)] + ["dtb"], [dk + "r"])
            act("scalar", DTR, DTR, AF.Exp, [dk + "r"], [dk + "r"])
            act("scalar", DTV, DTR, AF.Ln, [dk + "r"], [dk + "v"], bias=1.0)
            tt("vector", AV, DTV, A_bc.unsqueeze(1).broadcast_to([128, 4, 12]), ALU.mult, [dk + "v", "A_bc"], [dk + "a"])
            split3("gpsimd", AV.rearrange("p a b -> p (a b)"), [A3[:, i, :] for i in range(3)], ATMP, [dk + "a"], dk + "a3")

        def to_token_major(CV, cvk, c, XSB, xk):
            for t in range(4):
                tr(pbb(2, 512 + 128 * t, 512 + 128 * (t + 1)), CV[:, t, 128 * c:128 * (c + 1)], identb, [(cvk, t), "identb"], ["ptm"])
            cp("vector", XSB, pbb(2, 512, 1024), ["ptm"], [xk])

        def passA(seq, L):
            NSC = L // 512
            al = Alloc(PBASE)
            two = lambda shape, dt: [al.get(shape, dt) for _ in range(2)]
            uts = two([128, 8, 516], BF16)
            WIN2 = two([128, 5, 516], BF16)
            ACC = two([128, 512], F32)
            ACCD = two([128, 512], F32)
            CV2 = two([128, 5, 512], BF16)
            XSB = two([128, 512], BF16)
            DTR2, DTV2, AV2 = two([128, 4, 12], F32), two([128, 4, 12], F32), two([128, 4, 12], F32)
            A32 = two([128, 3, 48], BF16)
            ATMP2 = two([128, 48], F32)
            E = two([128, 2, 12], F32)
            SCL = two([128, 6], F32)
            XD = two([128, 384], BF16)
            SC_LOCAL = ("WIN", "CV", "dtr", "dtv", "dta", "dta30", "dta31", "dta32", "dta3t")
            H = al.get([128, 384], F32)
            V(lambda e: e.memset(H, 0.0), [], ["H"])
            load_uT(seq, L, 0, uts[0], ("uts", 0), "uts0")
            for sc in range(NSC):
                if sc + 1 < NSC:
                    load_uT(seq, L, sc + 1, uts[(sc + 1) % 2], ("uts", (sc + 1) % 2), "uts%d" % ((sc + 1) % 2))
                u, ukey = uts[sc % 2], ("uts", sc % 2)
                q_ = sc % 2
                WIN, CV, DTR, DTV, AV, A3, ATMP = WIN2[q_], CV2[q_], DTR2[q_], DTV2[q_], AV2[q_], A32[q_], ATMP2[q_]
                S.kmap.update({n_: q_ for n_ in SC_LOCAL})
                ssd_front(u, ukey, WIN, ACC, CV, "CV", sc, NSC, [0, 1, 2, 3, 4], (6, 7), ACCD)
                dt_block(u, ukey, DTR, DTV, AV, "dt", A3, ATMP)
                DMA(BCTs[:, :, 512 * sc:512 * (sc + 1)].rearrange("t k l -> k t l"), CV[:, 3:5, :], [("CV", 3), ("CV", 4)],
                    [("BCTs", sc)], "bct%d" % q_)
                DMA(DTPs[sc, :, 0:48], DTV.rearrange("p a b -> p (a b)"), ["dtv"], [("DTPs", sc, 0)], "dtpa%d" % q_)
                DMA(DTPs[sc, :, 48:96], AV.rearrange("p a b -> p (a b)"), ["dta"], [("DTPs", sc, 1)], "dtpb%d" % q_)
                for c in range(4):
                    j = 4 * sc + c
                    b = j % 2
                    to_token_major(CV, "CV", c, XSB[b], ("XSB", b))
                    DMA(XSBs[128 * j:128 * (j + 1), :], XSB[b], [("XSB", b)], [("XSBs", j)], "xsbw%d" % b)
                    for mi, m_ in enumerate((bGT, bONE)):
                        for i3 in range(3):
                            mm(pbf(2, 68 + 12 * mi, 80 + 12 * mi), m_, A3[:, i3, 12 * c:12 * c + 12], i3 == 0, i3 == 2,
                               ["masksb", "dta30", "dta31", "dta32"], [("pcs", mi)])
                    act("scalar", E[b], pbf(2, 68, 92).rearrange("p (a b) -> p a b", a=2), AF.Exp,
                        [("pcs", 0), ("pcs", 1)], [("E", b)])
                    tt("vector", SCL[b], E[b][:, 0, 0:6], DTV[:, c, 0:6], ALU.mult, [("E", b), "dtv"], [("SCL", b)])
                    tt("gpsimd", XD[b].rearrange("p (h d) -> p h d", h=6), XSB[b][:, 0:384].rearrange("p (h d) -> p h d", h=6),
                       SCL[b].unsqueeze(2).broadcast_to([128, 6, 64]), ALU.mult, [("XSB", b), ("SCL", b)], [("XD", b)])
                    mm(pbf(4 + b, 0, 384), XSB[b][:, 384:512], XD[b], True, True, [("XSB", b), ("XD", b)], [("pst", b)])
                    if j == (L // 128) // 2:
                        ts("vector", H, H, flag_col[:, 0:1], None, ALU.mult, None, ["H", "flag"], ["H"])
                    cp("scalar", Hst[:, j, :], H, ["H"], [("Hst", j)])
                    tt("vector", H.rearrange("p (h d) -> p h d", h=6), H.rearrange("p (h d) -> p h d", h=6),
                       E[b][:, 1, 0:6].unsqueeze(2).broadcast_to([128, 6, 64]), ALU.mult, ["H", ("E", b)], ["H"])
                    tt("vector", H, H, pbf(4 + b, 0, 384), ALU.add, ["H", ("pst", b)], ["H"])
            S.kmap.clear()
            S.barrier()

        def passB(seq, L, g):
            NSC = L // 512
            al = Alloc(PBASE)
            two = lambda shape, dt: [al.get(shape, dt) for _ in range(2)]
            uts = two([128, 8, 516], BF16)
            WIN2 = two([128, 5, 516], BF16)
            ACC = two([128, 512], F32)
            CV2 = two([128, 5, 512], BF16)
            XSB = two([128, 512], BF16)
            SZ = two([128, 512], F32)
            TZ = two([128, 512], F32)
            DTR2, DTV2, AV2 = two([128, 4, 12], F32), two([128, 4, 12], F32), two([128, 4, 12], F32)
            A32 = two([128, 3, 48], BF16)
            ATMP2 = two([128, 48], F32)
            E = two([128, 4, 12], F32)
            SCLB2 = two([128, 6], F32)
            RR2 = [two([128, 2, 6, 128], BF16) for _ in range(2)]
            CBM2 = [two([128, 128], F32) for _ in range(2)]
            DEC = two([128, 256], F32)
            MT2 = two([128, 2, 6, 128], BF16)
            XDT2 = [two([128, 384], BF16) for _ in range(2)]
            XSD2 = two([128, 384], BF16)
            XDB2 = two([128, 384], BF16)
            Gs = al.get([128, 384], F32)
            Gb = al.get([128, 384], BF16)
            T12, T22, YS2 = two([128, 384], F32), two([128, 384], F32), two([128, 384], F32)
            junk2 = two([128, 384], BF16)
            YFR = two([128, 128], F32)
            YG8 = [al.get([128, 384], F32) for _ in range(8)]
            YC8 = [al.get([128, 512], BF16) for _ in range(8)]
            ssq8 = al.get([128, 8], F32)
            rs8 = al.get([128, 8], F32)
            SC_LOCAL = ("WIN", "CV", "dtr", "dtv", "dta", "dta30", "dta31", "dta32", "dta3t")
            CH_LOCAL = ("CBM", "MT", "XDT", "XSD", "XDB", "SCLB", "RR", "T1", "T2a", "T2b", "YS", "junkb")
            V(lambda e: e.memset(Gs, 0.0), [], ["Gs"])
            V(lambda e: e.memset(Gb, 0.0), [], ["Gb"])
            order = list(range(NSC - 1, -1, -1))
            load_uT(seq, L, order[0], uts[0], ("uts", 0), "uts0")
            for oi, sc in enumerate(order):
                if oi + 1 < NSC:
                    load_uT(seq, L, order[oi + 1], uts[(oi + 1) % 2], ("uts", (oi + 1) % 2), "uts%d" % ((oi + 1) % 2))
                u, ukey = uts[oi % 2], ("uts", oi % 2)
                q_ = oi % 2
                WIN, CV, DTR, DTV, AV, A3, ATMP = WIN2[q_], CV2[q_], DTR2[q_], DTV2[q_], AV2[q_], A32[q_], ATMP2[q_]
                S.kmap.update({n_: q_ for n_ in SC_LOCAL})
                BC = CV[:, 3:5, :]
                DMA(BC, BCTs[:, :, 512 * sc:512 * (sc + 1)].rearrange("t k l -> k t l"), [], [("CV", 3), ("CV", 4)], "bcr%d" % q_)
                DMA(DTV.rearrange("p a b -> p (a b)"), DTPs[sc, :, 0:48], [], ["dtv"], "dtra%d" % q_)
                DMA(AV.rearrange("p a b -> p (a b)"), DTPs[sc, :, 48:96], [], ["dta"], "dtrb%d" % q_)
                split3("gpsimd", AV.rearrange("p a b -> p (a b)"), [A3[:, i, :] for i in range(3)], ATMP, ["dta"], "dta3")
                for c in range(3, -1, -1):
                    j = 4 * sc + c
                    b = j % 2
                    S.kmap.update({n_: b for n_ in CH_LOCAL})
                    RR, CBM, MT, XDT, XSD, XDB, SCLB = RR2[b], CBM2[b], MT2[b], XDT2[b], XSD2[b], XDB2[b], SCLB2[b]
                    T1, T2_, YS, junk = T12[b], T22[b], YS2[b], junk2[b]
                    DMA(YFR[b], YF[128 * j:128 * (j + 1), :], [], [("YFR", b)], "yfr%d" % b)
                    for kt in range(8):
                        mm(pbf(b, 0, 512), u[:, kt, 2 + 128 * c:2 + 128 * (c + 1)], Wg[:, kt, COL_Z:COL_Z + 512],
                           kt == 0, kt == 7, [ukey, ("Wg", kt)], [("pfm", b)])
                    act("scalar", TZ[b], pbf(b, 0, 512), AF.Tanh, [("pfm", b)], [("TZ", b)])
                    stt("vector", SZ[b], TZ[b], 1.0, pbf(b, 0, 512), ALU.add, ALU.mult, [("TZ", b), ("pfm", b)], [("SZ", b)])
                    DMA(XSB[b], XSBs[128 * j:128 * (j + 1), :], [], [("XSB", b)], "xsbr%d" % b)
                    for mi, m_ in enumerate((bLE, bGE, bLT, bONE)):
                        for i3 in range(3):
                            mm(pbf(2, 68 + 12 * mi, 80 + 12 * mi), m_, A3[:, i3, 12 * c:12 * c + 12], i3 == 0, i3 == 2,
                               ["masksb", "dta30", "dta31", "dta32"], [("pcs", mi)])
                    act("scalar", E[b], pbf(2, 68, 116).rearrange("p (a b) -> p a b", a=4), AF.Exp,
                        [("pcs", mi) for mi in range(4)], [("E", b)])
                    ef, eb_, dstb, cdb = E[b][:, 0, 0:6], E[b][:, 1, 6:12], E[b][:, 2, 6:12], E[b][:, 3, 6:12]
                    for d_ in range(2):
                        msk = bLE if d_ == 0 else bGE
                        for hl in range(2):
                            tt("gpsimd", RR[d_][:, hl, :, :], msk.unsqueeze(1).broadcast_to([128, 6, 128]),
                               A3[:, hl, 12 * c + 6 * d_:12 * c + 6 * d_ + 6].unsqueeze(2).broadcast_to([128, 6, 128]), ALU.mult,
                               ["masks", "dta30", "dta31"], [("RR", d_, hl)])
                    mm(pbf(2, 116, 244), CV[:, 3, 128 * c:128 * (c + 1)], CV[:, 4, 128 * c:128 * (c + 1)], True, True,
                       [("CV", 3), ("CV", 4)], ["pcb"])
                    tt("vector", CBM[0], pbf(2, 116, 244), bLE, ALU.mult, ["pcb", "masks"], [("CBM", 0)])
                    tt("vector", CBM[1], pbf(2, 116, 244), bGE, ALU.mult, ["pcb", "masks"], [("CBM", 1)])
                    it = 0
                    for d_ in range(2):
                        lt_ = gtb if d_ == 0 else ltb
                        for hp in range(3):
                            slot = it % 2
                            o = pbf(6 + slot, 0, 256)
                            for hl in range(2):
                                mm(o, lt_, RR[d_][:, hl, 2 * hp:2 * hp + 2, :].rearrange("p a b -> p (a b)"),
                                   hl == 0, hl == 1, ["gtb", "ltb", ("RR", d_, hl)], [("pseg", slot)])
                            act("scalar", DEC[slot], o, AF.Exp, [("pseg", slot)], [("DEC", slot)])
                            for hh in range(2):
                                h_ = 2 * hp + hh
                                stt("vector", MT[:, d_, h_, :], DEC[slot][:, 128 * hh:128 * (hh + 1)],
                                    DTV[:, c, 6 * d_ + h_:6 * d_ + h_ + 1], CBM[d_], ALU.mult, ALU.mult,
                                    [("DEC", slot), ("CBM", d_), "dtv"], [("MT", d_, hp)])
                            it += 1
                    for h in range(6):
                        ybk = 4
                        yo = pbf(ybk, 64 * h, 64 * (h + 1))
                        xh = XSB[b][:, 64 * h:64 * (h + 1)]
                        mm(yo, MT[:, 0, h, :], xh, True, False, [("MT", 0, h // 2), ("XSB", b)], [("py", b, h)])
                        mm(yo, MT[:, 1, h, :], xh, False, False, [("MT", 1, h // 2), ("XSB", b)], [("py", b, h)])
                        mm(yo, DgD[:, h, :], xh, False, True, [("DgD", h), ("XSB", b)], [("py", b, h)])
                    mm(pbf(5, 0, 384), CV[:, 4, 128 * c:128 * (c + 1)], Hst[:, j, :], True, True, [("CV", 4), ("Hst", j)], ["pzf"])
                    tt("vector", T1.rearrange("p (h d) -> p h d", h=6), pbf(5, 0, 384).rearrange("p (h d) -> p h d", h=6),
                       ef.unsqueeze(2).broadcast_to([128, 6, 64]), ALU.mult, ["pzf", ("E", b)], ["T1"])
                    mm(pbf(5, 0, 384), CV[:, 4, 128 * c:128 * (c + 1)], Gb, True, True, [("CV", 4), "Gb"], ["pzf"])
                    tt("vector", T2_.rearrange("p (h d) -> p h d", h=6), pbf(5, 0, 384).rearrange("p (h d) -> p h d", h=6),
                       eb_.unsqueeze(2).broadcast_to([128, 6, 64]), ALU.mult, ["pzf", ("E", b)], ["T2a", "T2b"])
                    tt("gpsimd", T1, T1, T2_, ALU.add, ["T1", "T2a", "T2b"], ["T1"])
                    tt("vector", YS, pbf(4, 0, 384), T1, ALU.add, [("py", b, h) for h in range(6)] + ["T1"], ["YS"])
                    tt("vector", SCLB, dstb, DTV[:, c, 6:12], ALU.mult, [("E", b), "dtv"], ["SCLB"])
                    tt("gpsimd", XDB.rearrange("p (h d) -> p h d", h=6), XSB[b][:, 0:384].rearrange("p (h d) -> p h d", h=6),
                       SCLB.unsqueeze(2).broadcast_to([128, 6, 64]), ALU.mult, [("XSB", b), "SCLB"], ["XDB"])
                    mm(pbf(3, 0, 384), XSB[b][:, 384:512], XDB, True, True, [("XSB", b), "XDB"], ["pst3"])
                    tt("vector", Gs.rearrange("p (h d) -> p h d", h=6), Gs.rearrange("p (h d) -> p h d", h=6),
                       cdb.unsqueeze(2).broadcast_to([128, 6, 64]), ALU.mult, ["Gs", ("E", b)], ["Gs"])
                    tt("vector", Gs, Gs, pbf(3, 0, 384), ALU.add, ["Gs", "pst3"], ["Gs"])
                    if j == (L // 128) // 2:
                        ts("vector", Gs, Gs, flag_col[:, 0:1], None, ALU.mult, None, ["Gs", "flag"], ["Gs"])
                    cp("scalar", Gb, Gs, ["Gs"], ["Gb"])
                    sl8 = 4 * q_ + c
                    tt("vector", YG8[sl8], YS, SZ[b][:, 128:512], ALU.mult, ["YS", ("SZ", b)], [("YG8", sl8)])
                    act("scalar", junk, YG8[sl8], AF.Square, [("YG8", sl8)], ["junkb", ("ssq8", sl8)], accum_out=ssq8[:, sl8:sl8 + 1])
                    tt("vector", YC8[sl8][:, 0:128], YFR[b], SZ[b][:, 0:128], ALU.mult, [("YFR", b), ("SZ", b)], [("YC8", sl8, 0)])
                    DMA(YCAT[128 * j:128 * (j + 1), 128 * g:128 * (g + 1)], YC8[sl8][:, 0:128], [("YC8", sl8, 0)], [("YCAT", j, 0)],
                        "yca%d" % sl8)
                rs4 = rs8[:, 4 * q_:4 * q_ + 4]
                ts("vector", rs4, ssq8[:, 4 * q_:4 * q_ + 4], 1.0 / 384, EPS, ALU.mult, ALU.add,
                   [("ssq8", 4 * q_ + c_) for c_ in range(4)], [("rs8", q_)])
                V(lambda e, rs4=rs4: e.reciprocal(out=rs4, in_=rs4), [("rs8", q_)], [("rs8", q_)])
                act("scalar", rs4, rs4, AF.Sqrt, [("rs8", q_)], [("rs8", q_)])
                for c in range(4):
                    j = 4 * sc + c
                    sl8 = 4 * q_ + c
                    stt("vector", YC8[sl8][:, 128:512], YG8[sl8], rs8[:, sl8:sl8 + 1], snw_bc, ALU.mult, ALU.mult,
                        [("YG8", sl8), ("rs8", q_), "snw"], [("YC8", sl8, 1)])
                    DMA(YCAT[128 * j:128 * (j + 1), 512 + 384 * g:512 + 384 * (g + 1)], YC8[sl8][:, 128:512], [("YC8", sl8, 1)],
                        [("YCAT", j, 1)], "ycb%d" % sl8)
            S.kmap.clear()
            S.barrier()

        def load_tail_weights():
            al = Alloc(HBASE)
            wo = al.get([128, 16, D], BF16)
            wg_ = al.get([128, 8, D], BF16)
            wp = al.get([128, 2, D], BF16)
            fnw = al.get([128, D], F32)
            wst = [al.get([128, D], F32) for _ in range(2)]
            i = 0
            for dst, src, n in ((wo, wout_d, 16), (wg_, wpg_d, 8), (wp, wpi_d, 2)):
                for kt in range(n):
                    b = i % 2
                    DMA(wst[b], src[128 * kt:128 * (kt + 1), :], [], [("wst", b)], "wst%d" % b)
                    if dst is wp:
                        ts("vector" if i % 2 == 0 else "gpsimd", dst[:, kt, :], wst[b], 0.5, None, ALU.mult, None, [("wst", b)], [("tw", i)])
                    else:
                        cp("vector" if i % 2 == 0 else "gpsimd", dst[:, kt, :], wst[b], [("wst", b)], [("tw", i)])
                    i += 1
            DMA(fnw, fnw_d.partition_broadcast(128), [], ["fnw"], "c0")
            S.barrier()
            return wo, wg_, wp, fnw, al

        def tail(seq, L, wo, wg_, wp, fnw, al):
            xt = [al.get([128, D], F32) for _ in range(2)]
            yc = [al.get([128, 2048], BF16) for _ in range(2)]
            pt = [al.get([128, 256], F32) for _ in range(2)]
            two = lambda shape, dt: [al.get(shape, dt) for _ in range(2)]
            ptb2, pT2, ycT2 = two([128, 256], BF16), two([128, 2, 128], BF16), two([128, 16, 128], BF16)
            h12, h1b2, h1T2 = two([128, D], F32), two([128, D], BF16), two([128, 8, 128], BF16)
            gate2, junk2 = two([128, D], F32), two([128, D], BF16)
            tq4 = [al.get([128, D], F32) for _ in range(4)]
            ssq4 = al.get([128, 4], F32)
            rs4 = al.get([128, 4], F32)
            T_LOCAL = ("ptb", "pT", "ycT", "h1", "h1b", "h1T", "gate", "junkb")
            ot = [al.get([128, D], F32) for _ in range(2)]
            for j in range(L // 128):
                b = j % 2
                S.kmap.update({n_: b for n_ in T_LOCAL})
                ptb, pT, ycT, h1, h1b, h1T = ptb2[b], pT2[b], ycT2[b], h12[b], h1b2[b], h1T2[b]
                gate, junk = gate2[b], junk2[b]
                s4 = j % 4
                tq = tq4[s4]
                DMA(xt[b], x_d[seq][128 * j:128 * (j + 1), :], [], [("xt", b)], "xt%d" % b)
                DMA(yc[b], YCAT[128 * j:128 * (j + 1), :], [], [("yc", b)], "ycl%d" % b)
                DMA(pt[b], pl_d[seq][128 * j:128 * (j + 1), :], [], [("pt", b)], "pt%d" % b)
                for half in range(2):
                    for i in range(8):
                        kt = 8 * half + i
                        tr(pbb(half, 128 * i, 128 * (i + 1)), yc[b][:, 128 * kt:128 * (kt + 1)], identb,
                           [("yc", b), "identb"], [("ptr", half)])
                    cp("vector" if half == 0 else "scalar", ycT[:, 8 * half:8 * half + 8, :],
                       pbb(half, 0, 1024).rearrange("p (a b) -> p a b", a=8), [("ptr", half)], [("ycT", half)])
                for nh in range(2):
                    for kt in range(16):
                        mm(pbf(2 + nh, 0, 512), ycT[:, kt, :], wo[:, kt, 512 * nh:512 * (nh + 1)], kt == 0, kt == 15,
                           [("ycT", kt // 8)], [("ph", nh)])
                    tt("vector", h1[:, 512 * nh:512 * (nh + 1)], pbf(2 + nh, 0, 512), xt[b][:, 512 * nh:512 * (nh + 1)], ALU.add,
                       [("ph", nh), ("xt", b)], [("h1", nh)])
                cp("scalar", h1b, h1, [("h1", 0), ("h1", 1)], ["h1b"])
                for kt in range(8):
                    tr(pbb(4, 128 * kt, 128 * (kt + 1)), h1b[:, 128 * kt:128 * (kt + 1)], identb, ["h1b", "identb"], ["ph1T"])
                cp("vector", h1T, pbb(4, 0, 1024).rearrange("p (a b) -> p a b", a=8), ["ph1T"], ["h1T"])
                cp("gpsimd", ptb, pt[b], [("pt", b)], ["ptb"])
                for kt in range(2):
                    tr(pbb(5, 128 * kt, 128 * (kt + 1)), ptb[:, 128 * kt:128 * (kt + 1)], identb, ["ptb", "identb"], ["ppT"])
                cp("vector", pT, pbb(5, 0, 256).rearrange("p (a b) -> p a b", a=2), ["ppT"], ["pT"])
                for nh in range(2):
                    for kt in range(8):
                        mm(pbf(6 + nh, 0, 512), h1T[:, kt, :], wg_[:, kt, 512 * nh:512 * (nh + 1)], kt == 0, kt == 7,
                           ["h1T"], [("pg", nh)])
                    act("scalar", gate[:, 512 * nh:512 * (nh + 1)], pbf(6 + nh, 0, 512), AF.Tanh, [("pg", nh)], [("gate", nh)], scale=0.5)
                    for kt in range(2):
                        mm(pbf(2 + nh, 0, 512), pT[:, kt, :], wp[:, kt, 512 * nh:512 * (nh + 1)], kt == 0, kt == 1,
                           ["pT"], [("ph", nh)])
                    stt("vector", tq[:, 512 * nh:512 * (nh + 1)], gate[:, 512 * nh:512 * (nh + 1)], 1.0, pbf(2 + nh, 0, 512),
                        ALU.add, ALU.mult, [("ph", nh), ("gate", nh)], [("tq4", s4, nh)])
                    tt("vector", tq[:, 512 * nh:512 * (nh + 1)], tq[:, 512 * nh:512 * (nh + 1)], h1[:, 512 * nh:512 * (nh + 1)],
                       ALU.add, [("tq4", s4, nh), ("h1", nh)], [("tq4", s4, nh)])
                act("scalar", junk, tq, AF.Square, [("tq4", s4, 0), ("tq4", s4, 1)], ["junkb", ("ssq4", s4)], accum_out=ssq4[:, s4:s4 + 1])
                if j % 2 == 1:
                    pr = (s4 // 2) * 2
                    rsp = rs4[:, pr:pr + 2]
                    ts("vector", rsp, ssq4[:, pr:pr + 2], 1.0 / D, EPS, ALU.mult, ALU.add, [("ssq4", pr), ("ssq4", pr + 1)], [("rs4", pr)])
                    V(lambda e, rsp=rsp: e.reciprocal(out=rsp, in_=rsp), [("rs4", pr)], [("rs4", pr)])
                    act("scalar", rsp, rsp, AF.Sqrt, [("rs4", pr)], [("rs4", pr)])
                    for jj in (j - 1, j):
                        sj = jj % 4
                        bb = jj % 2
                        stt("vector", ot[bb], tq4[sj], rs4[:, sj:sj + 1], fnw, ALU.mult, ALU.mult,
                            [("tq4", sj, 0), ("tq4", sj, 1), ("rs4", pr), "fnw"], [("ot", bb)])
                        DMA(y_d[seq][128 * jj:128 * (jj + 1), :], ot[bb], [("ot", bb)], [], "ot%d" % bb)
            S.kmap.clear()
            S.barrier()

        import os
        dbg = os.environ.get("KDEBUG", "")
        for seq, L in seqs:
            pass0(seq, L)
            S.barrier()
            for g in range(NG):
                if dbg and g > 0 and "4" not in dbg:
                    continue
                if dbg and "g" not in dbg:
                    continue
                load_group(g, L)
                if not dbg or "F" in dbg:
                    passF(seq, L)
                if not dbg or "A" in dbg:
                    passA(seq, L)
                if not dbg or "B" in dbg:
                    passB(seq, L, g)
            if not dbg or "T" in dbg:
                tw = load_tail_weights()
                tail(seq, L, *tw)
        S.emit(eng_sems, block)
    return nc


_CACHE = {}


def _prep_weights(w_in, conv_w, conv_b, a_log_f, a_log_b, dt_bias_f, dt_bias_b, d_skip, ssd_norm_w):
    w = w_in[0]
    cols = []
    for g in range(NG):
        idx = np.concatenate([
            np.arange(128 * g, 128 * g + 128),
            1024 + np.arange(384 * g, 384 * g + 384),
            512 + np.arange(128 * g, 128 * g + 128),
            2560 + np.arange(384 * g, 384 * g + 384),
            2560 + 1536 + np.arange(128 * g, 128 * g + 128),
            2560 + 2048 + np.arange(128 * g, 128 * g + 128),
            5120 + np.arange(6 * g, 6 * g + 6),
            5144 + np.arange(6 * g, 6 * g + 6),
        ])
        cols.append(idx)
    w_in_g = np.ascontiguousarray(np.stack([w[:, c] for c in cols], 0))
    cw, cbias = conv_w[0], conv_b[0]
    convw_g = np.zeros((NG, 128, 25), np.float32)
    convb_g = np.zeros((NG, 128, 5), np.float32)
    for g in range(NG):
        ch = np.concatenate([np.arange(384 * g, 384 * g + 384), 1536 + np.arange(128 * g, 128 * g + 128),
                             2048 + np.arange(128 * g, 128 * g + 128)])
        for t in range(5):
            cht = ch[128 * t:128 * (t + 1)]
            convw_g[g, :, 5 * t:5 * t + 5] = cw[:, cht].T
            convb_g[g, :, t] = cbias[cht]
    cbrow_g = np.ascontiguousarray(convb_g.transpose(0, 2, 1).reshape(NG, 640))
    sl = lambda v, g: v[0][6 * g:6 * g + 6]
    dtb_g = np.stack([np.concatenate([sl(dt_bias_f, g), sl(dt_bias_b, g)]) for g in range(NG)], 0)
    alog_g = np.stack([np.concatenate([sl(a_log_f, g), sl(a_log_b, g)]) for g in range(NG)], 0)
    dsk_g = np.stack([sl(d_skip, g) for g in range(NG)], 0)
    snw_g = np.ascontiguousarray(ssd_norm_w[0].reshape(NG, 384))
    return dict(w_in_g=w_in_g, convw_g=convw_g, convb_g=convb_g, cbrow_g=cbrow_g, dtb_g=np.ascontiguousarray(dtb_g, np.float32),
                alog_g=np.ascontiguousarray(alog_g, np.float32), dsk_g=np.ascontiguousarray(dsk_g, np.float32), snw_g=snw_g)


def run(x_prompt, x_sample, p_prompt, p_sample, norm_w, w_in, w_fmix, conv_w, conv_b, a_log_f, a_log_b,
        dt_bias_f, dt_bias_b, d_skip, ssd_norm_w, w_out, w_ple_in, w_ple_gate, final_norm_w):
    f = lambda a: np.ascontiguousarray(np.asarray(a, dtype=np.float32))
    x_prompt, x_sample, p_prompt, p_sample = f(x_prompt), f(x_sample), f(p_prompt), f(p_sample)
    Bp, Lp, _ = x_prompt.shape
    Bs, Ls, _ = x_sample.shape
    assert Ls == 2 * Lp
    if Ls not in _CACHE:
        _CACHE[Ls] = build(Ls)
    nc = _CACHE[Ls]
    common = _prep_weights(f(w_in), f(conv_w), f(conv_b), f(a_log_f), f(a_log_b), f(dt_bias_f), f(dt_bias_b),
                           f(d_skip), f(ssd_norm_w))
    masks, c64 = _masks()
    cs1, t21 = _consts(Ls, False)
    cs2, t22 = _consts(Ls, True)
    common.update(wfm=f(w_fmix)[0], w_out=f(w_out)[0], w_ple_in=f(w_ple_in)[0], w_ple_gate=f(w_ple_gate)[0],
                  fnw=f(final_norm_w), normw=f(norm_w)[0], masks=masks, c64=c64)
    import os
    n_dual = int(os.environ.get("KNDUAL", 8 - Bs))
    slots = []
    for c in range(Bs):
        slots.append(("s", c, None))
    pr = list(range(Bp))
    pairs = [[] for _ in range(n_dual)]
    for i, b in enumerate(pr):
        pairs[i % n_dual].append(b)
    assert all(len(p_) <= 2 for p_ in pairs), "prompt batch does not fit the slots"
    for p_ in pairs:
        a = p_[0] if len(p_) > 0 else 0
        b = p_[1] if len(p_) > 1 else a
        slots.append(("p", a, b if len(p_) > 1 else None, b))
    in_maps = []
    for sl in slots:
        m = dict(common)
        if sl[0] == "s":
            m["x_s"] = x_sample[sl[1]]
            m["pl_s"] = p_sample[0, sl[1]]
            m["cs_s"], m["t2_s"] = cs1, t21
            m["flag"] = np.ones((1,), np.float32)
        else:
            a, b = sl[1], sl[3]
            m["x_s"] = np.ascontiguousarray(np.concatenate([x_prompt[a], x_prompt[b]], 0))
            m["pl_s"] = np.ascontiguousarray(np.concatenate([p_prompt[0, a], p_prompt[0, b]], 0))
            m["cs_s"], m["t2_s"] = cs2, t22
            m["flag"] = np.zeros((1,), np.float32)
        in_maps.append(m)
    while len(in_maps) < 8:
        in_maps.append(in_maps[-1])
    res = run_bass_kernel_spmd(nc, in_maps, core_ids=list(range(8)))
    ys = np.stack([res.results[c]["y_s"] for c in range(Bs)], 0)
    yp = np.zeros((Bp, Lp, D), np.float32)
    for c, sl in enumerate(slots):
        if sl[0] == "p":
            o = res.results[c]["y_s"]
            yp[sl[1]] = o[:Lp]
            if sl[2] is not None:
                yp[sl[2]] = o[Lp:]
    return yp, ys.astype(np.float32)


def kernel(**inputs):
    return run(**inputs)
```

```python
import math
from contextlib import ExitStack
import numpy as np
import ml_dtypes
import concourse.bass as bass
import concourse.mybir as mybir
from concourse.bass_utils import run_bass_kernel_spmd

F32 = mybir.dt.float32
BF16 = mybir.dt.bfloat16
ALU = mybir.AluOpType
AF = mybir.ActivationFunctionType

D = 1024
NG = 4
GC = 1292
COL_Z, COL_UF, COL_XS, COL_B, COL_C, COL_DT = 0, 512, 640, 1024, 1152, 1280
EPS = 1e-6
SAME_ENGINE_RAW_SYNC = True
LIST_SCHEDULE = True
PRIO_CRITICAL = True
FILL_MIN_GAP = 700.0
FILL_SLACK = 250.0
FILL_COST = 240.0


class Op:
    __slots__ = ("eng", "fn", "deps", "odeps", "inc", "mile", "sem", "is_dma", "idx", "cost", "users", "npend", "ready", "fin", "blev")

    def __init__(self, eng, fn, is_dma=False, sem=None, cost=100.0):
        self.eng, self.fn, self.is_dma, self.sem, self.cost = eng, fn, is_dma, sem, cost
        self.deps = set()
        self.odeps = set()
        self.inc = False
        self.mile = None


class Sched:
    ENGS = ("sync", "scalar", "vector", "gpsimd", "tensor")

    def __init__(self, nc):
        self.nc = nc
        self.final = []
        self.seg = []
        self.last_w = {}
        self.readers = {}
        self.last_on = {}
        self.dmas_since = []
        self.excl_last = {}
        self.nops = 0
        self.kmap = {}
        self.filler = None
        self.nfill = 0

    def _xk(self, k):
        n = k if isinstance(k, str) else k[0]
        sfx = self.kmap.get(n)
        return k if sfx is None else (k, sfx)

    def _edge(self, op, d, raw):
        if d is op:
            return
        if d.is_dma or op.is_dma or d.eng != op.eng:
            op.deps.add(d)
        elif d.eng != "tensor" and SAME_ENGINE_RAW_SYNC:
            op.deps.add(d)
        else:
            op.odeps.add(d)

    def add(self, eng, fn, reads=(), writes=(), dma_sem=None, extra_deps=(), excl=(), cost=100.0):
        op = Op(eng, fn, dma_sem is not None, dma_sem, cost)
        op.idx = self.nops
        self.nops += 1
        if self.kmap:
            reads = [self._xk(k) for k in reads]
            writes = [self._xk(k) for k in writes]
        for k in excl:
            d = self.excl_last.setdefault(k, {})
            for e2, o2 in d.items():
                self._edge(op, o2, False)
            d[eng] = op
        for b in reads:
            w = self.last_w.get(b)
            if w is not None:
                self._edge(op, w, True)
        for b in writes:
            w = self.last_w.get(b)
            if w is not None:
                self._edge(op, w, False)
            for r in self.readers.get(b, ()):
                self._edge(op, r, False)
        for d in extra_deps:
            op.deps.add(d)
        for b in reads:
            self.readers.setdefault(b, []).append(op)
        for b in writes:
            self.last_w[b] = op
            self.readers[b] = []
        self.seg.append(op)
        if op.is_dma:
            self.dmas_since.append(op)
        else:
            self.last_on[eng] = op
        return op

    def _schedule_segment(self):
        import heapq
        seg = self.seg
        self.seg = []
        if not LIST_SCHEDULE:
            self.final.extend(seg)
            return
        inseg = set(id(o) for o in seg)
        for o in seg:
            o.users = []
            o.npend = 0
            o.ready = 0.0
        for o in seg:
            for d in list(o.deps) + list(o.odeps):
                if id(d) in inseg:
                    d.users.append(o)
                    o.npend += 1
        for o in reversed(seg):
            bl = 0.0
            for u in o.users:
                if u.blev > bl:
                    bl = u.blev
            o.blev = bl + o.cost
        if PRIO_CRITICAL:
            for o in seg:
                o.idx = (-o.blev, o.idx)
        future = {e: [] for e in self.ENGS}
        avail = {e: [] for e in self.ENGS}
        free = {e: 0.0 for e in self.ENGS}
        for o in seg:
            if o.npend == 0:
                heapq.heappush(future[o.eng], (0.0, o.idx, o))
        out = []
        n = len(seg)
        while len(out) < n:
            best = None
            for e in self.ENGS:
                fq, aq = future[e], avail[e]
                t0 = free[e]
                while fq and fq[0][0] <= t0:
                    r_, i_, o_ = heapq.heappop(fq)
                    heapq.heappush(aq, (i_, r_, o_))
                if aq:
                    st, key = t0, aq[0][0]
                elif fq:
                    st, key = fq[0][0], fq[0][1]
                else:
                    continue
                if best is None or (st, key) < (best[0], best[1]):
                    best = (st, key, e)
            st, key, e = best
            if avail[e]:
                o = heapq.heappop(avail[e])[2]
            else:
                o = heapq.heappop(future[e])[2]
            if e == "tensor" and self.filler is not None and free[e] > 0.0:
                gap = st - free[e]
                if gap > FILL_MIN_GAP:
                    for _ in range(min(int((gap - FILL_SLACK) / FILL_COST), 24)):
                        fo = Op("tensor", self.filler, False, None, FILL_COST)
                        fo.idx = -1
                        out.append(fo)
                        n += 1
                        self.nfill += 1
            if o.is_dma:
                free[e] = st + 60.0
                o.fin = st + o.cost
            else:
                free[e] = st + o.cost
                o.fin = free[e] + 40.0
            out.append(o)
            for u in o.users:
                u.npend -= 1
                if o.fin > u.ready:
                    u.ready = o.fin
                if u.npend == 0:
                    heapq.heappush(future[u.eng], (u.ready, u.idx, u))
        self.final.extend(out)

    def barrier(self):
        dmas = list(self.dmas_since)
        self.dmas_since = []
        self._schedule_segment()
        self.filler = None
        last = {}
        for o in self.final[::-1]:
            if not o.is_dma and o.eng not in last:
                last[o.eng] = o
                if len(last) == 4:
                    break
        deps = list(last.values()) + dmas
        for e in self.ENGS:
            self.add(e, lambda eng: eng.nop(), extra_deps=deps, cost=30.0)
        self._schedule_segment()
        self.last_w.clear()
        self.readers.clear()
        self.excl_last.clear()

    def emit(self, eng_sems, block):
        self._schedule_segment()
        ops = self.final
        for op in ops:
            for d in op.deps:
                if not d.is_dma:
                    d.inc = True
        cnt = {e: 0 for e in self.ENGS}
        dcnt = {}
        for op in ops:
            if op.is_dma:
                dcnt[op.sem] = dcnt.get(op.sem, 0) + 16
                op.mile = dcnt[op.sem]
            elif op.inc:
                cnt[op.eng] += 1
                op.mile = cnt[op.eng]
        by_eng = {e: [o for o in ops if o.eng == e] for e in self.ENGS}

        def run(eng_name, eng):
            waited = {}
            for op in by_eng[eng_name]:
                need = {}
                for d in op.deps:
                    s = d.sem if d.is_dma else eng_sems[d.eng]
                    if need.get(s, 0) < d.mile:
                        need[s] = d.mile
                for s, v in need.items():
                    if waited.get(s, 0) < v:
                        eng.wait_ge(s, v)
                        waited[s] = v
                ins = op.fn(eng)
                if op.is_dma:
                    ins.then_inc(op.sem, 16)
                elif op.inc:
                    ins.then_inc(eng_sems[eng_name], 1)

        @block.sync
        def _(e):
            run("sync", e)

        @block.scalar
        def _(e):
            run("scalar", e)

        @block.vector
        def _(e):
            run("vector", e)

        @block.gpsimd
        def _(e):
            run("gpsimd", e)

        @block.tensor
        def _(e):
            run("tensor", e)


def _consts(L, dual):
    C = L // 128
    Ch = C // 2
    j = np.arange(C)[:, None].astype(np.float64)
    k2 = np.arange(C)[None, :].astype(np.float64)
    if not dual:
        ang = 2 * np.pi * j * k2 / C
        cs = np.concatenate([np.cos(ang), np.sin(ang)], axis=1)
    else:
        cs = np.zeros((C, 2 * C))
        jj = np.arange(Ch)[:, None].astype(np.float64)
        rr = np.arange(Ch)[None, :].astype(np.float64)
        ang = 2 * np.pi * jj * rr / Ch
        cs[0:Ch, 0:Ch] = np.cos(ang)
        cs[0:Ch, C:C + Ch] = np.sin(ang)
        cs[Ch:C, Ch:C] = np.cos(ang)
        cs[Ch:C, C + Ch:2 * C] = np.sin(ang)
    p = np.arange(128)[:, None, None].astype(np.float64)
    kk2 = np.arange(C)[None, :, None].astype(np.float64)
    k1 = np.arange(128)[None, None, :].astype(np.float64)
    t2 = np.zeros((128, C, 2, 4, 128))
    if not dual:
        a2 = 2 * np.pi * ((p * (C * k1 + kk2)) % L) / L
        mc, ms = np.cos(a2), np.sin(a2)
        t2[:, :, 0] = np.stack([mc, ms, -ms, mc], axis=2)
    else:
        Lh = L // 2
        k1a = np.arange(128)[None, None, :]
        isA = (k1a < 64)
        freq = np.where(isA, C * k1 + kk2, C * (k1 - 64) + kk2)
        a2 = 2 * np.pi * ((p * freq) % Lh) / Lh
        mc, ms = np.cos(a2), np.sin(a2)
        full = np.stack([mc, ms, -ms, mc], axis=2) * math.sqrt(2.0)
        slotA = (np.arange(C) < Ch)[None, :, None, None]
        colA = isA[:, :, None, :] if isA.ndim == 3 else isA
        colA = np.broadcast_to((np.arange(128) < 64)[None, None, None, :], full.shape)
        own_mask = np.where(slotA, colA, ~colA)
        t2[:, :, 0] = np.where(own_mask, full, 0.0)
        t2[:, :, 1] = np.where(own_mask, 0.0, full)
    return cs.astype(ml_dtypes.bfloat16), np.ascontiguousarray(t2.reshape(128, C, 1024)).astype(ml_dtypes.bfloat16)


def _masks():
    k = np.arange(128)[:, None]
    s = np.arange(128)[None, :]
    m = np.stack([(k <= s), (k >= s), (k < s), (k > s), np.ones((128, 128), bool), np.eye(128, dtype=bool)], 0)
    c = np.arange(64)[:, None] * np.arange(64)[None, :]
    c64 = np.cos(2 * np.pi * c / 64)
    s64 = np.sin(2 * np.pi * c / 64)
    z = np.zeros((64, 64))
    cb = np.block([[c64, z], [z, c64]])
    sb = np.block([[s64, z], [z, s64]])
    out = []
    for t in (cb, sb):
        r = t.astype(np.float32)
        for _ in range(3):
            h = r.astype(ml_dtypes.bfloat16)
            out.append(h)
            r = (r - h.astype(np.float32)).astype(np.float32)
    return m.astype(ml_dtypes.bfloat16), np.stack(out, 0)


def build(Ls):
    nc = bass.Bass("TRN2", target_bir_lowering=False)
    LMAX = Ls
    seqs = [("s", Ls)]

    def din(name, shape, dt=F32):
        return nc.dram_tensor(name, shape, dt, kind="ExternalInput").ap()

    x_d = {"s": din("x_s", [Ls, D])}
    pl_d = {"s": din("pl_s", [Ls, 256])}
    y_d = {"s": nc.dram_tensor("y_s", [Ls, D], F32, kind="ExternalOutput").ap()}
    flag_d = din("flag", [1])
    win_d = din("w_in_g", [NG, D, GC])
    convw_d = din("convw_g", [NG, 128, 25])
    convb_d = din("convb_g", [NG, 128, 5])
    cbrow_d = din("cbrow_g", [NG, 640])
    dtb_d = din("dtb_g", [NG, 12])
    alog_d = din("alog_g", [NG, 12])
    dsk_d = din("dsk_g", [NG, 6])
    snw_d = din("snw_g", [NG, 384])
    wfm_d = din("wfm", [8, 64, 64])
    wout_d = din("w_out", [2048, D])
    wpi_d = din("w_ple_in", [256, D])
    wpg_d = din("w_ple_gate", [D, D])
    fnw_d = din("fnw", [D])
    normw_d = din("normw", [D])
    masks_d = din("masks", [6, 128, 128], BF16)
    c64_d = din("c64", [6, 128, 128], BF16)
    cs_d = {"s": din("cs_s", [Ls // 128, 2 * (Ls // 128)], BF16)}
    t2_d = {"s": din("t2_s", [128, Ls // 128, 1024], BF16)}
    UT = nc.dram_tensor("UT", [8, 128, LMAX], BF16).ap()
    YCAT = nc.dram_tensor("YCAT", [LMAX, 2048], BF16).ap()
    YF = nc.dram_tensor("YFs", [LMAX, 128], F32).ap()
    XSBs = nc.dram_tensor("XSBs", [LMAX, 512], BF16).ap()
    BCTs = nc.dram_tensor("BCTs", [2, 128, LMAX], BF16).ap()
    DTPs = nc.dram_tensor("DTPs", [LMAX // 512, 128, 96], F32).ap()

    es = ExitStack()
    with es:
        S = Sched(nc)
        eng_sems = {e: es.enter_context(nc.semaphore("s_" + e)) for e in Sched.ENGS}
        dsems = {}

        def dsem(name):
            if name not in dsems:
                dsems[name] = es.enter_context(nc.semaphore("d_" + name))
            return dsems[name]

        def banks(*aps):
            out = set()
            for a in aps:
                try:
                    nm = a.tensor.name
                except Exception:
                    continue
                if nm.startswith("pb"):
                    out.add(nm)
            return out

        def fsz(ap):
            n = 1
            for d_ in ap.shape[1:]:
                n *= d_
            return n

        def ecost(eng, ap, *ins):
            n = fsz(ap)
            slow = 1.0
            for a_ in ins:
                try:
                    if a_.ap[-1][0] == 0 and a_.ap[-1][1] > 1:
                        slow = 1.0
                except Exception:
                    pass
            if eng == "scalar":
                return 220.0 + n / 1.4
            if eng == "vector":
                return 70.0 + slow * n / 0.96
            return 120.0 + slow * n / 0.7

        def V(fn, r=(), w=()):
            return S.add("vector", fn, r, w, cost=100.0)

        def G(fn, r=(), w=()):
            return S.add("gpsimd", fn, r, w, cost=150.0)

        def DMA(out, in_, r, w, sem, **kw):
            nbytes = out.shape[0] * fsz(out) * (4 if out.dtype == F32 else 2)
            return S.add("sync", lambda e: e.dma_start(out=out, in_=in_, **kw), r, w, dma_sem=dsem(sem),
                         cost=2500.0 + nbytes / 60.0)

        def mm(out, lhsT, rhs, start, stop, r, w):
            bk = banks(out)
            return S.add("tensor", lambda e: e.matmul(out, lhsT=lhsT, rhs=rhs, start=start, stop=stop), r,
                         list(w) + [("accb", b_) for b_ in bk], excl=bk, cost=28.0 + fsz(rhs) * 0.45)

        def tr(out, in_, ident, r, w):
            bk = banks(out)
            return S.add("tensor", lambda e: e.transpose(out=out, in_=in_, identity=ident), r,
                         list(w) + [("accb", b_) for b_ in bk], excl=bk, cost=110.0)

        def act(eng, out, in_, func, r, w, **kw):
            return S.add(eng, lambda e: e.activation(out=out, in_=in_, func=func, **kw), r, w, excl=banks(out, in_),
                         cost=ecost(eng, out))

        def cp(eng, out, in_, r, w):
            if eng == "scalar":
                return S.add(eng, lambda e: e.activation(out=out, in_=in_, func=AF.Copy), r, w, excl=banks(out, in_),
                             cost=ecost(eng, out))
            return S.add(eng, lambda e: e.tensor_copy(out=out, in_=in_), r, w, excl=banks(out, in_), cost=ecost(eng, out))

        def tt(eng, out, in0, in1, op, r, w):
            return S.add(eng, lambda e: e.tensor_tensor(out=out, in0=in0, in1=in1, op=op), r, w, excl=banks(out, in0, in1),
                         cost=ecost(eng, out, in0, in1))

        def ts(eng, out, in0, s1, s2, op0, op1, r, w, **kw):
            if s2 is None:
                return S.add(eng, lambda e: e.tensor_scalar(out=out, in0=in0, scalar1=s1, scalar2=None, op0=op0, **kw), r, w,
                             excl=banks(out, in0), cost=ecost(eng, out))
            return S.add(eng, lambda e: e.tensor_scalar(out=out, in0=in0, scalar1=s1, scalar2=s2, op0=op0, op1=op1, **kw), r, w,
                         excl=banks(out, in0), cost=ecost(eng, out))

        def split3(eng, src, dst3, tmp, r, w):
            cp(eng, dst3[0], src, r, [w + "0"])
            tt(eng, tmp, src, dst3[0], ALU.subtract, list(r) + [w + "0"], [w + "t"])
            cp(eng, dst3[1], tmp, [w + "t"], [w + "1"])
            tt(eng, tmp, tmp, dst3[1], ALU.subtract, [w + "t", w + "1"], [w + "t"])
            cp(eng, dst3[2], tmp, [w + "t"], [w + "2"])

        def stt(eng, out, in0, scalar, in1, op0, op1, r, w):
            return S.add(eng, lambda e: e.scalar_tensor_tensor(out=out, in0=in0, scalar=scalar, in1=in1, op0=op0, op1=op1), r, w,
                         excl=banks(out, in0, in1), cost=ecost(eng, out))

        ARENA = 207 * 1024
        arena = es.enter_context(nc.sbuf_tensor("arena", [128, ARENA // 2], BF16))

        class Alloc:
            def __init__(self, base=0):
                self.off = base

            def get(self, shape, dt):
                n = 1
                for s_ in shape[1:]:
                    n *= s_
                nb = n * (4 if dt == F32 else 2)
                nb = (nb + 63) // 64 * 64
                assert self.off + nb <= ARENA, (self.off, nb)
                v = arena[:, self.off // 2:(self.off + nb) // 2]
                self.off += nb
                if dt == F32:
                    v = v.bitcast(F32)
                v = v[:, 0:n]
                if len(shape) == 3:
                    v = v.rearrange("p (a b) -> p a b", a=shape[1])
                elif len(shape) == 4:
                    v = v.rearrange("p (a b c) -> p a b c", a=shape[1], b=shape[2])
                if shape[0] < 128:
                    v = v[0:shape[0]]
                return v

        pers = Alloc(0)
        masksb = pers.get([128, 6, 128], BF16)
        bLE, bGE, bLT, bGT, bONE, identb = (masksb[:, i, :] for i in range(6))
        gtb, ltb = bGT, bLT
        c64 = pers.get([128, 6, 128], BF16)
        wf3 = pers.get([128, 3, 128], BF16)
        wftmp = pers.get([128, 128], F32)
        normw_col = pers.get([128, 8], F32)
        flag_col = pers.get([128, 1], F32)
        Wg = pers.get([128, 8, GC], BF16)
        convw = pers.get([128, 25], F32)
        convb = pers.get([128, 5], F32)
        Dg = pers.get([128, 25, 128], BF16)
        DgD = pers.get([128, 6, 128], BF16)
        convwh = pers.get([128, 25], F32)
        cbh = pers.get([128, 5], F32)
        cbrow = pers.get([1, 640], BF16)
        cbrow_f = pers.get([1, 640], F32)
        ones_row = pers.get([1, 512], BF16)
        dtb_bc = pers.get([128, 12], F32)
        A_bc = pers.get([128, 12], F32)
        dsk_bc = pers.get([128, 6], F32)
        snw_bc = pers.get([128, 384], F32)
        wfblk = pers.get([128, 128], F32)
        W1b = pers.get([128, 128], BF16)
        W2nb = pers.get([128, 128], BF16)
        HBASE = pers.off
        Hst = pers.get([128, LMAX // 128, 384], BF16)
        PBASE = pers.off

        pb = [es.enter_context(nc.psum_tensor("pb%d" % i, [128, 512], F32)) for i in range(8)]

        def pbf(i, a, b):
            return pb[i][:, a:b]

        def pbb(i, a, b):
            return pb[i][:, a // 2:b // 2].bitcast(BF16)

        block = es.enter_context(nc.Block())

        DMA(masksb, masks_d.rearrange("m k s -> k m s"), [], ["identb", "gtb", "ltb", "masksb", "masks"], "c0")
        DMA(c64, c64_d.rearrange("m k s -> k m s"), [], ["c64"], "c1")
        DMA(normw_col, normw_d.rearrange("(kt k) -> k kt", k=128), [], ["normw_col"], "c2", allow_slow_non_contiguous=True)
        DMA(flag_col, flag_d.partition_broadcast(128), [], ["flag"], "c3")
        V(lambda e: e.memset(ones_row, 1.0), [], ["ones_row"])
        S.barrier()

        def load_uT(seq, L, sc, buf, key, sem):
            t0 = 512 * sc - 2
            lo = max(t0, 0)
            hi = min(t0 + 516, L)
            src = UT[:, :, lo:hi].rearrange("kt k t -> k kt t")
            return DMA(buf[:, :, lo - t0:hi - t0], src, [("UT", sc - 1), ("UT", sc), ("UT", sc + 1)], [key], sem)

        def pass0(seq, L):
            al = Alloc(PBASE)
            xt = [al.get([128, D], F32) for _ in range(2)]
            junk = al.get([128, D], BF16)
            ub = [al.get([128, D], BF16) for _ in range(2)]
            ss = [al.get([128, 1], F32) for _ in range(2)]
            rstd = [al.get([128, 1], F32) for _ in range(2)]
            uts = [al.get([128, 8, 512], BF16) for _ in range(2)]
            NSC = L // 512
            for sc in range(NSC):
                for c in range(4):
                    j = 4 * sc + c
                    b = j % 2
                    DMA(xt[b], x_d[seq][128 * j:128 * (j + 1), :], [], [("xt", b)], "xt%d" % b)
                    act("scalar", junk, xt[b], AF.Square, [("xt", b)], ["junk", ("ss", b)], accum_out=ss[b])
                    ts("vector", rstd[b], ss[b], 1.0 / D, EPS, ALU.mult, ALU.add, [("ss", b)], [("rstd", b)])
                    V(lambda e, b=b: e.reciprocal(out=rstd[b], in_=rstd[b]), [("rstd", b)], [("rstd", b)])
                    act("scalar", rstd[b], rstd[b], AF.Sqrt, [("rstd", b)], [("rstd", b)])
                    act("scalar", ub[b], xt[b], AF.Copy, [("xt", b), ("rstd", b)], [("ub", b)], scale=rstd[b])
                    bank = j % 2
                    for kt in range(8):
                        tr(pbb(bank, 128 * kt, 128 * (kt + 1)), ub[b][:, 128 * kt:128 * (kt + 1)], identb,
                           [("ub", b), "identb"], [("pT", bank)])
                    cp("vector", uts[sc % 2][:, :, 128 * c:128 * (c + 1)],
                       pbb(bank, 0, 1024).rearrange("p (a b) -> p a b", a=8), [("pT", bank)], [("uts", sc % 2, c)])
                DMA(UT[:, :, 512 * sc:512 * (sc + 1)].rearrange("kt k t -> k kt t"), uts[sc % 2],
                    [("uts", sc % 2, c) for c in range(4)], [("UT", sc)], "uts%d" % (sc % 2))

        def load_group(g, L):
            import os
            dbg = os.environ.get("KDEBUG", "")
            al = Alloc(PBASE)
            wst = [al.get([128, GC], F32) for _ in range(2)]
            tmp12 = al.get([128, 12], F32)
            if not dbg or "w" in dbg:
              for kt in range(8):
                b = kt % 2
                DMA(wst[b], win_d[g, 128 * kt:128 * (kt + 1), :], [], [("wst", b)], "wst%d" % b)
                eng_ = "vector" if kt % 2 == 0 else "gpsimd"
                ts(eng_, Wg[:, kt, 512:GC], wst[b][:, 512:GC], normw_col[:, kt:kt + 1], None, ALU.mult, None,
                   [("wst", b), "normw_col"], [("Wg", kt)])
                ts(eng_, Wg[:, kt, 0:512], wst[b][:, 0:512], normw_col[:, kt:kt + 1], 0.5, ALU.mult, ALU.mult,
                   [("wst", b), "normw_col"], [("Wg", kt)])
            if not dbg or "c" in dbg:
              DMA(convw, convw_d[g], [], ["convw"], "c0")
              DMA(convb, convb_d[g], [], ["convb"], "c1")
              for i25 in range(25):
                  ts("vector" if i25 % 2 == 0 else "gpsimd", Dg[:, i25, :], identb, convw[:, i25:i25 + 1], 0.5, ALU.mult, ALU.mult,
                     ["identb", "convw"], [("Dg", i25)])
              ts("vector", convwh, convw, 0.5, None, ALU.mult, None, ["convw"], ["convwh"])
              ts("vector", cbh, convb, 0.5, None, ALU.mult, None, ["convb"], ["cbh"])
              DMA(cbrow_f, cbrow_d[g:g + 1, :], [], ["cbrow_f"], "c8")
              ts("vector", cbrow, cbrow_f, 0.5, None, ALU.mult, None, ["cbrow_f"], ["cbrow"])
              DMA(dtb_bc, dtb_d[g].partition_broadcast(128), [], ["dtb"], "c2")
              DMA(tmp12, alog_d[g].partition_broadcast(128), [], ["tmp12"], "c3")
              DMA(dsk_bc, dsk_d[g].partition_broadcast(128), [], ["dsk"], "c4")
              DMA(snw_bc, snw_d[g].partition_broadcast(128), [], ["snw"], "c5")
              for h_ in range(6):
                  ts("gpsimd", DgD[:, h_, :], identb, dsk_bc[:, h_:h_ + 1], None, ALU.mult, None, ["identb", "dsk"], [("DgD", h_)])
              act("scalar", A_bc, tmp12, AF.Exp, ["tmp12"], ["A_bc"])
              ts("vector", A_bc, A_bc, -1.0, None, ALU.mult, None, ["A_bc"], ["A_bc"])
            if not dbg or "f" in dbg:
              V(lambda e: e.memset(wfblk, 0.0), [], ["wfblk"])
              DMA(wfblk[0:64, 0:64], wfm_d[2 * g], [], ["wfblk"], "c6")
              DMA(wfblk[64:128, 64:128], wfm_d[2 * g + 1], [], ["wfblk"], "c7")
              sc_ = 1.0 / math.sqrt(64.0 * L)
              split3("vector", wfblk, [wf3[:, i, :] for i in range(3)], wftmp, ["wfblk"], "wf3")
              pairs = [(i, j_) for i in range(3) for j_ in range(3) if i + j_ <= 2]
              for m_ in range(2):
                  for n_, (i, j_) in enumerate(pairs):
                      mm(pbf(0, 128 * m_, 128 * (m_ + 1)), c64[:, 3 * m_ + i, :], wf3[:, j_, :], n_ == 0, n_ == len(pairs) - 1,
                         ["c64", "wf30", "wf31", "wf32"], ["w%dp" % (m_ + 1)])
              ts("vector", W1b, pbf(0, 0, 128), sc_, None, ALU.mult, None, ["w1p"], ["W1b"])
              ts("vector", W2nb, pbf(0, 128, 256), -sc_, None, ALU.mult, None, ["w2p"], ["W2nb"])
            S.barrier()

        def passF(seq, L):
            C = L // 128
            NSC = L // 512
            al = Alloc(PBASE)
            UF = al.get([128, C, 128], BF16)
            UTf = al.get([128, 128, 128], BF16)
            Asb = al.get([128, 2, C, 128], BF16)
            csb = al.get([128, 2 * C], BF16)
            uts = [al.get([128, 8, 516], BF16) for _ in range(2)]
            t2b = [al.get([128, 4, 1024], BF16) for _ in range(2)]
            pq = [al.get([128, 256], BF16) for _ in range(2)]
            yst = [al.get([128, 4, 128], F32) for _ in range(2)]
            DMA(csb[0:C, :], cs_d[seq], [], ["csb"], "c0")
            load_uT(seq, L, 0, uts[0], ("uts", 0), "uts0")
            for sc in range(NSC):
                if sc + 1 < NSC:
                    load_uT(seq, L, sc + 1, uts[(sc + 1) % 2], ("uts", (sc + 1) % 2), "uts%d" % ((sc + 1) % 2))
                u = uts[sc % 2]
                for c in range(4):
                    j = 4 * sc + c
                    bank = j % 2
                    for kt in range(8):
                        mm(pbf(bank, 0, 128), u[:, kt, 2 + 128 * c:2 + 128 * (c + 1)], Wg[:, kt, COL_UF:COL_UF + 128],
                           kt == 0, kt == 7, [("uts", sc % 2), ("Wg", kt)], [("pu", bank)])
                    cp("scalar" if j % 2 == 0 else "vector", UF[:, j, :], pbf(bank, 0, 128), [("pu", bank)], ["UF"])
            for cb in range(16):
                bank = 2 + cb % 2
                for i in range(8):
                    ch = 8 * cb + i
                    tr(pbb(bank, 128 * i, 128 * (i + 1))[0:C], UF[:, :, ch], identb, ["UF", "identb"], [("pt", bank)])
                cp("vector" if cb % 2 == 0 else "scalar", UTf[0:C, 8 * cb:8 * cb + 8, :],
                   pbb(bank, 0, 1024)[0:C].rearrange("p (a b) -> p a b", a=8), [("pt", bank)], [("UTf", cb)])
            nper = min(512 // (2 * C), 128)
            nb_ = 128 // nper
            for bi in range(nb_):
                bank = 4 + bi % 2
                for i in range(nper):
                    ch = bi * nper + i
                    mm(pbf(bank, 2 * C * i, 2 * C * (i + 1)), UTf[0:C, ch, :], csb[0:C, :], True, True,
                       [("UTf", ch // 8), "csb"], [("pa", bank)])
                cp("vector" if bi % 2 == 0 else "scalar", Asb[:, :, :, bi * nper:(bi + 1) * nper],
                   pbf(bank, 0, 2 * C * nper).rearrange("p (c r k) -> p r k c", c=nper, r=2),
                   [("pa", bank)], [("Asb", bi)])
            areads = [("Asb", bi) for bi in range(nb_)]
            NP = C // 4
            Ch = C // 2
            DMA(t2b[0], t2_d[seq][:, 0:4, :], [], [("t2b", 0)], "t2b0")
            for pc in range(NP):
                if pc + 1 < NP:
                    DMA(t2b[(pc + 1) % 2], t2_d[seq][:, 4 * (pc + 1):4 * (pc + 2), :], [], [("t2b", (pc + 1) % 2)],
                        "t2b%d" % ((pc + 1) % 2))
                tb = t2b[pc % 2]
                ybank = pc % 2
                for kk in range(4):
                    k2 = 4 * pc + kk
                    kp = (k2 + Ch) % C
                    sl = k2 % 2
                    o = pbf(6 + sl, 0, 256)
                    srcs = ((0, k2, 0), (1, k2, 256), (0, kp, 512), (1, kp, 768))
                    for n_, (ri, kq, off) in enumerate(srcs):
                        mm(o, Asb[:, ri, kq, :], tb[:, kk, off:off + 256], n_ == 0, n_ == 3,
                           areads + [("t2b", pc % 2)], [("ppq", sl)])
                    cp("scalar" if sl == 0 else "vector", pq[sl], o, [("ppq", sl)], [("pq", sl)])
                    yo = pbf(ybank, 128 * kk, 128 * (kk + 1))
                    mm(yo, pq[sl][:, 0:128], W1b, True, False, [("pq", sl), "W1b"], [("py", ybank, kk)])
                    mm(yo, pq[sl][:, 128:256], W2nb, False, True, [("pq", sl), "W2nb"], [("py", ybank, kk)])
                cp("vector" if pc % 2 == 0 else "scalar", yst[pc % 2],
                   pbf(ybank, 0, 512).rearrange("p (a b) -> p a b", a=4),
                   [("py", ybank, qq) for qq in range(4)], [("yst", pc % 2)])
                DMA(YF[0:L].rearrange("(k1 k2) d -> k1 k2 d", k2=C)[:, 4 * pc:4 * pc + 4, :], yst[pc % 2],
                    [("yst", pc % 2)], [("YF", pc)], "yst%d" % (pc % 2))
            S.barrier()

        def ssd_front(u, ukey, WIN, ACC, CV, cvk, sc, NSC, tiles, cbanks, ACCD):
            for ti, t in enumerate(tiles):
                bank = ti % 2
                col = COL_XS + 128 * t
                for kt in range(8):
                    mm(pbf(bank, 0, 512), Wg[:, kt, col:col + 128], u[:, kt, 2:514], kt == 0, kt == 7,
                       [ukey, ("Wg", kt)], [("pfm", bank)])
                cp("scalar", WIN[:, t, 2:514], pbf(bank, 0, 512), [("pfm", bank)], [("WIN", t)])
                for side, (a, b_) in enumerate(((0, 2), (514, 516))):
                    edge = (sc == 0 and side == 0) or (sc == NSC - 1 and side == 1)
                    if edge:
                        G(lambda e, t=t, a=a, b_=b_: e.memset(WIN[:, t, a:b_], 0.0), [], [("WIN", t)])
                    else:
                        hb = pbf(2, 4 * ti + 2 * side, 4 * ti + 2 * side + 2)
                        for kt in range(8):
                            mm(hb, Wg[:, kt, col:col + 128], u[:, kt, a:b_], kt == 0, kt == 7,
                               [ukey, ("Wg", kt)], [("phalo", ti, side)])
                        if (side == 0 and sc == NSC // 2) or (side == 1 and sc == NSC // 2 - 1):
                            ts("vector", WIN[:, t, a:b_], hb, flag_col[:, 0:1], None, ALU.mult, None,
                               [("phalo", ti, side), "flag"], [("WIN", t)])
                        else:
                            cp("vector", WIN[:, t, a:b_], hb, [("phalo", ti, side)], [("WIN", t)])
                if ti in (1, 3):
                    ab = ACCD[(ti // 2) % 2]
                    ak = ("ACCD", (ti // 2) % 2)
                    ts("vector", ab, WIN[:, t, 0:512], convwh[:, 5 * t:5 * t + 1], cbh[:, t:t + 1], ALU.mult, ALU.add,
                       [("WIN", t), "convwh", "cbh"], [ak])
                    for k in range(1, 5):
                        stt("vector", ab, WIN[:, t, k:k + 512], convwh[:, 5 * t + k:5 * t + k + 1], ab, ALU.mult, ALU.add,
                            [("WIN", t), "convwh", ak], [ak])
                    act("scalar", ACC[ti % 2], ab, AF.Tanh, [ak], [("TT", ti % 2)])
                    stt("vector", CV[:, t, :], ACC[ti % 2], 1.0, ab, ALU.add, ALU.mult, [("TT", ti % 2), ak], [(cvk, t)])
                    continue
                cbk = cbanks[(ti // 2) % 2]
                for k in range(5):
                    mm(pbf(cbk, 0, 512), Dg[:, 5 * t + k, :], WIN[:, t, k:k + 512], k == 0, False,
                       [("WIN", t), ("Dg", 5 * t + k)], [("pcv", cbk)])
                mm(pbf(cbk, 0, 512), cbrow[0:1, 128 * t:128 * (t + 1)], ones_row[0:1, 0:512], False, True,
                   ["cbrow", "ones_row"], [("pcv", cbk)])
                act("scalar", ACC[ti % 2], pbf(cbk, 0, 512), AF.Tanh, [("pcv", cbk)], [("TT", ti % 2)])
                stt("vector", CV[:, t, :], ACC[ti % 2], 1.0, pbf(cbk, 0, 512), ALU.add, ALU.mult,
                    [("TT", ti % 2), ("pcv", cbk)], [(cvk, t)])

        def dt_block(u, ukey, DTR, DTV, AV, dk, A3, ATMP):
            for c in range(4):
                for kt in range(8):
                    mm(pbf(2, 20 + 12 * c, 20 + 12 * (c + 1)), u[:, kt, 2 + 128 * c:2 + 128 * (c + 1)],
                       Wg[:, kt, COL_DT:COL_DT + 12], kt == 0, kt == 7, [ukey, ("Wg", kt)], [("pdt", c)])
            tt("vector", DTR, pbf(2, 20, 68).rearrange("p (a b) -> p a b", a=4),
               dtb_bc.unsqueeze(1).broadcast_to([128, 4, 12]), ALU.add, [("pdt", c) for c in range(4)] + ["dtb"], [dk + "r"])
            act("scalar", DTR, DTR, AF.Exp, [dk + "r"], [dk + "r"])
            act("scalar", DTV, DTR, AF.Ln, [dk + "r"], [dk + "v"], bias=1.0)
            tt("vector", AV, DTV, A_bc.unsqueeze(1).broadcast_to([128, 4, 12]), ALU.mult, [dk + "v", "A_bc"], [dk + "a"])
            split3("gpsimd", AV.rearrange("p a b -> p (a b)"), [A3[:, i, :] for i in range(3)], ATMP, [dk + "a"], dk + "a3")

        def to_token_major(CV, cvk, c, XSB, xk):
            for t in range(4):
                tr(pbb(2, 512 + 128 * t, 512 + 128 * (t + 1)), CV[:, t, 128 * c:128 * (c + 1)], identb, [(cvk, t), "identb"], ["ptm"])
            cp("vector", XSB, pbb(2, 512, 1024), ["ptm"], [xk])

        def passA(seq, L):
            NSC = L // 512
            al = Alloc(PBASE)
            two = lambda shape, dt: [al.get(shape, dt) for _ in range(2)]
            uts = two([128, 8, 516], BF16)
            WIN2 = two([128, 5, 516], BF16)
            ACC = two([128, 512], F32)
            ACCD = two([128, 512], F32)
            CV2 = two([128, 5, 512], BF16)
            XSB = two([128, 512], BF16)
            DTR2, DTV2, AV2 = two([128, 4, 12], F32), two([128, 4, 12], F32), two([128, 4, 12], F32)
            A32 = two([128, 3, 48], BF16)
            ATMP2 = two([128, 48], F32)
            E = two([128, 2, 12], F32)
            SCL = two([128, 6], F32)
            XD = two([128, 384], BF16)
            SC_LOCAL = ("WIN", "CV", "dtr", "dtv", "dta", "dta30", "dta31", "dta32", "dta3t")
            H = al.get([128, 384], F32)
            V(lambda e: e.memset(H, 0.0), [], ["H"])
            load_uT(seq, L, 0, uts[0], ("uts", 0), "uts0")
            for sc in range(NSC):
                if sc + 1 < NSC:
                    load_uT(seq, L, sc + 1, uts[(sc + 1) % 2], ("uts", (sc + 1) % 2), "uts%d" % ((sc + 1) % 2))
                u, ukey = uts[sc % 2], ("uts", sc % 2)
                q_ = sc % 2
                WIN, CV, DTR, DTV, AV, A3, ATMP = WIN2[q_], CV2[q_], DTR2[q_], DTV2[q_], AV2[q_], A32[q_], ATMP2[q_]
                S.kmap.update({n_: q_ for n_ in SC_LOCAL})
                ssd_front(u, ukey, WIN, ACC, CV, "CV", sc, NSC, [0, 1, 2, 3, 4], (6, 7), ACCD)
                dt_block(u, ukey, DTR, DTV, AV, "dt", A3, ATMP)
                DMA(BCTs[:, :, 512 * sc:512 * (sc + 1)].rearrange("t k l -> k t l"), CV[:, 3:5, :], [("CV", 3), ("CV", 4)],
                    [("BCTs", sc)], "bct%d" % q_)
                DMA(DTPs[sc, :, 0:48], DTV.rearrange("p a b -> p (a b)"), ["dtv"], [("DTPs", sc, 0)], "dtpa%d" % q_)
                DMA(DTPs[sc, :, 48:96], AV.rearrange("p a b -> p (a b)"), ["dta"], [("DTPs", sc, 1)], "dtpb%d" % q_)
                for c in range(4):
                    j = 4 * sc + c
                    b = j % 2
                    to_token_major(CV, "CV", c, XSB[b], ("XSB", b))
                    DMA(XSBs[128 * j:128 * (j + 1), :], XSB[b], [("XSB", b)], [("XSBs", j)], "xsbw%d" % b)
                    for mi, m_ in enumerate((bGT, bONE)):
                        for i3 in range(3):
                            mm(pbf(2, 68 + 12 * mi, 80 + 12 * mi), m_, A3[:, i3, 12 * c:12 * c + 12], i3 == 0, i3 == 2,
                               ["masksb", "dta30", "dta31", "dta32"], [("pcs", mi)])
                    act("scalar", E[b], pbf(2, 68, 92).rearrange("p (a b) -> p a b", a=2), AF.Exp,
                        [("pcs", 0), ("pcs", 1)], [("E", b)])
                    tt("vector", SCL[b], E[b][:, 0, 0:6], DTV[:, c, 0:6], ALU.mult, [("E", b), "dtv"], [("SCL", b)])
                    tt("gpsimd", XD[b].rearrange("p (h d) -> p h d", h=6), XSB[b][:, 0:384].rearrange("p (h d) -> p h d", h=6),
                       SCL[b].unsqueeze(2).broadcast_to([128, 6, 64]), ALU.mult, [("XSB", b), ("SCL", b)], [("XD", b)])
                    mm(pbf(4 + b, 0, 384), XSB[b][:, 384:512], XD[b], True, True, [("XSB", b), ("XD", b)], [("pst", b)])
                    if j == (L // 128) // 2:
                        ts("vector", H, H, flag_col[:, 0:1], None, ALU.mult, None, ["H", "flag"], ["H"])
                    cp("scalar", Hst[:, j, :], H, ["H"], [("Hst", j)])
                    tt("vector", H.rearrange("p (h d) -> p h d", h=6), H.rearrange("p (h d) -> p h d", h=6),
                       E[b][:, 1, 0:6].unsqueeze(2).broadcast_to([128, 6, 64]), ALU.mult, ["H", ("E", b)], ["H"])
                    tt("vector", H, H, pbf(4 + b, 0, 384), ALU.add, ["H", ("pst", b)], ["H"])
            S.kmap.clear()
            S.barrier()

        def passB(seq, L, g):
            NSC = L // 512
            al = Alloc(PBASE)
            two = lambda shape, dt: [al.get(shape, dt) for _ in range(2)]
            uts = two([128, 8, 516], BF16)
            WIN2 = two([128, 5, 516], BF16)
            ACC = two([128, 512], F32)
            CV2 = two([128, 5, 512], BF16)
            XSB = two([128, 512], BF16)
            SZ = two([128, 512], F32)
            TZ = two([128, 512], F32)
            DTR2, DTV2, AV2 = two([128, 4, 12], F32), two([128, 4, 12], F32), two([128, 4, 12], F32)
            A32 = two([128, 3, 48], BF16)
            ATMP2 = two([128, 48], F32)
            E = two([128, 4, 12], F32)
            SCLB2 = two([128, 6], F32)
            RR2 = [two([128, 2, 6, 128], BF16) for _ in range(2)]
            CBM2 = [two([128, 128], F32) for _ in range(2)]
            DEC = two([128, 256], F32)
            MT2 = two([128, 2, 6, 128], BF16)
            XDT2 = [two([128, 384], BF16) for _ in range(2)]
            XSD2 = two([128, 384], BF16)
            XDB2 = two([128, 384], BF16)
            Gs = al.get([128, 384], F32)
            Gb = al.get([128, 384], BF16)
            T12, T22, YS2 = two([128, 384], F32), two([128, 384], F32), two([128, 384], F32)
            junk2 = two([128, 384], BF16)
            YFR = two([128, 128], F32)
            YG8 = [al.get([128, 384], F32) for _ in range(8)]
            YC8 = [al.get([128, 512], BF16) for _ in range(8)]
            ssq8 = al.get([128, 8], F32)
            rs8 = al.get([128, 8], F32)
            SC_LOCAL = ("WIN", "CV", "dtr", "dtv", "dta", "dta30", "dta31", "dta32", "dta3t")
            CH_LOCAL = ("CBM", "MT", "XDT", "XSD", "XDB", "SCLB", "RR", "T1", "T2a", "T2b", "YS", "junkb")
            V(lambda e: e.memset(Gs, 0.0), [], ["Gs"])
            V(lambda e: e.memset(Gb, 0.0), [], ["Gb"])
            order = list(range(NSC - 1, -1, -1))
            load_uT(seq, L, order[0], uts[0], ("uts", 0), "uts0")
            for oi, sc in enumerate(order):
                if oi + 1 < NSC:
                    load_uT(seq, L, order[oi + 1], uts[(oi + 1) % 2], ("uts", (oi + 1) % 2), "uts%d" % ((oi + 1) % 2))
                u, ukey = uts[oi % 2], ("uts", oi % 2)
                q_ = oi % 2
                WIN, CV, DTR, DTV, AV, A3, ATMP = WIN2[q_], CV2[q_], DTR2[q_], DTV2[q_], AV2[q_], A32[q_], ATMP2[q_]
                S.kmap.update({n_: q_ for n_ in SC_LOCAL})
                BC = CV[:, 3:5, :]
                DMA(BC, BCTs[:, :, 512 * sc:512 * (sc + 1)].rearrange("t k l -> k t l"), [], [("CV", 3), ("CV", 4)], "bcr%d" % q_)
                DMA(DTV.rearrange("p a b -> p (a b)"), DTPs[sc, :, 0:48], [], ["dtv"], "dtra%d" % q_)
                DMA(AV.rearrange("p a b -> p (a b)"), DTPs[sc, :, 48:96], [], ["dta"], "dtrb%d" % q_)
                split3("gpsimd", AV.rearrange("p a b -> p (a b)"), [A3[:, i, :] for i in range(3)], ATMP, ["dta"], "dta3")
                for c in range(3, -1, -1):
                    j = 4 * sc + c
                    b = j % 2
                    S.kmap.update({n_: b for n_ in CH_LOCAL})
                    RR, CBM, MT, XDT, XSD, XDB, SCLB = RR2[b], CBM2[b], MT2[b], XDT2[b], XSD2[b], XDB2[b], SCLB2[b]
                    T1, T2_, YS, junk = T12[b], T22[b], YS2[b], junk2[b]
                    DMA(YFR[b], YF[128 * j:128 * (j + 1), :], [], [("YFR", b)], "yfr%d" % b)
                    for kt in range(8):
                        mm(pbf(b, 0, 512), u[:, kt, 2 + 128 * c:2 + 128 * (c + 1)], Wg[:, kt, COL_Z:COL_Z + 512],
                           kt == 0, kt == 7, [ukey, ("Wg", kt)], [("pfm", b)])
                    act("scalar", TZ[b], pbf(b, 0, 512), AF.Tanh, [("pfm", b)], [("TZ", b)])
                    stt("vector", SZ[b], TZ[b], 1.0, pbf(b, 0, 512), ALU.add, ALU.mult, [("TZ", b), ("pfm", b)], [("SZ", b)])
                    DMA(XSB[b], XSBs[128 * j:128 * (j + 1), :], [], [("XSB", b)], "xsbr%d" % b)
                    for mi, m_ in enumerate((bLE, bGE, bLT, bONE)):
                        for i3 in range(3):
                            mm(pbf(2, 68 + 12 * mi, 80 + 12 * mi), m_, A3[:, i3, 12 * c:12 * c + 12], i3 == 0, i3 == 2,
                               ["masksb", "dta30", "dta31", "dta32"], [("pcs", mi)])
                    act("scalar", E[b], pbf(2, 68, 116).rearrange("p (a b) -> p a b", a=4), AF.Exp,
                        [("pcs", mi) for mi in range(4)], [("E", b)])
                    ef, eb_, dstb, cdb = E[b][:, 0, 0:6], E[b][:, 1, 6:12], E[b][:, 2, 6:12], E[b][:, 3, 6:12]
                    for d_ in range(2):
                        msk = bLE if d_ == 0 else bGE
                        for hl in range(2):
                            tt("gpsimd", RR[d_][:, hl, :, :], msk.unsqueeze(1).broadcast_to([128, 6, 128]),
                               A3[:, hl, 12 * c + 6 * d_:12 * c + 6 * d_ + 6].unsqueeze(2).broadcast_to([128, 6, 128]), ALU.mult,
                               ["masks", "dta30", "dta31"], [("RR", d_, hl)])
                    mm(pbf(2, 116, 244), CV[:, 3, 128 * c:128 * (c + 1)], CV[:, 4, 128 * c:128 * (c + 1)], True, True,
                       [("CV", 3), ("CV", 4)], ["pcb"])
                    tt("vector", CBM[0], pbf(2, 116, 244), bLE, ALU.mult, ["pcb", "masks"], [("CBM", 0)])
                    tt("vector", CBM[1], pbf(2, 116, 244), bGE, ALU.mult, ["pcb", "masks"], [("CBM", 1)])
                    it = 0
                    for d_ in range(2):
                        lt_ = gtb if d_ == 0 else ltb
                        for hp in range(3):
                            slot = it % 2
                            o = pbf(6 + slot, 0, 256)
                            for hl in range(2):
                                mm(o, lt_, RR[d_][:, hl, 2 * hp:2 * hp + 2, :].rearrange("p a b -> p (a b)"),
                                   hl == 0, hl == 1, ["gtb", "ltb", ("RR", d_, hl)], [("pseg", slot)])
                            act("scalar", DEC[slot], o, AF.Exp, [("pseg", slot)], [("DEC", slot)])
                            for hh in range(2):
                                h_ = 2 * hp + hh
                                stt("vector", MT[:, d_, h_, :], DEC[slot][:, 128 * hh:128 * (hh + 1)],
                                    DTV[:, c, 6 * d_ + h_:6 * d_ + h_ + 1], CBM[d_], ALU.mult, ALU.mult,
                                    [("DEC", slot), ("CBM", d_), "dtv"], [("MT", d_, hp)])
                            it += 1
                    for h in range(6):
                        ybk = 4
                        yo = pbf(ybk, 64 * h, 64 * (h + 1))
                        xh = XSB[b][:, 64 * h:64 * (h + 1)]
                        mm(yo, MT[:, 0, h, :], xh, True, False, [("MT", 0, h // 2), ("XSB", b)], [("py", b, h)])
                        mm(yo, MT[:, 1, h, :], xh, False, False, [("MT", 1, h // 2), ("XSB", b)], [("py", b, h)])
                        mm(yo, DgD[:, h, :], xh, False, True, [("DgD", h), ("XSB", b)], [("py", b, h)])
                    mm(pbf(5, 0, 384), CV[:, 4, 128 * c:128 * (c + 1)], Hst[:, j, :], True, True, [("CV", 4), ("Hst", j)], ["pzf"])
                    for h in range(6):
                        act("scalar", T1[:, 64 * h:64 * (h + 1)], pbf(5, 64 * h, 64 * (h + 1)), AF.Copy, ["pzf", ("E", b)], [("T1", h)],
                            scale=E[b][:, 0, h:h + 1])
                    mm(pbf(5, 0, 384), CV[:, 4, 128 * c:128 * (c + 1)], Gb, True, True, [("CV", 4), "Gb"], ["pzf"])
                    for h in range(6):
                        stt("vector", T1[:, 64 * h:64 * (h + 1)], pbf(5, 64 * h, 64 * (h + 1)), E[b][:, 1, 6 + h:7 + h],
                            T1[:, 64 * h:64 * (h + 1)], ALU.mult, ALU.add, ["pzf", ("E", b), ("T1", h)], [("T1", h)])
                    tt("vector", YS, pbf(4, 0, 384), T1, ALU.add, [("py", b, h) for h in range(6)] + [("T1", h) for h in range(6)], ["YS"])
                    tt("vector", SCLB, dstb, DTV[:, c, 6:12], ALU.mult, [("E", b), "dtv"], ["SCLB"])
                    tt("gpsimd", XDB.rearrange("p (h d) -> p h d", h=6), XSB[b][:, 0:384].rearrange("p (h d) -> p h d", h=6),
                       SCLB.unsqueeze(2).broadcast_to([128, 6, 64]), ALU.mult, [("XSB", b), "SCLB"], ["XDB"])
                    mm(pbf(3, 0, 384), XSB[b][:, 384:512], XDB, True, True, [("XSB", b), "XDB"], ["pst3"])
                    tt("vector", Gs.rearrange("p (h d) -> p h d", h=6), Gs.rearrange("p (h d) -> p h d", h=6),
                       cdb.unsqueeze(2).broadcast_to([128, 6, 64]), ALU.mult, ["Gs", ("E", b)], ["Gs"])
                    tt("vector", Gs, Gs, pbf(3, 0, 384), ALU.add, ["Gs", "pst3"], ["Gs"])
                    if j == (L // 128) // 2:
                        ts("vector", Gs, Gs, flag_col[:, 0:1], None, ALU.mult, None, ["Gs", "flag"], ["Gs"])
                    cp("scalar", Gb, Gs, ["Gs"], ["Gb"])
                    sl8 = 4 * q_ + c
                    tt("gpsimd", YG8[sl8], YS, SZ[b][:, 128:512], ALU.mult, ["YS", ("SZ", b)], [("YG8", sl8)])
                    act("scalar", junk, YG8[sl8], AF.Square, [("YG8", sl8)], ["junkb", ("ssq8", sl8)], accum_out=ssq8[:, sl8:sl8 + 1])
                    tt("gpsimd", YC8[sl8][:, 0:128], YFR[b], SZ[b][:, 0:128], ALU.mult, [("YFR", b), ("SZ", b)], [("YC8", sl8, 0)])
                    DMA(YCAT[128 * j:128 * (j + 1), 128 * g:128 * (g + 1)], YC8[sl8][:, 0:128], [("YC8", sl8, 0)], [("YCAT", j, 0)],
                        "yca%d" % sl8)
                rs4 = rs8[:, 4 * q_:4 * q_ + 4]
                ts("vector", rs4, ssq8[:, 4 * q_:4 * q_ + 4], 1.0 / 384, EPS, ALU.mult, ALU.add,
                   [("ssq8", 4 * q_ + c_) for c_ in range(4)], [("rs8", q_)])
                V(lambda e, rs4=rs4: e.reciprocal(out=rs4, in_=rs4), [("rs8", q_)], [("rs8", q_)])
                act("scalar", rs4, rs4, AF.Sqrt, [("rs8", q_)], [("rs8", q_)])
                for c in range(4):
                    j = 4 * sc + c
                    sl8 = 4 * q_ + c
                    stt("vector", YC8[sl8][:, 128:512], YG8[sl8], rs8[:, sl8:sl8 + 1], snw_bc, ALU.mult, ALU.mult,
                        [("YG8", sl8), ("rs8", q_), "snw"], [("YC8", sl8, 1)])
                    DMA(YCAT[128 * j:128 * (j + 1), 512 + 384 * g:512 + 384 * (g + 1)], YC8[sl8][:, 128:512], [("YC8", sl8, 1)],
                        [("YCAT", j, 1)], "ycb%d" % sl8)
            S.kmap.clear()
            S.barrier()

        def load_tail_weights():
            al = Alloc(HBASE)
            wo = al.get([128, 16, D], BF16)
            wg_ = al.get([128, 8, D], BF16)
            wp = al.get([128, 2, D], BF16)
            fnw = al.get([128, D], F32)
            wst = [al.get([128, D], F32) for _ in range(2)]
            i = 0
            for dst, src, n in ((wo, wout_d, 16), (wg_, wpg_d, 8), (wp, wpi_d, 2)):
                for kt in range(n):
                    b = i % 2
                    DMA(wst[b], src[128 * kt:128 * (kt + 1), :], [], [("wst", b)], "wst%d" % b)
                    if dst is wp:
                        ts("vector" if i % 2 == 0 else "gpsimd", dst[:, kt, :], wst[b], 0.5, None, ALU.mult, None, [("wst", b)], [("tw", i)])
                    else:
                        cp("vector" if i % 2 == 0 else "gpsimd", dst[:, kt, :], wst[b], [("wst", b)], [("tw", i)])
                    i += 1
            DMA(fnw, fnw_d.partition_broadcast(128), [], ["fnw"], "c0")
            S.barrier()
            return wo, wg_, wp, fnw, al

        def tail(seq, L, wo, wg_, wp, fnw, al):
            xt = [al.get([128, D], F32) for _ in range(2)]
            yc = [al.get([128, 2048], BF16) for _ in range(2)]
            pt = [al.get([128, 256], F32) for _ in range(2)]
            two = lambda shape, dt: [al.get(shape, dt) for _ in range(2)]
            ptb2, pT2, ycT2 = two([128, 256], BF16), two([128, 2, 128], BF16), two([128, 16, 128], BF16)
            h12, h1b2, h1T2 = two([128, D], F32), two([128, D], BF16), two([128, 8, 128], BF16)
            gate2, junk2 = two([128, D], F32), two([128, D], BF16)
            tq4 = [al.get([128, D], F32) for _ in range(4)]
            ssq4 = al.get([128, 4], F32)
            rs4 = al.get([128, 4], F32)
            T_LOCAL = ("ptb", "pT", "ycT", "h1", "h1b", "h1T", "gate", "junkb")
            ot = [al.get([128, D], F32) for _ in range(2)]
            for j in range(L // 128):
                b = j % 2
                S.kmap.update({n_: b for n_ in T_LOCAL})
                ptb, pT, ycT, h1, h1b, h1T = ptb2[b], pT2[b], ycT2[b], h12[b], h1b2[b], h1T2[b]
                gate, junk = gate2[b], junk2[b]
                s4 = j % 4
                tq = tq4[s4]
                DMA(xt[b], x_d[seq][128 * j:128 * (j + 1), :], [], [("xt", b)], "xt%d" % b)
                DMA(yc[b], YCAT[128 * j:128 * (j + 1), :], [], [("yc", b)], "ycl%d" % b)
                DMA(pt[b], pl_d[seq][128 * j:128 * (j + 1), :], [], [("pt", b)], "pt%d" % b)
                for half in range(2):
                    for i in range(8):
                        kt = 8 * half + i
                        tr(pbb(half, 128 * i, 128 * (i + 1)), yc[b][:, 128 * kt:128 * (kt + 1)], identb,
                           [("yc", b), "identb"], [("ptr", half)])
                    cp("vector" if half == 0 else "scalar", ycT[:, 8 * half:8 * half + 8, :],
                       pbb(half, 0, 1024).rearrange("p (a b) -> p a b", a=8), [("ptr", half)], [("ycT", half)])
                for nh in range(2):
                    for kt in range(16):
                        mm(pbf(2 + nh, 0, 512), ycT[:, kt, :], wo[:, kt, 512 * nh:512 * (nh + 1)], kt == 0, kt == 15,
                           [("ycT", kt // 8)], [("ph", nh)])
                    tt("vector", h1[:, 512 * nh:512 * (nh + 1)], pbf(2 + nh, 0, 512), xt[b][:, 512 * nh:512 * (nh + 1)], ALU.add,
                       [("ph", nh), ("xt", b)], [("h1", nh)])
                cp("scalar", h1b, h1, [("h1", 0), ("h1", 1)], ["h1b"])
                for kt in range(8):
                    tr(pbb(4, 128 * kt, 128 * (kt + 1)), h1b[:, 128 * kt:128 * (kt + 1)], identb, ["h1b", "identb"], ["ph1T"])
                cp("vector", h1T, pbb(4, 0, 1024).rearrange("p (a b) -> p a b", a=8), ["ph1T"], ["h1T"])
                cp("gpsimd", ptb, pt[b], [("pt", b)], ["ptb"])
                for kt in range(2):
                    tr(pbb(5, 128 * kt, 128 * (kt + 1)), ptb[:, 128 * kt:128 * (kt + 1)], identb, ["ptb", "identb"], ["ppT"])
                cp("vector", pT, pbb(5, 0, 256).rearrange("p (a b) -> p a b", a=2), ["ppT"], ["pT"])
                for nh in range(2):
                    for kt in range(8):
                        mm(pbf(6 + nh, 0, 512), h1T[:, kt, :], wg_[:, kt, 512 * nh:512 * (nh + 1)], kt == 0, kt == 7,
                           ["h1T"], [("pg", nh)])
                    act("scalar", gate[:, 512 * nh:512 * (nh + 1)], pbf(6 + nh, 0, 512), AF.Tanh, [("pg", nh)], [("gate", nh)], scale=0.5)
                    for kt in range(2):
                        mm(pbf(2 + nh, 0, 512), pT[:, kt, :], wp[:, kt, 512 * nh:512 * (nh + 1)], kt == 0, kt == 1,
                           ["pT"], [("ph", nh)])
                    stt("vector", tq[:, 512 * nh:512 * (nh + 1)], gate[:, 512 * nh:512 * (nh + 1)], 1.0, pbf(2 + nh, 0, 512),
                        ALU.add, ALU.mult, [("ph", nh), ("gate", nh)], [("tq4", s4, nh)])
                    tt("vector", tq[:, 512 * nh:512 * (nh + 1)], tq[:, 512 * nh:512 * (nh + 1)], h1[:, 512 * nh:512 * (nh + 1)],
                       ALU.add, [("tq4", s4, nh), ("h1", nh)], [("tq4", s4, nh)])
                act("scalar", junk, tq, AF.Square, [("tq4", s4, 0), ("tq4", s4, 1)], ["junkb", ("ssq4", s4)], accum_out=ssq4[:, s4:s4 + 1])
                if j % 2 == 1:
                    pr = (s4 // 2) * 2
                    rsp = rs4[:, pr:pr + 2]
                    ts("vector", rsp, ssq4[:, pr:pr + 2], 1.0 / D, EPS, ALU.mult, ALU.add, [("ssq4", pr), ("ssq4", pr + 1)], [("rs4", pr)])
                    V(lambda e, rsp=rsp: e.reciprocal(out=rsp, in_=rsp), [("rs4", pr)], [("rs4", pr)])
                    act("scalar", rsp, rsp, AF.Sqrt, [("rs4", pr)], [("rs4", pr)])
                    for jj in (j - 1, j):
                        sj = jj % 4
                        bb = jj % 2
                        stt("vector", ot[bb], tq4[sj], rs4[:, sj:sj + 1], fnw, ALU.mult, ALU.mult,
                            [("tq4", sj, 0), ("tq4", sj, 1), ("rs4", pr), "fnw"], [("ot", bb)])
                        DMA(y_d[seq][128 * jj:128 * (jj + 1), :], ot[bb], [("ot", bb)], [], "ot%d" % bb)
            S.kmap.clear()
            S.barrier()

        import os
        dbg = os.environ.get("KDEBUG", "")
        for seq, L in seqs:
            pass0(seq, L)
            S.barrier()
            for g in range(NG):
                if dbg and g > 0 and "4" not in dbg:
                    continue
                if dbg and "g" not in dbg:
                    continue
                load_group(g, L)
                if not dbg or "F" in dbg:
                    passF(seq, L)
                if not dbg or "A" in dbg:
                    passA(seq, L)
                if not dbg or "B" in dbg:
                    passB(seq, L, g)
            if not dbg or "T" in dbg:
                tw = load_tail_weights()
                tail(seq, L, *tw)
        S.emit(eng_sems, block)
    return nc


_CACHE = {}


def _prep_weights(w_in, conv_w, conv_b, a_log_f, a_log_b, dt_bias_f, dt_bias_b, d_skip, ssd_norm_w):
    w = w_in[0]
    cols = []
    for g in range(NG):
        idx = np.concatenate([
            np.arange(128 * g, 128 * g + 128),
            1024 + np.arange(384 * g, 384 * g + 384),
            512 + np.arange(128 * g, 128 * g + 128),
            2560 + np.arange(384 * g, 384 * g + 384),
            2560 + 1536 + np.arange(128 * g, 128 * g + 128),
            2560 + 2048 + np.arange(128 * g, 128 * g + 128),
            5120 + np.arange(6 * g, 6 * g + 6),
            5144 + np.arange(6 * g, 6 * g + 6),
        ])
        cols.append(idx)
    w_in_g = np.ascontiguousarray(np.stack([w[:, c] for c in cols], 0))
    cw, cbias = conv_w[0], conv_b[0]
    convw_g = np.zeros((NG, 128, 25), np.float32)
    convb_g = np.zeros((NG, 128, 5), np.float32)
    for g in range(NG):
        ch = np.concatenate([np.arange(384 * g, 384 * g + 384), 1536 + np.arange(128 * g, 128 * g + 128),
                             2048 + np.arange(128 * g, 128 * g + 128)])
        for t in range(5):
            cht = ch[128 * t:128 * (t + 1)]
            convw_g[g, :, 5 * t:5 * t + 5] = cw[:, cht].T
            convb_g[g, :, t] = cbias[cht]
    cbrow_g = np.ascontiguousarray(convb_g.transpose(0, 2, 1).reshape(NG, 640))
    sl = lambda v, g: v[0][6 * g:6 * g + 6]
    dtb_g = np.stack([np.concatenate([sl(dt_bias_f, g), sl(dt_bias_b, g)]) for g in range(NG)], 0)
    alog_g = np.stack([np.concatenate([sl(a_log_f, g), sl(a_log_b, g)]) for g in range(NG)], 0)
    dsk_g = np.stack([sl(d_skip, g) for g in range(NG)], 0)
    snw_g = np.ascontiguousarray(ssd_norm_w[0].reshape(NG, 384))
    return dict(w_in_g=w_in_g, convw_g=convw_g, convb_g=convb_g, cbrow_g=cbrow_g, dtb_g=np.ascontiguousarray(dtb_g, np.float32),
                alog_g=np.ascontiguousarray(alog_g, np.float32), dsk_g=np.ascontiguousarray(dsk_g, np.float32), snw_g=snw_g)


def run(x_prompt, x_sample, p_prompt, p_sample, norm_w, w_in, w_fmix, conv_w, conv_b, a_log_f, a_log_b,
        dt_bias_f, dt_bias_b, d_skip, ssd_norm_w, w_out, w_ple_in, w_ple_gate, final_norm_w):
    f = lambda a: np.ascontiguousarray(np.asarray(a, dtype=np.float32))
    x_prompt, x_sample, p_prompt, p_sample = f(x_prompt), f(x_sample), f(p_prompt), f(p_sample)
    Bp, Lp, _ = x_prompt.shape
    Bs, Ls, _ = x_sample.shape
    assert Ls == 2 * Lp
    if Ls not in _CACHE:
        _CACHE[Ls] = build(Ls)
    nc = _CACHE[Ls]
    common = _prep_weights(f(w_in), f(conv_w), f(conv_b), f(a_log_f), f(a_log_b), f(dt_bias_f), f(dt_bias_b),
                           f(d_skip), f(ssd_norm_w))
    masks, c64 = _masks()
    cs1, t21 = _consts(Ls, False)
    cs2, t22 = _consts(Ls, True)
    common.update(wfm=f(w_fmix)[0], w_out=f(w_out)[0], w_ple_in=f(w_ple_in)[0], w_ple_gate=f(w_ple_gate)[0],
                  fnw=f(final_norm_w), normw=f(norm_w)[0], masks=masks, c64=c64)
    import os
    n_dual = int(os.environ.get("KNDUAL", 8 - Bs))
    slots = []
    for c in range(Bs):
        slots.append(("s", c, None))
    pr = list(range(Bp))
    pairs = [[] for _ in range(n_dual)]
    for i, b in enumerate(pr):
        pairs[i % n_dual].append(b)
    assert all(len(p_) <= 2 for p_ in pairs), "prompt batch does not fit the slots"
    for p_ in pairs:
        a = p_[0] if len(p_) > 0 else 0
        b = p_[1] if len(p_) > 1 else a
        slots.append(("p", a, b if len(p_) > 1 else None, b))
    in_maps = []
    for sl in slots:
        m = dict(common)
        if sl[0] == "s":
            m["x_s"] = x_sample[sl[1]]
            m["pl_s"] = p_sample[0, sl[1]]
            m["cs_s"], m["t2_s"] = cs1, t21
            m["flag"] = np.ones((1,), np.float32)
        else:
            a, b = sl[1], sl[3]
            m["x_s"] = np.ascontiguousarray(np.concatenate([x_prompt[a], x_prompt[b]], 0))
            m["pl_s"] = np.ascontiguousarray(np.concatenate([p_prompt[0, a], p_prompt[0, b]], 0))
            m["cs_s"], m["t2_s"] = cs2, t22
            m["flag"] = np.zeros((1,), np.float32)
        in_maps.append(m)
    while len(in_maps) < 8:
        in_maps.append(in_maps[-1])
    res = run_bass_kernel_spmd(nc, in_maps, core_ids=list(range(8)))
    ys = np.stack([res.results[c]["y_s"] for c in range(Bs)], 0)
    yp = np.zeros((Bp, Lp, D), np.float32)
    for c, sl in enumerate(slots):
        if sl[0] == "p":
            o = res.results[c]["y_s"]
            yp[sl[1]] = o[:Lp]
            if sl[2] is not None:
                yp[sl[2]] = o[Lp:]
    return yp, ys.astype(np.float32)


def kernel(**inputs):
    return run(**inputs)
```

```python
import math
from contextlib import ExitStack
import numpy as np
import ml_dtypes
import concourse.bass as bass
import concourse.mybir as mybir
from concourse.bass_utils import run_bass_kernel_spmd

F32 = mybir.dt.float32
BF16 = mybir.dt.bfloat16
ALU = mybir.AluOpType
AF = mybir.ActivationFunctionType

D = 1024
NG = 4
GC = 1292
COL_Z, COL_UF, COL_XS, COL_B, COL_C, COL_DT = 0, 512, 640, 1024, 1152, 1280
EPS = 1e-6
SAME_ENGINE_RAW_SYNC = True
LIST_SCHEDULE = True
PRIO_CRITICAL = True
FILL_MIN_GAP = 700.0
FILL_SLACK = 250.0
FILL_COST = 240.0


class Op:
    __slots__ = ("eng", "fn", "deps", "odeps", "inc", "mile", "sem", "is_dma", "idx", "cost", "users", "npend", "ready", "fin", "blev")

    def __init__(self, eng, fn, is_dma=False, sem=None, cost=100.0):
        self.eng, self.fn, self.is_dma, self.sem, self.cost = eng, fn, is_dma, sem, cost
        self.deps = set()
        self.odeps = set()
        self.inc = False
        self.mile = None


class Sched:
    ENGS = ("sync", "scalar", "vector", "gpsimd", "tensor")

    def __init__(self, nc):
        self.nc = nc
        self.final = []
        self.seg = []
        self.last_w = {}
        self.readers = {}
        self.last_on = {}
        self.dmas_since = []
        self.excl_last = {}
        self.nops = 0
        self.kmap = {}
        self.filler = None
        self.nfill = 0

    def _xk(self, k):
        n = k if isinstance(k, str) else k[0]
        sfx = self.kmap.get(n)
        return k if sfx is None else (k, sfx)

    def _edge(self, op, d, raw):
        if d is op:
            return
        if d.is_dma or op.is_dma or d.eng != op.eng:
            op.deps.add(d)
        elif d.eng != "tensor" and SAME_ENGINE_RAW_SYNC:
            op.deps.add(d)
        else:
            op.odeps.add(d)

    def add(self, eng, fn, reads=(), writes=(), dma_sem=None, extra_deps=(), excl=(), cost=100.0):
        op = Op(eng, fn, dma_sem is not None, dma_sem, cost)
        op.idx = self.nops
        self.nops += 1
        if self.kmap:
            reads = [self._xk(k) for k in reads]
            writes = [self._xk(k) for k in writes]
        for k in excl:
            d = self.excl_last.setdefault(k, {})
            for e2, o2 in d.items():
                self._edge(op, o2, False)
            d[eng] = op
        for b in reads:
            w = self.last_w.get(b)
            if w is not None:
                self._edge(op, w, True)
        for b in writes:
            w = self.last_w.get(b)
            if w is not None:
                self._edge(op, w, False)
            for r in self.readers.get(b, ()):
                self._edge(op, r, False)
        for d in extra_deps:
            op.deps.add(d)
        for b in reads:
            self.readers.setdefault(b, []).append(op)
        for b in writes:
            self.last_w[b] = op
            self.readers[b] = []
        self.seg.append(op)
        if op.is_dma:
            self.dmas_since.append(op)
        else:
            self.last_on[eng] = op
        return op

    def _schedule_segment(self):
        import heapq
        seg = self.seg
        self.seg = []
        if not LIST_SCHEDULE:
            self.final.extend(seg)
            return
        inseg = set(id(o) for o in seg)
        for o in seg:
            o.users = []
            o.npend = 0
            o.ready = 0.0
        for o in seg:
            for d in list(o.deps) + list(o.odeps):
                if id(d) in inseg:
                    d.users.append(o)
                    o.npend += 1
        for o in reversed(seg):
            bl = 0.0
            for u in o.users:
                if u.blev > bl:
                    bl = u.blev
            o.blev = bl + o.cost
        if PRIO_CRITICAL:
            for o in seg:
                o.idx = (-o.blev, o.idx)
        future = {e: [] for e in self.ENGS}
        avail = {e: [] for e in self.ENGS}
        free = {e: 0.0 for e in self.ENGS}
        for o in seg:
            if o.npend == 0:
                heapq.heappush(future[o.eng], (0.0, o.idx, o))
        out = []
        n = len(seg)
        while len(out) < n:
            best = None
            for e in self.ENGS:
                fq, aq = future[e], avail[e]
                t0 = free[e]
                while fq and fq[0][0] <= t0:
                    r_, i_, o_ = heapq.heappop(fq)
                    heapq.heappush(aq, (i_, r_, o_))
                if aq:
                    st, key = t0, aq[0][0]
                elif fq:
                    st, key = fq[0][0], fq[0][1]
                else:
                    continue
                if best is None or (st, key) < (best[0], best[1]):
                    best = (st, key, e)
            st, key, e = best
            if avail[e]:
                o = heapq.heappop(avail[e])[2]
            else:
                o = heapq.heappop(future[e])[2]
            if e == "tensor" and self.filler is not None and free[e] > 0.0:
                gap = st - free[e]
                if gap > FILL_MIN_GAP:
                    for _ in range(min(int((gap - FILL_SLACK) / FILL_COST), 24)):
                        fo = Op("tensor", self.filler, False, None, FILL_COST)
                        fo.idx = -1
                        out.append(fo)
                        n += 1
                        self.nfill += 1
            if o.is_dma:
                free[e] = st + 60.0
                o.fin = st + o.cost
            else:
                free[e] = st + o.cost
                o.fin = free[e] + 40.0
            out.append(o)
            for u in o.users:
                u.npend -= 1
                if o.fin > u.ready:
                    u.ready = o.fin
                if u.npend == 0:
                    heapq.heappush(future[u.eng], (u.ready, u.idx, u))
        self.final.extend(out)

    def barrier(self):
        dmas = list(self.dmas_since)
        self.dmas_since = []
        self._schedule_segment()
        self.filler = None
        last = {}
        for o in self.final[::-1]:
            if not o.is_dma and o.eng not in last:
                last[o.eng] = o
                if len(last) == 4:
                    break
        deps = list(last.values()) + dmas
        for e in self.ENGS:
            self.add(e, lambda eng: eng.nop(), extra_deps=deps, cost=30.0)
        self._schedule_segment()
        self.last_w.clear()
        self.readers.clear()
        self.excl_last.clear()

    def emit(self, eng_sems, block):
        self._schedule_segment()
        ops = self.final
        for op in ops:
            for d in op.deps:
                if not d.is_dma:
                    d.inc = True
        cnt = {e: 0 for e in self.ENGS}
        dcnt = {}
        for op in ops:
            if op.is_dma:
                dcnt[op.sem] = dcnt.get(op.sem, 0) + 16
                op.mile = dcnt[op.sem]
            elif op.inc:
                cnt[op.eng] += 1
                op.mile = cnt[op.eng]
        by_eng = {e: [o for o in ops if o.eng == e] for e in self.ENGS}

        def run(eng_name, eng):
            waited = {}
            for op in by_eng[eng_name]:
                need = {}
                for d in op.deps:
                    s = d.sem if d.is_dma else eng_sems[d.eng]
                    if need.get(s, 0) < d.mile:
                        need[s] = d.mile
                for s, v in need.items():
                    if waited.get(s, 0) < v:
                        eng.wait_ge(s, v)
                        waited[s] = v
                ins = op.fn(eng)
                if op.is_dma:
                    ins.then_inc(op.sem, 16)
                elif op.inc:
                    ins.then_inc(eng_sems[eng_name], 1)

        @block.sync
        def _(e):
            run("sync", e)

        @block.scalar
        def _(e):
            run("scalar", e)

        @block.vector
        def _(e):
            run("vector", e)

        @block.gpsimd
        def _(e):
            run("gpsimd", e)

        @block.tensor
        def _(e):
            run("tensor", e)


def _consts(L, dual):
    C = L // 128
    Ch = C // 2
    j = np.arange(C)[:, None].astype(np.float64)
    k2 = np.arange(C)[None, :].astype(np.float64)
    if not dual:
        ang = 2 * np.pi * j * k2 / C
        cs = np.concatenate([np.cos(ang), np.sin(ang)], axis=1)
    else:
        cs = np.zeros((C, 2 * C))
        jj = np.arange(Ch)[:, None].astype(np.float64)
        rr = np.arange(Ch)[None, :].astype(np.float64)
        ang = 2 * np.pi * jj * rr / Ch
        cs[0:Ch, 0:Ch] = np.cos(ang)
        cs[0:Ch, C:C + Ch] = np.sin(ang)
        cs[Ch:C, Ch:C] = np.cos(ang)
        cs[Ch:C, C + Ch:2 * C] = np.sin(ang)
    p = np.arange(128)[:, None, None].astype(np.float64)
    kk2 = np.arange(C)[None, :, None].astype(np.float64)
    k1 = np.arange(128)[None, None, :].astype(np.float64)
    t2 = np.zeros((128, C, 2, 4, 128))
    if not dual:
        a2 = 2 * np.pi * ((p * (C * k1 + kk2)) % L) / L
        mc, ms = np.cos(a2), np.sin(a2)
        t2[:, :, 0] = np.stack([mc, ms, -ms, mc], axis=2)
    else:
        Lh = L // 2
        k1a = np.arange(128)[None, None, :]
        isA = (k1a < 64)
        freq = np.where(isA, C * k1 + kk2, C * (k1 - 64) + kk2)
        a2 = 2 * np.pi * ((p * freq) % Lh) / Lh
        mc, ms = np.cos(a2), np.sin(a2)
        full = np.stack([mc, ms, -ms, mc], axis=2) * math.sqrt(2.0)
        slotA = (np.arange(C) < Ch)[None, :, None, None]
        colA = isA[:, :, None, :] if isA.ndim == 3 else isA
        colA = np.broadcast_to((np.arange(128) < 64)[None, None, None, :], full.shape)
        own_mask = np.where(slotA, colA, ~colA)
        t2[:, :, 0] = np.where(own_mask, full, 0.0)
        t2[:, :, 1] = np.where(own_mask, 0.0, full)
    return cs.astype(ml_dtypes.bfloat16), np.ascontiguousarray(t2.reshape(128, C, 1024)).astype(ml_dtypes.bfloat16)


def _masks():
    k = np.arange(128)[:, None]
    s = np.arange(128)[None, :]
    m = np.stack([(k <= s), (k >= s), (k < s), (k > s), np.ones((128, 128), bool), np.eye(128, dtype=bool)], 0)
    c = np.arange(64)[:, None] * np.arange(64)[None, :]
    c64 = np.cos(2 * np.pi * c / 64)
    s64 = np.sin(2 * np.pi * c / 64)
    z = np.zeros((64, 64))
    cb = np.block([[c64, z], [z, c64]])
    sb = np.block([[s64, z], [z, s64]])
    out = []
    for t in (cb, sb):
        r = t.astype(np.float32)
        for _ in range(3):
            h = r.astype(ml_dtypes.bfloat16)
            out.append(h)
            r = (r - h.astype(np.float32)).astype(np.float32)
    return m.astype(ml_dtypes.bfloat16), np.stack(out, 0)


def build(Ls):
    nc = bass.Bass("TRN2", target_bir_lowering=False)
    LMAX = Ls
    seqs = [("s", Ls)]

    def din(name, shape, dt=F32):
        return nc.dram_tensor(name, shape, dt, kind="ExternalInput").ap()

    x_d = {"s": din("x_s", [Ls, D])}
    pl_d = {"s": din("pl_s", [Ls, 256])}
    y_d = {"s": nc.dram_tensor("y_s", [Ls, D], F32, kind="ExternalOutput").ap()}
    flag_d = din("flag", [1])
    win_d = din("w_in_g", [NG, D, GC])
    convw_d = din("convw_g", [NG, 128, 25])
    convb_d = din("convb_g", [NG, 128, 5])
    cbrow_d = din("cbrow_g", [NG, 640])
    dtb_d = din("dtb_g", [NG, 12])
    alog_d = din("alog_g", [NG, 12])
    dsk_d = din("dsk_g", [NG, 6])
    snw_d = din("snw_g", [NG, 384])
    wfm_d = din("wfm", [8, 64, 64])
    wout_d = din("w_out", [2048, D])
    wpi_d = din("w_ple_in", [256, D])
    wpg_d = din("w_ple_gate", [D, D])
    fnw_d = din("fnw", [D])
    normw_d = din("normw", [D])
    masks_d = din("masks", [6, 128, 128], BF16)
    c64_d = din("c64", [6, 128, 128], BF16)
    cs_d = {"s": din("cs_s", [Ls // 128, 2 * (Ls // 128)], BF16)}
    t2_d = {"s": din("t2_s", [128, Ls // 128, 1024], BF16)}
    UT = nc.dram_tensor("UT", [8, 128, LMAX], BF16).ap()
    YCAT = nc.dram_tensor("YCAT", [LMAX, 2048], BF16).ap()
    YF = nc.dram_tensor("YFs", [LMAX, 128], F32).ap()
    XSBs = nc.dram_tensor("XSBs", [LMAX, 512], BF16).ap()
    BCTs = nc.dram_tensor("BCTs", [2, 128, LMAX], BF16).ap()
    DTPs = nc.dram_tensor("DTPs", [LMAX // 512, 128, 96], F32).ap()

    es = ExitStack()
    with es:
        S = Sched(nc)
        eng_sems = {e: es.enter_context(nc.semaphore("s_" + e)) for e in Sched.ENGS}
        dsems = {}

        def dsem(name):
            if name not in dsems:
                dsems[name] = es.enter_context(nc.semaphore("d_" + name))
            return dsems[name]

        def banks(*aps):
            out = set()
            for a in aps:
                try:
                    nm = a.tensor.name
                except Exception:
                    continue
                if nm.startswith("pb"):
                    out.add(nm)
            return out

        def fsz(ap):
            n = 1
            for d_ in ap.shape[1:]:
                n *= d_
            return n

        def ecost(eng, ap, *ins):
            n = fsz(ap)
            slow = 1.0
            for a_ in ins:
                try:
                    if a_.ap[-1][0] == 0 and a_.ap[-1][1] > 1:
                        slow = 1.0
                except Exception:
                    pass
            if eng == "scalar":
                return 220.0 + n / 1.4
            if eng == "vector":
                return 70.0 + slow * n / 0.96
            return 120.0 + slow * n / 0.7

        def V(fn, r=(), w=()):
            return S.add("vector", fn, r, w, cost=100.0)

        def G(fn, r=(), w=()):
            return S.add("gpsimd", fn, r, w, cost=150.0)

        def DMA(out, in_, r, w, sem, **kw):
            nbytes = out.shape[0] * fsz(out) * (4 if out.dtype == F32 else 2)
            return S.add("sync", lambda e: e.dma_start(out=out, in_=in_, **kw), r, w, dma_sem=dsem(sem),
                         cost=2500.0 + nbytes / 60.0)

        def mm(out, lhsT, rhs, start, stop, r, w):
            bk = banks(out)
            return S.add("tensor", lambda e: e.matmul(out, lhsT=lhsT, rhs=rhs, start=start, stop=stop), r,
                         list(w) + [("accb", b_) for b_ in bk], excl=bk, cost=28.0 + fsz(rhs) * 0.45)

        def tr(out, in_, ident, r, w):
            bk = banks(out)
            return S.add("tensor", lambda e: e.transpose(out=out, in_=in_, identity=ident), r,
                         list(w) + [("accb", b_) for b_ in bk], excl=bk, cost=110.0)

        def act(eng, out, in_, func, r, w, **kw):
            return S.add(eng, lambda e: e.activation(out=out, in_=in_, func=func, **kw), r, w, excl=banks(out, in_),
                         cost=ecost(eng, out))

        def cp(eng, out, in_, r, w):
            if eng == "scalar":
                return S.add(eng, lambda e: e.activation(out=out, in_=in_, func=AF.Copy), r, w, excl=banks(out, in_),
                             cost=ecost(eng, out))
            return S.add(eng, lambda e: e.tensor_copy(out=out, in_=in_), r, w, excl=banks(out, in_), cost=ecost(eng, out))

        def tt(eng, out, in0, in1, op, r, w):
            return S.add(eng, lambda e: e.tensor_tensor(out=out, in0=in0, in1=in1, op=op), r, w, excl=banks(out, in0, in1),
                         cost=ecost(eng, out, in0, in1))

        def ts(eng, out, in0, s1, s2, op0, op1, r, w, **kw):
            if s2 is None:
                return S.add(eng, lambda e: e.tensor_scalar(out=out, in0=in0, scalar1=s1, scalar2=None, op0=op0, **kw), r, w,
                             excl=banks(out, in0), cost=ecost(eng, out))
            return S.add(eng, lambda e: e.tensor_scalar(out=out, in0=in0, scalar1=s1, scalar2=s2, op0=op0, op1=op1, **kw), r, w,
                         excl=banks(out, in0), cost=ecost(eng, out))

        def split3(eng, src, dst3, tmp, r, w):
            cp(eng, dst3[0], src, r, [w + "0"])
            tt(eng, tmp, src, dst3[0], ALU.subtract, list(r) + [w + "0"], [w + "t"])
            cp(eng, dst3[1], tmp, [w + "t"], [w + "1"])
            tt(eng, tmp, tmp, dst3[1], ALU.subtract, [w + "t", w + "1"], [w + "t"])
            cp(eng, dst3[2], tmp, [w + "t"], [w + "2"])

        def stt(eng, out, in0, scalar, in1, op0, op1, r, w):
            return S.add(eng, lambda e: e.scalar_tensor_tensor(out=out, in0=in0, scalar=scalar, in1=in1, op0=op0, op1=op1), r, w,
                         excl=banks(out, in0, in1), cost=ecost(eng, out))

        ARENA = 207 * 1024
        arena = es.enter_context(nc.sbuf_tensor("arena", [128, ARENA // 2], BF16))

        class Alloc:
            def __init__(self, base=0):
                self.off = base

            def get(self, shape, dt):
                n = 1
                for s_ in shape[1:]:
                    n *= s_
                nb = n * (4 if dt == F32 else 2)
                nb = (nb + 63) // 64 * 64
                assert self.off + nb <= ARENA, (self.off, nb)
                v = arena[:, self.off // 2:(self.off + nb) // 2]
                self.off += nb
                if dt == F32:
                    v = v.bitcast(F32)
                v = v[:, 0:n]
                if len(shape) == 3:
                    v = v.rearrange("p (a b) -> p a b", a=shape[1])
                elif len(shape) == 4:
                    v = v.rearrange("p (a b c) -> p a b c", a=shape[1], b=shape[2])
                if shape[0] < 128:
                    v = v[0:shape[0]]
                return v

        pers = Alloc(0)
        masksb = pers.get([128, 6, 128], BF16)
        bLE, bGE, bLT, bGT, bONE, identb = (masksb[:, i, :] for i in range(6))
        gtb, ltb = bGT, bLT
        c64 = pers.get([128, 6, 128], BF16)
        wf3 = pers.get([128, 3, 128], BF16)
        wftmp = pers.get([128, 128], F32)
        normw_col = pers.get([128, 8], F32)
        flag_col = pers.get([128, 1], F32)
        Wg = pers.get([128, 8, GC], BF16)
        convw = pers.get([128, 25], F32)
        convb = pers.get([128, 5], F32)
        Dg = pers.get([128, 25, 128], BF16)
        DgD = pers.get([128, 6, 128], BF16)
        convwh = pers.get([128, 25], F32)
        cbh = pers.get([128, 5], F32)
        cbrow = pers.get([1, 640], BF16)
        cbrow_f = pers.get([1, 640], F32)
        ones_row = pers.get([1, 512], BF16)
        dtb_bc = pers.get([128, 12], F32)
        A_bc = pers.get([128, 12], F32)
        dsk_bc = pers.get([128, 6], F32)
        snw_bc = pers.get([128, 384], F32)
        wfblk = pers.get([128, 128], F32)
        W1b = pers.get([128, 128], BF16)
        W2nb = pers.get([128, 128], BF16)
        HBASE = pers.off
        Hst = pers.get([128, LMAX // 128, 384], BF16)
        PBASE = pers.off

        pb = [es.enter_context(nc.psum_tensor("pb%d" % i, [128, 512], F32)) for i in range(8)]

        def pbf(i, a, b):
            return pb[i][:, a:b]

        def pbb(i, a, b):
            return pb[i][:, a // 2:b // 2].bitcast(BF16)

        block = es.enter_context(nc.Block())

        DMA(masksb, masks_d.rearrange("m k s -> k m s"), [], ["identb", "gtb", "ltb", "masksb", "masks"], "c0")
        DMA(c64, c64_d.rearrange("m k s -> k m s"), [], ["c64"], "c1")
        DMA(normw_col, normw_d.rearrange("(kt k) -> k kt", k=128), [], ["normw_col"], "c2", allow_slow_non_contiguous=True)
        DMA(flag_col, flag_d.partition_broadcast(128), [], ["flag"], "c3")
        V(lambda e: e.memset(ones_row, 1.0), [], ["ones_row"])
        S.barrier()

        def load_uT(seq, L, sc, buf, key, sem):
            t0 = 512 * sc - 2
            lo = max(t0, 0)
            hi = min(t0 + 516, L)
            src = UT[:, :, lo:hi].rearrange("kt k t -> k kt t")
            return DMA(buf[:, :, lo - t0:hi - t0], src, [("UT", sc - 1), ("UT", sc), ("UT", sc + 1)], [key], sem)

        def pass0(seq, L):
            al = Alloc(PBASE)
            xt = [al.get([128, D], F32) for _ in range(2)]
            junk = al.get([128, D], BF16)
            ub = [al.get([128, D], BF16) for _ in range(2)]
            ss = [al.get([128, 1], F32) for _ in range(2)]
            rstd = [al.get([128, 1], F32) for _ in range(2)]
            uts = [al.get([128, 8, 512], BF16) for _ in range(2)]
            NSC = L // 512
            for sc in range(NSC):
                for c in range(4):
                    j = 4 * sc + c
                    b = j % 2
                    DMA(xt[b], x_d[seq][128 * j:128 * (j + 1), :], [], [("xt", b)], "xt%d" % b)
                    act("scalar", junk, xt[b], AF.Square, [("xt", b)], ["junk", ("ss", b)], accum_out=ss[b])
                    ts("vector", rstd[b], ss[b], 1.0 / D, EPS, ALU.mult, ALU.add, [("ss", b)], [("rstd", b)])
                    V(lambda e, b=b: e.reciprocal(out=rstd[b], in_=rstd[b]), [("rstd", b)], [("rstd", b)])
                    act("scalar", rstd[b], rstd[b], AF.Sqrt, [("rstd", b)], [("rstd", b)])
                    act("scalar", ub[b], xt[b], AF.Copy, [("xt", b), ("rstd", b)], [("ub", b)], scale=rstd[b])
                    bank = j % 2
                    for kt in range(8):
                        tr(pbb(bank, 128 * kt, 128 * (kt + 1)), ub[b][:, 128 * kt:128 * (kt + 1)], identb,
                           [("ub", b), "identb"], [("pT", bank)])
                    cp("vector", uts[sc % 2][:, :, 128 * c:128 * (c + 1)],
                       pbb(bank, 0, 1024).rearrange("p (a b) -> p a b", a=8), [("pT", bank)], [("uts", sc % 2, c)])
                DMA(UT[:, :, 512 * sc:512 * (sc + 1)].rearrange("kt k t -> k kt t"), uts[sc % 2],
                    [("uts", sc % 2, c) for c in range(4)], [("UT", sc)], "uts%d" % (sc % 2))

        def load_group(g, L):
            import os
            dbg = os.environ.get("KDEBUG", "")
            al = Alloc(PBASE)
            wst = [al.get([128, GC], F32) for _ in range(2)]
            tmp12 = al.get([128, 12], F32)
            if not dbg or "w" in dbg:
              for kt in range(8):
                b = kt % 2
                DMA(wst[b], win_d[g, 128 * kt:128 * (kt + 1), :], [], [("wst", b)], "wst%d" % b)
                eng_ = "vector" if kt % 2 == 0 else "gpsimd"
                ts(eng_, Wg[:, kt, 512:GC], wst[b][:, 512:GC], normw_col[:, kt:kt + 1], None, ALU.mult, None,
                   [("wst", b), "normw_col"], [("Wg", kt)])
                ts(eng_, Wg[:, kt, 0:512], wst[b][:, 0:512], normw_col[:, kt:kt + 1], 0.5, ALU.mult, ALU.mult,
                   [("wst", b), "normw_col"], [("Wg", kt)])
            if not dbg or "c" in dbg:
              DMA(convw, convw_d[g], [], ["convw"], "c0")
              DMA(convb, convb_d[g], [], ["convb"], "c1")
              for i25 in range(25):
                  ts("vector" if i25 % 2 == 0 else "gpsimd", Dg[:, i25, :], identb, convw[:, i25:i25 + 1], 0.5, ALU.mult, ALU.mult,
                     ["identb", "convw"], [("Dg", i25)])
              ts("vector", convwh, convw, 0.5, None, ALU.mult, None, ["convw"], ["convwh"])
              ts("vector", cbh, convb, 0.5, None, ALU.mult, None, ["convb"], ["cbh"])
              DMA(cbrow_f, cbrow_d[g:g + 1, :], [], ["cbrow_f"], "c8")
              ts("vector", cbrow, cbrow_f, 0.5, None, ALU.mult, None, ["cbrow_f"], ["cbrow"])
              DMA(dtb_bc, dtb_d[g].partition_broadcast(128), [], ["dtb"], "c2")
              DMA(tmp12, alog_d[g].partition_broadcast(128), [], ["tmp12"], "c3")
              DMA(dsk_bc, dsk_d[g].partition_broadcast(128), [], ["dsk"], "c4")
              DMA(snw_bc, snw_d[g].partition_broadcast(128), [], ["snw"], "c5")
              for h_ in range(6):
                  ts("gpsimd", DgD[:, h_, :], identb, dsk_bc[:, h_:h_ + 1], None, ALU.mult, None, ["identb", "dsk"], [("DgD", h_)])
              act("scalar", A_bc, tmp12, AF.Exp, ["tmp12"], ["A_bc"])
              ts("vector", A_bc, A_bc, -1.0, None, ALU.mult, None, ["A_bc"], ["A_bc"])
            if not dbg or "f" in dbg:
              V(lambda e: e.memset(wfblk, 0.0), [], ["wfblk"])
              DMA(wfblk[0:64, 0:64], wfm_d[2 * g], [], ["wfblk"], "c6")
              DMA(wfblk[64:128, 64:128], wfm_d[2 * g + 1], [], ["wfblk"], "c7")
              sc_ = 1.0 / math.sqrt(64.0 * L)
              split3("vector", wfblk, [wf3[:, i, :] for i in range(3)], wftmp, ["wfblk"], "wf3")
              pairs = [(i, j_) for i in range(3) for j_ in range(3) if i + j_ <= 2]
              for m_ in range(2):
                  for n_, (i, j_) in enumerate(pairs):
                      mm(pbf(0, 128 * m_, 128 * (m_ + 1)), c64[:, 3 * m_ + i, :], wf3[:, j_, :], n_ == 0, n_ == len(pairs) - 1,
                         ["c64", "wf30", "wf31", "wf32"], ["w%dp" % (m_ + 1)])
              ts("vector", W1b, pbf(0, 0, 128), sc_, None, ALU.mult, None, ["w1p"], ["W1b"])
              ts("vector", W2nb, pbf(0, 128, 256), -sc_, None, ALU.mult, None, ["w2p"], ["W2nb"])
            S.barrier()

        def passF(seq, L):
            C = L // 128
            NSC = L // 512
            al = Alloc(PBASE)
            UF = al.get([128, C, 128], BF16)
            UTf = al.get([128, 128, 128], BF16)
            Asb = al.get([128, 2, C, 128], BF16)
            csb = al.get([128, 2 * C], BF16)
            uts = [al.get([128, 8, 516], BF16) for _ in range(2)]
            t2b = [al.get([128, 4, 1024], BF16) for _ in range(2)]
            pq = [al.get([128, 256], BF16) for _ in range(2)]
            yst = [al.get([128, 4, 128], F32) for _ in range(2)]
            DMA(csb[0:C, :], cs_d[seq], [], ["csb"], "c0")
            load_uT(seq, L, 0, uts[0], ("uts", 0), "uts0")
            for sc in range(NSC):
                if sc + 1 < NSC:
                    load_uT(seq, L, sc + 1, uts[(sc + 1) % 2], ("uts", (sc + 1) % 2), "uts%d" % ((sc + 1) % 2))
                u = uts[sc % 2]
                for c in range(4):
                    j = 4 * sc + c
                    bank = j % 2
                    for kt in range(8):
                        mm(pbf(bank, 0, 128), u[:, kt, 2 + 128 * c:2 + 128 * (c + 1)], Wg[:, kt, COL_UF:COL_UF + 128],
                           kt == 0, kt == 7, [("uts", sc % 2), ("Wg", kt)], [("pu", bank)])
                    cp("scalar" if j % 2 == 0 else "vector", UF[:, j, :], pbf(bank, 0, 128), [("pu", bank)], ["UF"])
            for cb in range(16):
                bank = 2 + cb % 2
                for i in range(8):
                    ch = 8 * cb + i
                    tr(pbb(bank, 128 * i, 128 * (i + 1))[0:C], UF[:, :, ch], identb, ["UF", "identb"], [("pt", bank)])
                cp("vector" if cb % 2 == 0 else "scalar", UTf[0:C, 8 * cb:8 * cb + 8, :],
                   pbb(bank, 0, 1024)[0:C].rearrange("p (a b) -> p a b", a=8), [("pt", bank)], [("UTf", cb)])
            nper = min(512 // (2 * C), 128)
            nb_ = 128 // nper
            for bi in range(nb_):
                bank = 4 + bi % 2
                for i in range(nper):
                    ch = bi * nper + i
                    mm(pbf(bank, 2 * C * i, 2 * C * (i + 1)), UTf[0:C, ch, :], csb[0:C, :], True, True,
                       [("UTf", ch // 8), "csb"], [("pa", bank)])
                cp("vector" if bi % 2 == 0 else "scalar", Asb[:, :, :, bi * nper:(bi + 1) * nper],
                   pbf(bank, 0, 2 * C * nper).rearrange("p (c r k) -> p r k c", c=nper, r=2),
                   [("pa", bank)], [("Asb", bi)])
            areads = [("Asb", bi) for bi in range(nb_)]
            NP = C // 4
            Ch = C // 2
            DMA(t2b[0], t2_d[seq][:, 0:4, :], [], [("t2b", 0)], "t2b0")
            for pc in range(NP):
                if pc + 1 < NP:
                    DMA(t2b[(pc + 1) % 2], t2_d[seq][:, 4 * (pc + 1):4 * (pc + 2), :], [], [("t2b", (pc + 1) % 2)],
                        "t2b%d" % ((pc + 1) % 2))
                tb = t2b[pc % 2]
                ybank = pc % 2
                for kk in range(4):
                    k2 = 4 * pc + kk
                    kp = (k2 + Ch) % C
                    sl = k2 % 2
                    o = pbf(6 + sl, 0, 256)
                    srcs = ((0, k2, 0), (1, k2, 256), (0, kp, 512), (1, kp, 768))
                    for n_, (ri, kq, off) in enumerate(srcs):
                        mm(o, Asb[:, ri, kq, :], tb[:, kk, off:off + 256], n_ == 0, n_ == 3,
                           areads + [("t2b", pc % 2)], [("ppq", sl)])
                    cp("scalar" if sl == 0 else "vector", pq[sl], o, [("ppq", sl)], [("pq", sl)])
                    yo = pbf(ybank, 128 * kk, 128 * (kk + 1))
                    mm(yo, pq[sl][:, 0:128], W1b, True, False, [("pq", sl), "W1b"], [("py", ybank, kk)])
                    mm(yo, pq[sl][:, 128:256], W2nb, False, True, [("pq", sl), "W2nb"], [("py", ybank, kk)])
                cp("vector" if pc % 2 == 0 else "scalar", yst[pc % 2],
                   pbf(ybank, 0, 512).rearrange("p (a b) -> p a b", a=4),
                   [("py", ybank, qq) for qq in range(4)], [("yst", pc % 2)])
                DMA(YF[0:L].rearrange("(k1 k2) d -> k1 k2 d", k2=C)[:, 4 * pc:4 * pc + 4, :], yst[pc % 2],
                    [("yst", pc % 2)], [("YF", pc)], "yst%d" % (pc % 2))
            S.barrier()

        def ssd_front(u, ukey, WIN, ACC, CV, cvk, sc, NSC, tiles, cbanks, ACCD, WINP=None, qcur=0):
            for ti, t in enumerate(tiles):
                bank = ti % 2
                col = COL_XS + 128 * t
                for kt in range(8):
                    mm(pbf(bank, 0, 512), Wg[:, kt, col:col + 128], u[:, kt, 2:514], kt == 0, kt == 7,
                       [ukey, ("Wg", kt)], [("pfm", bank)])
                cp("scalar", WIN[:, t, 2:514], pbf(bank, 0, 512), [("pfm", bank)], [("WIN", t)])
                for side, (a, b_) in enumerate(((0, 2), (514, 516))):
                    edge = (sc == 0 and side == 0) or (sc == NSC - 1 and side == 1)
                    if edge:
                        G(lambda e, t=t, a=a, b_=b_: e.memset(WIN[:, t, a:b_], 0.0), [], [("WIN", t)])
                    elif side == 0 and WINP is not None:
                        src = WINP[:, t, 512:514]
                        rk = (("WIN", t), 1 - qcur)
                        if sc == NSC // 2:
                            ts("vector", WIN[:, t, a:b_], src, flag_col[:, 0:1], None, ALU.mult, None, [rk, "flag"], [("WIN", t)])
                        else:
                            cp("vector", WIN[:, t, a:b_], src, [rk], [("WIN", t)])
                    else:
                        hb = pbf(2, 4 * ti + 2 * side, 4 * ti + 2 * side + 2)
                        for kt in range(8):
                            mm(hb, Wg[:, kt, col:col + 128], u[:, kt, a:b_], kt == 0, kt == 7,
                               [ukey, ("Wg", kt)], [("phalo", ti, side)])
                        if (side == 0 and sc == NSC // 2) or (side == 1 and sc == NSC // 2 - 1):
                            ts("vector", WIN[:, t, a:b_], hb, flag_col[:, 0:1], None, ALU.mult, None,
                               [("phalo", ti, side), "flag"], [("WIN", t)])
                        else:
                            cp("vector", WIN[:, t, a:b_], hb, [("phalo", ti, side)], [("WIN", t)])
                if ti in (1, 3):
                    ab = ACCD[(ti // 2) % 2]
                    ak = ("ACCD", (ti // 2) % 2)
                    ts("vector", ab, WIN[:, t, 0:512], convwh[:, 5 * t:5 * t + 1], cbh[:, t:t + 1], ALU.mult, ALU.add,
                       [("WIN", t), "convwh", "cbh"], [ak])
                    for k in range(1, 5):
                        stt("vector", ab, WIN[:, t, k:k + 512], convwh[:, 5 * t + k:5 * t + k + 1], ab, ALU.mult, ALU.add,
                            [("WIN", t), "convwh", ak], [ak])
                    act("scalar", ACC[ti % 2], ab, AF.Tanh, [ak], [("TT", ti % 2)])
                    stt("vector", CV[:, t, :], ACC[ti % 2], 1.0, ab, ALU.add, ALU.mult, [("TT", ti % 2), ak], [(cvk, t)])
                    continue
                cbk = cbanks[(ti // 2) % 2]
                for k in range(5):
                    mm(pbf(cbk, 0, 512), Dg[:, 5 * t + k, :], WIN[:, t, k:k + 512], k == 0, False,
                       [("WIN", t), ("Dg", 5 * t + k)], [("pcv", cbk)])
                mm(pbf(cbk, 0, 512), cbrow[0:1, 128 * t:128 * (t + 1)], ones_row[0:1, 0:512], False, True,
                   ["cbrow", "ones_row"], [("pcv", cbk)])
                act("scalar", ACC[ti % 2], pbf(cbk, 0, 512), AF.Tanh, [("pcv", cbk)], [("TT", ti % 2)])
                stt("vector", CV[:, t, :], ACC[ti % 2], 1.0, pbf(cbk, 0, 512), ALU.add, ALU.mult,
                    [("TT", ti % 2), ("pcv", cbk)], [(cvk, t)])

        def dt_block(u, ukey, DTR, DTV, AV, dk, A3, ATMP):
            for c in range(4):
                for kt in range(8):
                    mm(pbf(2, 20 + 12 * c, 20 + 12 * (c + 1)), u[:, kt, 2 + 128 * c:2 + 128 * (c + 1)],
                       Wg[:, kt, COL_DT:COL_DT + 12], kt == 0, kt == 7, [ukey, ("Wg", kt)], [("pdt", c)])
            tt("vector", DTR, pbf(2, 20, 68).rearrange("p (a b) -> p a b", a=4),
               dtb_bc.unsqueeze(1).broadcast_to([128, 4, 12]), ALU.add, [("pdt", c) for c in range(4)] + ["dtb"], [dk + "r"])
            act("scalar", DTR, DTR, AF.Exp, [dk + "r"], [dk + "r"])
            act("scalar", DTV, DTR, AF.Ln, [dk + "r"], [dk + "v"], bias=1.0)
            tt("vector", AV, DTV, A_bc.unsqueeze(1).broadcast_to([128, 4, 12]), ALU.mult, [dk + "v", "A_bc"], [dk + "a"])
            split3("gpsimd", AV.rearrange("p a b -> p (a b)"), [A3[:, i, :] for i in range(3)], ATMP, [dk + "a"], dk + "a3")

        def to_token_major(CV, cvk, c, XSB, xk):
            for t in range(4):
                tr(pbb(2, 512 + 128 * t, 512 + 128 * (t + 1)), CV[:, t, 128 * c:128 * (c + 1)], identb, [(cvk, t), "identb"], ["ptm"])
            cp("vector", XSB, pbb(2, 512, 1024), ["ptm"], [xk])

        def passA(seq, L):
            NSC = L // 512
            al = Alloc(PBASE)
            two = lambda shape, dt: [al.get(shape, dt) for _ in range(2)]
            uts = two([128, 8, 516], BF16)
            WIN2 = two([128, 5, 516], BF16)
            ACC = two([128, 512], F32)
            ACCD = two([128, 512], F32)
            CV2 = two([128, 5, 512], BF16)
            XSB = two([128, 512], BF16)
            DTR2, DTV2, AV2 = two([128, 4, 12], F32), two([128, 4, 12], F32), two([128, 4, 12], F32)
            A32 = two([128, 3, 48], BF16)
            ATMP2 = two([128, 48], F32)
            E = two([128, 2, 12], F32)
            SCL = two([128, 6], F32)
            XD = two([128, 384], BF16)
            SC_LOCAL = ("WIN", "CV", "dtr", "dtv", "dta", "dta30", "dta31", "dta32", "dta3t")
            H = al.get([128, 384], F32)
            V(lambda e: e.memset(H, 0.0), [], ["H"])
            load_uT(seq, L, 0, uts[0], ("uts", 0), "uts0")
            for sc in range(NSC):
                if sc + 1 < NSC:
                    load_uT(seq, L, sc + 1, uts[(sc + 1) % 2], ("uts", (sc + 1) % 2), "uts%d" % ((sc + 1) % 2))
                u, ukey = uts[sc % 2], ("uts", sc % 2)
                q_ = sc % 2
                WIN, CV, DTR, DTV, AV, A3, ATMP = WIN2[q_], CV2[q_], DTR2[q_], DTV2[q_], AV2[q_], A32[q_], ATMP2[q_]
                S.kmap.update({n_: q_ for n_ in SC_LOCAL})
                ssd_front(u, ukey, WIN, ACC, CV, "CV", sc, NSC, [0, 1, 2, 3, 4], (6, 7), ACCD, WIN2[1 - q_] if sc > 0 else None, q_)
                dt_block(u, ukey, DTR, DTV, AV, "dt", A3, ATMP)
                DMA(BCTs[:, :, 512 * sc:512 * (sc + 1)].rearrange("t k l -> k t l"), CV[:, 3:5, :], [("CV", 3), ("CV", 4)],
                    [("BCTs", sc)], "bct%d" % q_)
                DMA(DTPs[sc, :, 0:48], DTV.rearrange("p a b -> p (a b)"), ["dtv"], [("DTPs", sc, 0)], "dtpa%d" % q_)
                DMA(DTPs[sc, :, 48:96], AV.rearrange("p a b -> p (a b)"), ["dta"], [("DTPs", sc, 1)], "dtpb%d" % q_)
                for c in range(4):
                    j = 4 * sc + c
                    b = j % 2
                    to_token_major(CV, "CV", c, XSB[b], ("XSB", b))
                    DMA(XSBs[128 * j:128 * (j + 1), :], XSB[b], [("XSB", b)], [("XSBs", j)], "xsbw%d" % b)
                    for mi, m_ in enumerate((bGT, bONE)):
                        for i3 in range(3):
                            mm(pbf(2, 68 + 12 * mi, 80 + 12 * mi), m_, A3[:, i3, 12 * c:12 * c + 12], i3 == 0, i3 == 2,
                               ["masksb", "dta30", "dta31", "dta32"], [("pcs", mi)])
                    act("scalar", E[b], pbf(2, 68, 92).rearrange("p (a b) -> p a b", a=2), AF.Exp,
                        [("pcs", 0), ("pcs", 1)], [("E", b)])
                    tt("vector", SCL[b], E[b][:, 0, 0:6], DTV[:, c, 0:6], ALU.mult, [("E", b), "dtv"], [("SCL", b)])
                    tt("gpsimd", XD[b].rearrange("p (h d) -> p h d", h=6), XSB[b][:, 0:384].rearrange("p (h d) -> p h d", h=6),
                       SCL[b].unsqueeze(2).broadcast_to([128, 6, 64]), ALU.mult, [("XSB", b), ("SCL", b)], [("XD", b)])
                    mm(pbf(4 + b, 0, 384), XSB[b][:, 384:512], XD[b], True, True, [("XSB", b), ("XD", b)], [("pst", b)])
                    if j == (L // 128) // 2:
                        ts("vector", H, H, flag_col[:, 0:1], None, ALU.mult, None, ["H", "flag"], ["H"])
                    cp("scalar", Hst[:, j, :], H, ["H"], [("Hst", j)])
                    tt("vector", H.rearrange("p (h d) -> p h d", h=6), H.rearrange("p (h d) -> p h d", h=6),
                       E[b][:, 1, 0:6].unsqueeze(2).broadcast_to([128, 6, 64]), ALU.mult, ["H", ("E", b)], ["H"])
                    tt("vector", H, H, pbf(4 + b, 0, 384), ALU.add, ["H", ("pst", b)], ["H"])
            S.kmap.clear()
            S.barrier()

        def passB(seq, L, g):
            NSC = L // 512
            al = Alloc(PBASE)
            two = lambda shape, dt: [al.get(shape, dt) for _ in range(2)]
            uts = two([128, 8, 516], BF16)
            WIN2 = two([128, 5, 516], BF16)
            ACC = two([128, 512], F32)
            CV2 = two([128, 5, 512], BF16)
            XSB = two([128, 512], BF16)
            SZ = two([128, 512], F32)
            TZ = two([128, 512], F32)
            DTR2, DTV2, AV2 = two([128, 4, 12], F32), two([128, 4, 12], F32), two([128, 4, 12], F32)
            A32 = two([128, 3, 48], BF16)
            ATMP2 = two([128, 48], F32)
            E = two([128, 4, 12], F32)
            SCLB2 = two([128, 6], F32)
            RR2 = [two([128, 2, 6, 128], BF16) for _ in range(2)]
            CBM2 = [two([128, 128], F32) for _ in range(2)]
            DEC = two([128, 256], F32)
            MT2 = two([128, 2, 6, 128], BF16)
            XDT2 = [two([128, 384], BF16) for _ in range(2)]
            XSD2 = two([128, 384], BF16)
            XDB2 = two([128, 384], BF16)
            Gs = al.get([128, 384], F32)
            Gb = al.get([128, 384], BF16)
            T12, T22, YS2 = two([128, 384], F32), two([128, 384], F32), two([128, 384], F32)
            junk2 = two([128, 384], BF16)
            YFR = two([128, 128], F32)
            YG8 = [al.get([128, 384], F32) for _ in range(8)]
            YC8 = [al.get([128, 512], BF16) for _ in range(8)]
            ssq8 = al.get([128, 8], F32)
            rs8 = al.get([128, 8], F32)
            SC_LOCAL = ("WIN", "CV", "dtr", "dtv", "dta", "dta30", "dta31", "dta32", "dta3t")
            CH_LOCAL = ("CBM", "MT", "XDT", "XSD", "XDB", "SCLB", "RR", "T1", "T2a", "T2b", "YS", "junkb")
            V(lambda e: e.memset(Gs, 0.0), [], ["Gs"])
            V(lambda e: e.memset(Gb, 0.0), [], ["Gb"])
            order = list(range(NSC - 1, -1, -1))
            load_uT(seq, L, order[0], uts[0], ("uts", 0), "uts0")
            for oi, sc in enumerate(order):
                if oi + 1 < NSC:
                    load_uT(seq, L, order[oi + 1], uts[(oi + 1) % 2], ("uts", (oi + 1) % 2), "uts%d" % ((oi + 1) % 2))
                u, ukey = uts[oi % 2], ("uts", oi % 2)
                q_ = oi % 2
                WIN, CV, DTR, DTV, AV, A3, ATMP = WIN2[q_], CV2[q_], DTR2[q_], DTV2[q_], AV2[q_], A32[q_], ATMP2[q_]
                S.kmap.update({n_: q_ for n_ in SC_LOCAL})
                BC = CV[:, 3:5, :]
                DMA(BC, BCTs[:, :, 512 * sc:512 * (sc + 1)].rearrange("t k l -> k t l"), [], [("CV", 3), ("CV", 4)], "bcr%d" % q_)
                DMA(DTV.rearrange("p a b -> p (a b)"), DTPs[sc, :, 0:48], [], ["dtv"], "dtra%d" % q_)
                DMA(AV.rearrange("p a b -> p (a b)"), DTPs[sc, :, 48:96], [], ["dta"], "dtrb%d" % q_)
                split3("gpsimd", AV.rearrange("p a b -> p (a b)"), [A3[:, i, :] for i in range(3)], ATMP, ["dta"], "dta3")
                for c in range(3, -1, -1):
                    j = 4 * sc + c
                    b = j % 2
                    S.kmap.update({n_: b for n_ in CH_LOCAL})
                    RR, CBM, MT, XDT, XSD, XDB, SCLB = RR2[b], CBM2[b], MT2[b], XDT2[b], XSD2[b], XDB2[b], SCLB2[b]
                    T1, T2_, YS, junk = T12[b], T22[b], YS2[b], junk2[b]
                    DMA(YFR[b], YF[128 * j:128 * (j + 1), :], [], [("YFR", b)], "yfr%d" % b)
                    for kt in range(8):
                        mm(pbf(b, 0, 512), u[:, kt, 2 + 128 * c:2 + 128 * (c + 1)], Wg[:, kt, COL_Z:COL_Z + 512],
                           kt == 0, kt == 7, [ukey, ("Wg", kt)], [("pfm", b)])
                    act("scalar", TZ[b], pbf(b, 0, 512), AF.Tanh, [("pfm", b)], [("TZ", b)])
                    stt("vector", SZ[b], TZ[b], 1.0, pbf(b, 0, 512), ALU.add, ALU.mult, [("TZ", b), ("pfm", b)], [("SZ", b)])
                    DMA(XSB[b], XSBs[128 * j:128 * (j + 1), :], [], [("XSB", b)], "xsbr%d" % b)
                    for mi, m_ in enumerate((bLE, bGE, bLT, bONE)):
                        for i3 in range(3):
                            mm(pbf(2, 68 + 12 * mi, 80 + 12 * mi), m_, A3[:, i3, 12 * c:12 * c + 12], i3 == 0, i3 == 2,
                               ["masksb", "dta30", "dta31", "dta32"], [("pcs", mi)])
                    act("scalar", E[b], pbf(2, 68, 116).rearrange("p (a b) -> p a b", a=4), AF.Exp,
                        [("pcs", mi) for mi in range(4)], [("E", b)])
                    ef, eb_, dstb, cdb = E[b][:, 0, 0:6], E[b][:, 1, 6:12], E[b][:, 2, 6:12], E[b][:, 3, 6:12]
                    for d_ in range(2):
                        msk = bLE if d_ == 0 else bGE
                        for hl in range(2):
                            tt("gpsimd", RR[d_][:, hl, :, :], msk.unsqueeze(1).broadcast_to([128, 6, 128]),
                               A3[:, hl, 12 * c + 6 * d_:12 * c + 6 * d_ + 6].unsqueeze(2).broadcast_to([128, 6, 128]), ALU.mult,
                               ["masks", "dta30", "dta31"], [("RR", d_, hl)])
                    mm(pbf(2, 116, 244), CV[:, 3, 128 * c:128 * (c + 1)], CV[:, 4, 128 * c:128 * (c + 1)], True, True,
                       [("CV", 3), ("CV", 4)], ["pcb"])
                    tt("vector", CBM[0], pbf(2, 116, 244), bLE, ALU.mult, ["pcb", "masks"], [("CBM", 0)])
                    tt("vector", CBM[1], pbf(2, 116, 244), bGE, ALU.mult, ["pcb", "masks"], [("CBM", 1)])
                    it = 0
                    for d_ in range(2):
                        lt_ = gtb if d_ == 0 else ltb
                        for hp in range(3):
                            slot = it % 2
                            o = pbf(6 + slot, 0, 256)
                            for hl in range(2):
                                mm(o, lt_, RR[d_][:, hl, 2 * hp:2 * hp + 2, :].rearrange("p a b -> p (a b)"),
                                   hl == 0, hl == 1, ["gtb", "ltb", ("RR", d_, hl)], [("pseg", slot)])
                            act("scalar", DEC[slot], o, AF.Exp, [("pseg", slot)], [("DEC", slot)])
                            for hh in range(2):
                                h_ = 2 * hp + hh
                                stt("vector", MT[:, d_, h_, :], DEC[slot][:, 128 * hh:128 * (hh + 1)],
                                    DTV[:, c, 6 * d_ + h_:6 * d_ + h_ + 1], CBM[d_], ALU.mult, ALU.mult,
                                    [("DEC", slot), ("CBM", d_), "dtv"], [("MT", d_, hp)])
                            it += 1
                    for h in range(6):
                        ybk = 4
                        yo = pbf(ybk, 64 * h, 64 * (h + 1))
                        xh = XSB[b][:, 64 * h:64 * (h + 1)]
                        mm(yo, MT[:, 0, h, :], xh, True, False, [("MT", 0, h // 2), ("XSB", b)], [("py", b, h)])
                        mm(yo, MT[:, 1, h, :], xh, False, False, [("MT", 1, h // 2), ("XSB", b)], [("py", b, h)])
                        mm(yo, DgD[:, h, :], xh, False, True, [("DgD", h), ("XSB", b)], [("py", b, h)])
                    mm(pbf(5, 0, 384), CV[:, 4, 128 * c:128 * (c + 1)], Hst[:, j, :], True, True, [("CV", 4), ("Hst", j)], ["pzf"])
                    for h in range(6):
                        act("scalar", T1[:, 64 * h:64 * (h + 1)], pbf(5, 64 * h, 64 * (h + 1)), AF.Copy, ["pzf", ("E", b)], [("T1", h)],
                            scale=E[b][:, 0, h:h + 1])
                    mm(pbf(5, 0, 384), CV[:, 4, 128 * c:128 * (c + 1)], Gb, True, True, [("CV", 4), "Gb"], ["pzf"])
                    for h in range(6):
                        stt("vector", T1[:, 64 * h:64 * (h + 1)], pbf(5, 64 * h, 64 * (h + 1)), E[b][:, 1, 6 + h:7 + h],
                            T1[:, 64 * h:64 * (h + 1)], ALU.mult, ALU.add, ["pzf", ("E", b), ("T1", h)], [("T1", h)])
                    tt("vector", YS, pbf(4, 0, 384), T1, ALU.add, [("py", b, h) for h in range(6)] + [("T1", h) for h in range(6)], ["YS"])
                    tt("vector", SCLB, dstb, DTV[:, c, 6:12], ALU.mult, [("E", b), "dtv"], ["SCLB"])
                    tt("gpsimd", XDB.rearrange("p (h d) -> p h d", h=6), XSB[b][:, 0:384].rearrange("p (h d) -> p h d", h=6),
                       SCLB.unsqueeze(2).broadcast_to([128, 6, 64]), ALU.mult, [("XSB", b), "SCLB"], ["XDB"])
                    mm(pbf(3, 0, 384), XSB[b][:, 384:512], XDB, True, True, [("XSB", b), "XDB"], ["pst3"])
                    tt("vector", Gs.rearrange("p (h d) -> p h d", h=6), Gs.rearrange("p (h d) -> p h d", h=6),
                       cdb.unsqueeze(2).broadcast_to([128, 6, 64]), ALU.mult, ["Gs", ("E", b)], ["Gs"])
                    tt("vector", Gs, Gs, pbf(3, 0, 384), ALU.add, ["Gs", "pst3"], ["Gs"])
                    if j == (L // 128) // 2:
                        ts("vector", Gs, Gs, flag_col[:, 0:1], None, ALU.mult, None, ["Gs", "flag"], ["Gs"])
                    cp("scalar", Gb, Gs, ["Gs"], ["Gb"])
                    sl8 = 4 * q_ + c
                    tt("gpsimd", YG8[sl8], YS, SZ[b][:, 128:512], ALU.mult, ["YS", ("SZ", b)], [("YG8", sl8)])
                    act("scalar", junk, YG8[sl8], AF.Square, [("YG8", sl8)], ["junkb", ("ssq8", sl8)], accum_out=ssq8[:, sl8:sl8 + 1])
                    tt("gpsimd", YC8[sl8][:, 0:128], YFR[b], SZ[b][:, 0:128], ALU.mult, [("YFR", b), ("SZ", b)], [("YC8", sl8, 0)])
                    DMA(YCAT[128 * j:128 * (j + 1), 128 * g:128 * (g + 1)], YC8[sl8][:, 0:128], [("YC8", sl8, 0)], [("YCAT", j, 0)],
                        "yca%d" % sl8)
                rs4 = rs8[:, 4 * q_:4 * q_ + 4]
                ts("vector", rs4, ssq8[:, 4 * q_:4 * q_ + 4], 1.0 / 384, EPS, ALU.mult, ALU.add,
                   [("ssq8", 4 * q_ + c_) for c_ in range(4)], [("rs8", q_)])
                V(lambda e, rs4=rs4: e.reciprocal(out=rs4, in_=rs4), [("rs8", q_)], [("rs8", q_)])
                act("scalar", rs4, rs4, AF.Sqrt, [("rs8", q_)], [("rs8", q_)])
                for c in range(4):
                    j = 4 * sc + c
                    sl8 = 4 * q_ + c
                    stt("vector", YC8[sl8][:, 128:512], YG8[sl8], rs8[:, sl8:sl8 + 1], snw_bc, ALU.mult, ALU.mult,
                        [("YG8", sl8), ("rs8", q_), "snw"], [("YC8", sl8, 1)])
                    DMA(YCAT[128 * j:128 * (j + 1), 512 + 384 * g:512 + 384 * (g + 1)], YC8[sl8][:, 128:512], [("YC8", sl8, 1)],
                        [("YCAT", j, 1)], "ycb%d" % sl8)
            S.kmap.clear()
            S.barrier()

        def load_tail_weights():
            al = Alloc(HBASE)
            wo = al.get([128, 16, D], BF16)
            wg_ = al.get([128, 8, D], BF16)
            wp = al.get([128, 2, D], BF16)
            fnw = al.get([128, D], F32)
            wst = [al.get([128, D], F32) for _ in range(2)]
            i = 0
            for dst, src, n in ((wo, wout_d, 16), (wg_, wpg_d, 8), (wp, wpi_d, 2)):
                for kt in range(n):
                    b = i % 2
                    DMA(wst[b], src[128 * kt:128 * (kt + 1), :], [], [("wst", b)], "wst%d" % b)
                    if dst is wp:
                        ts("vector" if i % 2 == 0 else "gpsimd", dst[:, kt, :], wst[b], 0.5, None, ALU.mult, None, [("wst", b)], [("tw", i)])
                    else:
                        cp("vector" if i % 2 == 0 else "gpsimd", dst[:, kt, :], wst[b], [("wst", b)], [("tw", i)])
                    i += 1
            DMA(fnw, fnw_d.partition_broadcast(128), [], ["fnw"], "c0")
            S.barrier()
            return wo, wg_, wp, fnw, al

        def tail(seq, L, wo, wg_, wp, fnw, al):
            xt = [al.get([128, D], F32) for _ in range(2)]
            yc = [al.get([128, 2048], BF16) for _ in range(2)]
            pt = [al.get([128, 256], F32) for _ in range(2)]
            two = lambda shape, dt: [al.get(shape, dt) for _ in range(2)]
            ptb2, pT2, ycT2 = two([128, 256], BF16), two([128, 2, 128], BF16), two([128, 16, 128], BF16)
            h12, h1b2, h1T2 = two([128, D], F32), two([128, D], BF16), two([128, 8, 128], BF16)
            gate2, junk2 = two([128, D], F32), two([128, D], BF16)
            tq4 = [al.get([128, D], F32) for _ in range(4)]
            ssq4 = al.get([128, 4], F32)
            rs4 = al.get([128, 4], F32)
            T_LOCAL = ("ptb", "pT", "ycT", "h1", "h1b", "h1T", "gate", "junkb")
            ot = [al.get([128, D], F32) for _ in range(2)]
            for j in range(L // 128):
                b = j % 2
                S.kmap.update({n_: b for n_ in T_LOCAL})
                ptb, pT, ycT, h1, h1b, h1T = ptb2[b], pT2[b], ycT2[b], h12[b], h1b2[b], h1T2[b]
                gate, junk = gate2[b], junk2[b]
                s4 = j % 4
                tq = tq4[s4]
                DMA(xt[b], x_d[seq][128 * j:128 * (j + 1), :], [], [("xt", b)], "xt%d" % b)
                DMA(yc[b], YCAT[128 * j:128 * (j + 1), :], [], [("yc", b)], "ycl%d" % b)
                DMA(pt[b], pl_d[seq][128 * j:128 * (j + 1), :], [], [("pt", b)], "pt%d" % b)
                for half in range(2):
                    for i in range(8):
                        kt = 8 * half + i
                        tr(pbb(half, 128 * i, 128 * (i + 1)), yc[b][:, 128 * kt:128 * (kt + 1)], identb,
                           [("yc", b), "identb"], [("ptr", half)])
                    cp("vector" if half == 0 else "scalar", ycT[:, 8 * half:8 * half + 8, :],
                       pbb(half, 0, 1024).rearrange("p (a b) -> p a b", a=8), [("ptr", half)], [("ycT", half)])
                for nh in range(2):
                    for kt in range(16):
                        mm(pbf(2 + nh, 0, 512), ycT[:, kt, :], wo[:, kt, 512 * nh:512 * (nh + 1)], kt == 0, kt == 15,
                           [("ycT", kt // 8)], [("ph", nh)])
                    tt("vector", h1[:, 512 * nh:512 * (nh + 1)], pbf(2 + nh, 0, 512), xt[b][:, 512 * nh:512 * (nh + 1)], ALU.add,
                       [("ph", nh), ("xt", b)], [("h1", nh)])
                cp("scalar", h1b, h1, [("h1", 0), ("h1", 1)], ["h1b"])
                for kt in range(8):
                    tr(pbb(4, 128 * kt, 128 * (kt + 1)), h1b[:, 128 * kt:128 * (kt + 1)], identb, ["h1b", "identb"], ["ph1T"])
                cp("vector", h1T, pbb(4, 0, 1024).rearrange("p (a b) -> p a b", a=8), ["ph1T"], ["h1T"])
                cp("gpsimd", ptb, pt[b], [("pt", b)], ["ptb"])
                for kt in range(2):
                    tr(pbb(5, 128 * kt, 128 * (kt + 1)), ptb[:, 128 * kt:128 * (kt + 1)], identb, ["ptb", "identb"], ["ppT"])
                cp("vector", pT, pbb(5, 0, 256).rearrange("p (a b) -> p a b", a=2), ["ppT"], ["pT"])
                for nh in range(2):
                    for kt in range(8):
                        mm(pbf(6 + nh, 0, 512), h1T[:, kt, :], wg_[:, kt, 512 * nh:512 * (nh + 1)], kt == 0, kt == 7,
                           ["h1T"], [("pg", nh)])
                    act("scalar", gate[:, 512 * nh:512 * (nh + 1)], pbf(6 + nh, 0, 512), AF.Tanh, [("pg", nh)], [("gate", nh)], scale=0.5)
                    for kt in range(2):
                        mm(pbf(2 + nh, 0, 512), pT[:, kt, :], wp[:, kt, 512 * nh:512 * (nh + 1)], kt == 0, kt == 1,
                           ["pT"], [("ph", nh)])
                    stt("vector", tq[:, 512 * nh:512 * (nh + 1)], gate[:, 512 * nh:512 * (nh + 1)], 1.0, pbf(2 + nh, 0, 512),
                        ALU.add, ALU.mult, [("ph", nh), ("gate", nh)], [("tq4", s4, nh)])
                    tt("vector", tq[:, 512 * nh:512 * (nh + 1)], tq[:, 512 * nh:512 * (nh + 1)], h1[:, 512 * nh:512 * (nh + 1)],
                       ALU.add, [("tq4", s4, nh), ("h1", nh)], [("tq4", s4, nh)])
                act("scalar", junk, tq, AF.Square, [("tq4", s4, 0), ("tq4", s4, 1)], ["junkb", ("ssq4", s4)], accum_out=ssq4[:, s4:s4 + 1])
                if j % 2 == 1:
                    pr = (s4 // 2) * 2
                    rsp = rs4[:, pr:pr + 2]
                    ts("vector", rsp, ssq4[:, pr:pr + 2], 1.0 / D, EPS, ALU.mult, ALU.add, [("ssq4", pr), ("ssq4", pr + 1)], [("rs4", pr)])
                    V(lambda e, rsp=rsp: e.reciprocal(out=rsp, in_=rsp), [("rs4", pr)], [("rs4", pr)])
                    act("scalar", rsp, rsp, AF.Sqrt, [("rs4", pr)], [("rs4", pr)])
                    for jj in (j - 1, j):
                        sj = jj % 4
                        bb = jj % 2
                        stt("vector", ot[bb], tq4[sj], rs4[:, sj:sj + 1], fnw, ALU.mult, ALU.mult,
                            [("tq4", sj, 0), ("tq4", sj, 1), ("rs4", pr), "fnw"], [("ot", bb)])
                        DMA(y_d[seq][128 * jj:128 * (jj + 1), :], ot[bb], [("ot", bb)], [], "ot%d" % bb)
            S.kmap.clear()
            S.barrier()

        import os
        dbg = os.environ.get("KDEBUG", "")
        for seq, L in seqs:
            pass0(seq, L)
            S.barrier()
            for g in range(NG):
                if dbg and g > 0 and "4" not in dbg:
                    continue
                if dbg and "g" not in dbg:
                    continue
                load_group(g, L)
                if not dbg or "F" in dbg:
                    passF(seq, L)
                if not dbg or "A" in dbg:
                    passA(seq, L)
                if not dbg or "B" in dbg:
                    passB(seq, L, g)
            if not dbg or "T" in dbg:
                tw = load_tail_weights()
                tail(seq, L, *tw)
        S.emit(eng_sems, block)
    return nc


_CACHE = {}


def _prep_weights(w_in, conv_w, conv_b, a_log_f, a_log_b, dt_bias_f, dt_bias_b, d_skip, ssd_norm_w):
    w = w_in[0]
    cols = []
    for g in range(NG):
        idx = np.concatenate([
            np.arange(128 * g, 128 * g + 128),
            1024 + np.arange(384 * g, 384 * g + 384),
            512 + np.arange(128 * g, 128 * g + 128),
            2560 + np.arange(384 * g, 384 * g + 384),
            2560 + 1536 + np.arange(128 * g, 128 * g + 128),
            2560 + 2048 + np.arange(128 * g, 128 * g + 128),
            5120 + np.arange(6 * g, 6 * g + 6),
            5144 + np.arange(6 * g, 6 * g + 6),
        ])
        cols.append(idx)
    w_in_g = np.ascontiguousarray(np.stack([w[:, c] for c in cols], 0))
    cw, cbias = conv_w[0], conv_b[0]
    convw_g = np.zeros((NG, 128, 25), np.float32)
    convb_g = np.zeros((NG, 128, 5), np.float32)
    for g in range(NG):
        ch = np.concatenate([np.arange(384 * g, 384 * g + 384), 1536 + np.arange(128 * g, 128 * g + 128),
                             2048 + np.arange(128 * g, 128 * g + 128)])
        for t in range(5):
            cht = ch[128 * t:128 * (t + 1)]
            convw_g[g, :, 5 * t:5 * t + 5] = cw[:, cht].T
            convb_g[g, :, t] = cbias[cht]
    cbrow_g = np.ascontiguousarray(convb_g.transpose(0, 2, 1).reshape(NG, 640))
    sl = lambda v, g: v[0][6 * g:6 * g + 6]
    dtb_g = np.stack([np.concatenate([sl(dt_bias_f, g), sl(dt_bias_b, g)]) for g in range(NG)], 0)
    alog_g = np.stack([np.concatenate([sl(a_log_f, g), sl(a_log_b, g)]) for g in range(NG)], 0)
    dsk_g = np.stack([sl(d_skip, g) for g in range(NG)], 0)
    snw_g = np.ascontiguousarray(ssd_norm_w[0].reshape(NG, 384))
    return dict(w_in_g=w_in_g, convw_g=convw_g, convb_g=convb_g, cbrow_g=cbrow_g, dtb_g=np.ascontiguousarray(dtb_g, np.float32),
                alog_g=np.ascontiguousarray(alog_g, np.float32), dsk_g=np.ascontiguousarray(dsk_g, np.float32), snw_g=snw_g)


def run(x_prompt, x_sample, p_prompt, p_sample, norm_w, w_in, w_fmix, conv_w, conv_b, a_log_f, a_log_b,
        dt_bias_f, dt_bias_b, d_skip, ssd_norm_w, w_out, w_ple_in, w_ple_gate, final_norm_w):
    f = lambda a: np.ascontiguousarray(np.asarray(a, dtype=np.float32))
    x_prompt, x_sample, p_prompt, p_sample = f(x_prompt), f(x_sample), f(p_prompt), f(p_sample)
    Bp, Lp, _ = x_prompt.shape
    Bs, Ls, _ = x_sample.shape
    assert Ls == 2 * Lp
    if Ls not in _CACHE:
        _CACHE[Ls] = build(Ls)
    nc = _CACHE[Ls]
    common = _prep_weights(f(w_in), f(conv_w), f(conv_b), f(a_log_f), f(a_log_b), f(dt_bias_f), f(dt_bias_b),
                           f(d_skip), f(ssd_norm_w))
    masks, c64 = _masks()
    cs1, t21 = _consts(Ls, False)
    cs2, t22 = _consts(Ls, True)
    common.update(wfm=f(w_fmix)[0], w_out=f(w_out)[0], w_ple_in=f(w_ple_in)[0], w_ple_gate=f(w_ple_gate)[0],
                  fnw=f(final_norm_w), normw=f(norm_w)[0], masks=masks, c64=c64)
    import os
    n_dual = int(os.environ.get("KNDUAL", 8 - Bs))
    slots = []
    for c in range(Bs):
        slots.append(("s", c, None))
    pr = list(range(Bp))
    pairs = [[] for _ in range(n_dual)]
    for i, b in enumerate(pr):
        pairs[i % n_dual].append(b)
    assert all(len(p_) <= 2 for p_ in pairs), "prompt batch does not fit the slots"
    for p_ in pairs:
        a = p_[0] if len(p_) > 0 else 0
        b = p_[1] if len(p_) > 1 else a
        slots.append(("p", a, b if len(p_) > 1 else None, b))
    in_maps = []
    for sl in slots:
        m = dict(common)
        if sl[0] == "s":
            m["x_s"] = x_sample[sl[1]]
            m["pl_s"] = p_sample[0, sl[1]]
            m["cs_s"], m["t2_s"] = cs1, t21
            m["flag"] = np.ones((1,), np.float32)
        else:
            a, b = sl[1], sl[3]
            m["x_s"] = np.ascontiguousarray(np.concatenate([x_prompt[a], x_prompt[b]], 0))
            m["pl_s"] = np.ascontiguousarray(np.concatenate([p_prompt[0, a], p_prompt[0, b]], 0))
            m["cs_s"], m["t2_s"] = cs2, t22
            m["flag"] = np.zeros((1,), np.float32)
        in_maps.append(m)
    while len(in_maps) < 8:
        in_maps.append(in_maps[-1])
    res = run_bass_kernel_spmd(nc, in_maps, core_ids=list(range(8)))
    ys = np.stack([res.results[c]["y_s"] for c in range(Bs)], 0)
    yp = np.zeros((Bp, Lp, D), np.float32)
    for c, sl in enumerate(slots):
        if sl[0] == "p":
            o = res.results[c]["y_s"]
            yp[sl[1]] = o[:Lp]
            if sl[2] is not None:
                yp[sl[2]] = o[Lp:]
    return yp, ys.astype(np.float32)


def kernel(**inputs):
    return run(**inputs)
```

```python
import math
from contextlib import ExitStack
import numpy as np
import ml_dtypes
import concourse.bass as bass
import concourse.mybir as mybir
from concourse.bass_utils import run_bass_kernel_spmd

F32 = mybir.dt.float32
BF16 = mybir.dt.bfloat16
ALU = mybir.AluOpType
AF = mybir.ActivationFunctionType

D = 1024
NG = 4
GC = 1292
COL_Z, COL_UF, COL_XS, COL_B, COL_C, COL_DT = 0, 512, 640, 1024, 1152, 1280
EPS = 1e-6
SAME_ENGINE_RAW_SYNC = True
LIST_SCHEDULE = True
PRIO_CRITICAL = True
FILL_MIN_GAP = 700.0
FILL_SLACK = 250.0
FILL_COST = 240.0


class Op:
    __slots__ = ("eng", "fn", "deps", "odeps", "inc", "mile", "sem", "is_dma", "idx", "cost", "users", "npend", "ready", "fin", "blev")

    def __init__(self, eng, fn, is_dma=False, sem=None, cost=100.0):
        self.eng, self.fn, self.is_dma, self.sem, self.cost = eng, fn, is_dma, sem, cost
        self.deps = set()
        self.odeps = set()
        self.inc = False
        self.mile = None


class Sched:
    ENGS = ("sync", "scalar", "vector", "gpsimd", "tensor")

    def __init__(self, nc):
        self.nc = nc
        self.final = []
        self.seg = []
        self.last_w = {}
        self.readers = {}
        self.last_on = {}
        self.dmas_since = []
        self.excl_last = {}
        self.nops = 0
        self.kmap = {}
        self.filler = None
        self.nfill = 0

    def _xk(self, k):
        n = k if isinstance(k, str) else k[0]
        sfx = self.kmap.get(n)
        return k if sfx is None else (k, sfx)

    def _edge(self, op, d, raw):
        if d is op:
            return
        if d.is_dma or op.is_dma or d.eng != op.eng:
            op.deps.add(d)
        elif d.eng != "tensor" and SAME_ENGINE_RAW_SYNC:
            op.deps.add(d)
        else:
            op.odeps.add(d)

    def add(self, eng, fn, reads=(), writes=(), dma_sem=None, extra_deps=(), excl=(), cost=100.0):
        op = Op(eng, fn, dma_sem is not None, dma_sem, cost)
        op.idx = self.nops
        self.nops += 1
        if self.kmap:
            reads = [self._xk(k) for k in reads]
            writes = [self._xk(k) for k in writes]
        for k in excl:
            d = self.excl_last.setdefault(k, {})
            for e2, o2 in d.items():
                self._edge(op, o2, False)
            d[eng] = op
        for b in reads:
            w = self.last_w.get(b)
            if w is not None:
                self._edge(op, w, True)
        for b in writes:
            w = self.last_w.get(b)
            if w is not None:
                self._edge(op, w, False)
            for r in self.readers.get(b, ()):
                self._edge(op, r, False)
        for d in extra_deps:
            op.deps.add(d)
        for b in reads:
            self.readers.setdefault(b, []).append(op)
        for b in writes:
            self.last_w[b] = op
            self.readers[b] = []
        self.seg.append(op)
        if op.is_dma:
            self.dmas_since.append(op)
        else:
            self.last_on[eng] = op
        return op

    def _schedule_segment(self):
        import heapq
        seg = self.seg
        self.seg = []
        if not LIST_SCHEDULE:
            self.final.extend(seg)
            return
        inseg = set(id(o) for o in seg)
        for o in seg:
            o.users = []
            o.npend = 0
            o.ready = 0.0
        for o in seg:
            for d in list(o.deps) + list(o.odeps):
                if id(d) in inseg:
                    d.users.append(o)
                    o.npend += 1
        for o in reversed(seg):
            bl = 0.0
            for u in o.users:
                if u.blev > bl:
                    bl = u.blev
            o.blev = bl + o.cost
        if PRIO_CRITICAL:
            for o in seg:
                o.idx = (-o.blev, o.idx)
        future = {e: [] for e in self.ENGS}
        avail = {e: [] for e in self.ENGS}
        free = {e: 0.0 for e in self.ENGS}
        for o in seg:
            if o.npend == 0:
                heapq.heappush(future[o.eng], (0.0, o.idx, o))
        out = []
        n = len(seg)
        while len(out) < n:
            best = None
            for e in self.ENGS:
                fq, aq = future[e], avail[e]
                t0 = free[e]
                while fq and fq[0][0] <= t0:
                    r_, i_, o_ = heapq.heappop(fq)
                    heapq.heappush(aq, (i_, r_, o_))
                if aq:
                    st, key = t0, aq[0][0]
                elif fq:
                    st, key = fq[0][0], fq[0][1]
                else:
                    continue
                if best is None or (st, key) < (best[0], best[1]):
                    best = (st, key, e)
            st, key, e = best
            if avail[e]:
                o = heapq.heappop(avail[e])[2]
            else:
                o = heapq.heappop(future[e])[2]
            if e == "tensor" and self.filler is not None and free[e] > 0.0:
                gap = st - free[e]
                if gap > FILL_MIN_GAP:
                    for _ in range(min(int((gap - FILL_SLACK) / FILL_COST), 24)):
                        fo = Op("tensor", self.filler, False, None, FILL_COST)
                        fo.idx = -1
                        out.append(fo)
                        n += 1
                        self.nfill += 1
            if o.is_dma:
                free[e] = st + 60.0
                o.fin = st + o.cost
            else:
                free[e] = st + o.cost
                o.fin = free[e] + 40.0
            out.append(o)
            for u in o.users:
                u.npend -= 1
                if o.fin > u.ready:
                    u.ready = o.fin
                if u.npend == 0:
                    heapq.heappush(future[u.eng], (u.ready, u.idx, u))
        self.final.extend(out)

    def barrier(self):
        dmas = list(self.dmas_since)
        self.dmas_since = []
        self._schedule_segment()
        self.filler = None
        last = {}
        for o in self.final[::-1]:
            if not o.is_dma and o.eng not in last:
                last[o.eng] = o
                if len(last) == 4:
                    break
        deps = list(last.values()) + dmas
        for e in self.ENGS:
            self.add(e, lambda eng: eng.nop(), extra_deps=deps, cost=30.0)
        self._schedule_segment()
        self.last_w.clear()
        self.readers.clear()
        self.excl_last.clear()

    def emit(self, eng_sems, block):
        self._schedule_segment()
        ops = self.final
        for op in ops:
            for d in op.deps:
                if not d.is_dma:
                    d.inc = True
        cnt = {e: 0 for e in self.ENGS}
        dcnt = {}
        for op in ops:
            if op.is_dma:
                dcnt[op.sem] = dcnt.get(op.sem, 0) + 16
                op.mile = dcnt[op.sem]
            elif op.inc:
                cnt[op.eng] += 1
                op.mile = cnt[op.eng]
        by_eng = {e: [o for o in ops if o.eng == e] for e in self.ENGS}

        def run(eng_name, eng):
            waited = {}
            for op in by_eng[eng_name]:
                need = {}
                for d in op.deps:
                    s = d.sem if d.is_dma else eng_sems[d.eng]
                    if need.get(s, 0) < d.mile:
                        need[s] = d.mile
                for s, v in need.items():
                    if waited.get(s, 0) < v:
                        eng.wait_ge(s, v)
                        waited[s] = v
                ins = op.fn(eng)
                if op.is_dma:
                    ins.then_inc(op.sem, 16)
                elif op.inc:
                    ins.then_inc(eng_sems[eng_name], 1)

        @block.sync
        def _(e):
            run("sync", e)

        @block.scalar
        def _(e):
            run("scalar", e)

        @block.vector
        def _(e):
            run("vector", e)

        @block.gpsimd
        def _(e):
            run("gpsimd", e)

        @block.tensor
        def _(e):
            run("tensor", e)


def _consts(L, dual):
    C = L // 128
    Ch = C // 2
    j = np.arange(C)[:, None].astype(np.float64)
    k2 = np.arange(C)[None, :].astype(np.float64)
    if not dual:
        ang = 2 * np.pi * j * k2 / C
        cs = np.concatenate([np.cos(ang), np.sin(ang)], axis=1)
    else:
        cs = np.zeros((C, 2 * C))
        jj = np.arange(Ch)[:, None].astype(np.float64)
        rr = np.arange(Ch)[None, :].astype(np.float64)
        ang = 2 * np.pi * jj * rr / Ch
        cs[0:Ch, 0:Ch] = np.cos(ang)
        cs[0:Ch, C:C + Ch] = np.sin(ang)
        cs[Ch:C, Ch:C] = np.cos(ang)
        cs[Ch:C, C + Ch:2 * C] = np.sin(ang)
    p = np.arange(128)[:, None, None].astype(np.float64)
    kk2 = np.arange(C)[None, :, None].astype(np.float64)
    k1 = np.arange(128)[None, None, :].astype(np.float64)
    t2 = np.zeros((128, C, 2, 4, 128))
    if not dual:
        a2 = 2 * np.pi * ((p * (C * k1 + kk2)) % L) / L
        mc, ms = np.cos(a2), np.sin(a2)
        t2[:, :, 0] = np.stack([mc, ms, -ms, mc], axis=2)
    else:
        Lh = L // 2
        k1a = np.arange(128)[None, None, :]
        isA = (k1a < 64)
        freq = np.where(isA, C * k1 + kk2, C * (k1 - 64) + kk2)
        a2 = 2 * np.pi * ((p * freq) % Lh) / Lh
        mc, ms = np.cos(a2), np.sin(a2)
        full = np.stack([mc, ms, -ms, mc], axis=2) * math.sqrt(2.0)
        slotA = (np.arange(C) < Ch)[None, :, None, None]
        colA = isA[:, :, None, :] if isA.ndim == 3 else isA
        colA = np.broadcast_to((np.arange(128) < 64)[None, None, None, :], full.shape)
        own_mask = np.where(slotA, colA, ~colA)
        t2[:, :, 0] = np.where(own_mask, full, 0.0)
        t2[:, :, 1] = np.where(own_mask, 0.0, full)
    return cs.astype(ml_dtypes.bfloat16), np.ascontiguousarray(t2.reshape(128, C, 1024)).astype(ml_dtypes.bfloat16)


def _masks():
    k = np.arange(128)[:, None]
    s = np.arange(128)[None, :]
    m = np.stack([(k <= s), (k >= s), (k < s), (k > s), np.ones((128, 128), bool), np.eye(128, dtype=bool)], 0)
    c = np.arange(64)[:, None] * np.arange(64)[None, :]
    c64 = np.cos(2 * np.pi * c / 64)
    s64 = np.sin(2 * np.pi * c / 64)
    z = np.zeros((64, 64))
    cb = np.block([[c64, z], [z, c64]])
    sb = np.block([[s64, z], [z, s64]])
    out = []
    for t in (cb, sb):
        r = t.astype(np.float32)
        for _ in range(3):
            h = r.astype(ml_dtypes.bfloat16)
            out.append(h)
            r = (r - h.astype(np.float32)).astype(np.float32)
    return m.astype(ml_dtypes.bfloat16), np.stack(out, 0)


def build(Ls):
    nc = bass.Bass("TRN2", target_bir_lowering=False)
    LMAX = Ls
    seqs = [("s", Ls)]

    def din(name, shape, dt=F32):
        return nc.dram_tensor(name, shape, dt, kind="ExternalInput").ap()

    x_d = {"s": din("x_s", [Ls, D])}
    pl_d = {"s": din("pl_s", [Ls, 256])}
    y_d = {"s": nc.dram_tensor("y_s", [Ls, D], F32, kind="ExternalOutput").ap()}
    flag_d = din("flag", [1])
    win_d = din("w_in_g", [NG, D, GC])
    convw_d = din("convw_g", [NG, 128, 25])
    convb_d = din("convb_g", [NG, 128, 5])
    cbrow_d = din("cbrow_g", [NG, 640])
    dtb_d = din("dtb_g", [NG, 12])
    alog_d = din("alog_g", [NG, 12])
    dsk_d = din("dsk_g", [NG, 6])
    snw_d = din("snw_g", [NG, 384])
    wfm_d = din("wfm", [8, 64, 64])
    wout_d = din("w_out", [2048, D])
    wpi_d = din("w_ple_in", [256, D])
    wpg_d = din("w_ple_gate", [D, D])
    fnw_d = din("fnw", [D])
    normw_d = din("normw", [D])
    masks_d = din("masks", [6, 128, 128], BF16)
    c64_d = din("c64", [6, 128, 128], BF16)
    cs_d = {"s": din("cs_s", [Ls // 128, 2 * (Ls // 128)], BF16)}
    t2_d = {"s": din("t2_s", [128, Ls // 128, 1024], BF16)}
    UT = nc.dram_tensor("UT", [8, 128, LMAX], BF16).ap()
    YCAT = nc.dram_tensor("YCAT", [LMAX, 2048], BF16).ap()
    YF = nc.dram_tensor("YFs", [LMAX, 128], F32).ap()
    XSBs = nc.dram_tensor("XSBs", [LMAX, 512], BF16).ap()
    BCTs = nc.dram_tensor("BCTs", [2, 128, LMAX], BF16).ap()
    DTPs = nc.dram_tensor("DTPs", [LMAX // 512, 128, 96], F32).ap()

    es = ExitStack()
    with es:
        S = Sched(nc)
        eng_sems = {e: es.enter_context(nc.semaphore("s_" + e)) for e in Sched.ENGS}
        dsems = {}

        def dsem(name):
            if name not in dsems:
                dsems[name] = es.enter_context(nc.semaphore("d_" + name))
            return dsems[name]

        def banks(*aps):
            out = set()
            for a in aps:
                try:
                    nm = a.tensor.name
                except Exception:
                    continue
                if nm.startswith("pb"):
                    out.add(nm)
            return out

        def fsz(ap):
            n = 1
            for d_ in ap.shape[1:]:
                n *= d_
            return n

        def ecost(eng, ap, *ins):
            n = fsz(ap)
            slow = 1.0
            for a_ in ins:
                try:
                    if a_.ap[-1][0] == 0 and a_.ap[-1][1] > 1:
                        slow = 1.0
                except Exception:
                    pass
            if eng == "scalar":
                return 220.0 + n / 1.4
            if eng == "vector":
                return 70.0 + slow * n / 0.96
            return 120.0 + slow * n / 0.7

        def V(fn, r=(), w=()):
            return S.add("vector", fn, r, w, cost=100.0)

        def G(fn, r=(), w=()):
            return S.add("gpsimd", fn, r, w, cost=150.0)

        def DMA(out, in_, r, w, sem, **kw):
            nbytes = out.shape[0] * fsz(out) * (4 if out.dtype == F32 else 2)
            return S.add("sync", lambda e: e.dma_start(out=out, in_=in_, **kw), r, w, dma_sem=dsem(sem),
                         cost=2500.0 + nbytes / 60.0)

        def mm(out, lhsT, rhs, start, stop, r, w):
            bk = banks(out)
            return S.add("tensor", lambda e: e.matmul(out, lhsT=lhsT, rhs=rhs, start=start, stop=stop), r,
                         list(w) + [("accb", b_) for b_ in bk], excl=bk, cost=28.0 + fsz(rhs) * 0.45)

        def tr(out, in_, ident, r, w):
            bk = banks(out)
            return S.add("tensor", lambda e: e.transpose(out=out, in_=in_, identity=ident), r,
                         list(w) + [("accb", b_) for b_ in bk], excl=bk, cost=110.0)

        def act(eng, out, in_, func, r, w, **kw):
            return S.add(eng, lambda e: e.activation(out=out, in_=in_, func=func, **kw), r, w, excl=banks(out, in_),
                         cost=ecost(eng, out))

        def cp(eng, out, in_, r, w):
            if eng == "scalar":
                return S.add(eng, lambda e: e.activation(out=out, in_=in_, func=AF.Copy), r, w, excl=banks(out, in_),
                             cost=ecost(eng, out))
            return S.add(eng, lambda e: e.tensor_copy(out=out, in_=in_), r, w, excl=banks(out, in_), cost=ecost(eng, out))

        def tt(eng, out, in0, in1, op, r, w):
            return S.add(eng, lambda e: e.tensor_tensor(out=out, in0=in0, in1=in1, op=op), r, w, excl=banks(out, in0, in1),
                         cost=ecost(eng, out, in0, in1))

        def ts(eng, out, in0, s1, s2, op0, op1, r, w, **kw):
            if s2 is None:
                return S.add(eng, lambda e: e.tensor_scalar(out=out, in0=in0, scalar1=s1, scalar2=None, op0=op0, **kw), r, w,
                             excl=banks(out, in0), cost=ecost(eng, out))
            return S.add(eng, lambda e: e.tensor_scalar(out=out, in0=in0, scalar1=s1, scalar2=s2, op0=op0, op1=op1, **kw), r, w,
                         excl=banks(out, in0), cost=ecost(eng, out))

        def split3(eng, src, dst3, tmp, r, w):
            cp(eng, dst3[0], src, r, [w + "0"])
            tt(eng, tmp, src, dst3[0], ALU.subtract, list(r) + [w + "0"], [w + "t"])
            cp(eng, dst3[1], tmp, [w + "t"], [w + "1"])
            tt(eng, tmp, tmp, dst3[1], ALU.subtract, [w + "t", w + "1"], [w + "t"])
            cp(eng, dst3[2], tmp, [w + "t"], [w + "2"])

        def stt(eng, out, in0, scalar, in1, op0, op1, r, w):
            return S.add(eng, lambda e: e.scalar_tensor_tensor(out=out, in0=in0, scalar=scalar, in1=in1, op0=op0, op1=op1), r, w,
                         excl=banks(out, in0, in1), cost=ecost(eng, out))

        ARENA = 207 * 1024
        arena = es.enter_context(nc.sbuf_tensor("arena", [128, ARENA // 2], BF16))

        class Alloc:
            def __init__(self, base=0):
                self.off = base

            def get(self, shape, dt):
                n = 1
                for s_ in shape[1:]:
                    n *= s_
                nb = n * (4 if dt == F32 else 2)
                nb = (nb + 63) // 64 * 64
                assert self.off + nb <= ARENA, (self.off, nb)
                v = arena[:, self.off // 2:(self.off + nb) // 2]
                self.off += nb
                if dt == F32:
                    v = v.bitcast(F32)
                v = v[:, 0:n]
                if len(shape) == 3:
                    v = v.rearrange("p (a b) -> p a b", a=shape[1])
                elif len(shape) == 4:
                    v = v.rearrange("p (a b c) -> p a b c", a=shape[1], b=shape[2])
                if shape[0] < 128:
                    v = v[0:shape[0]]
                return v

        pers = Alloc(0)
        masksb = pers.get([128, 6, 128], BF16)
        bLE, bGE, bLT, bGT, bONE, identb = (masksb[:, i, :] for i in range(6))
        gtb, ltb = bGT, bLT
        c64 = pers.get([128, 6, 128], BF16)
        wf3 = pers.get([128, 3, 128], BF16)
        wftmp = pers.get([128, 128], F32)
        normw_col = pers.get([128, 8], F32)
        flag_col = pers.get([128, 1], F32)
        Wg = pers.get([128, 8, GC], BF16)
        convw = pers.get([128, 25], F32)
        convb = pers.get([128, 5], F32)
        Dg = pers.get([128, 25, 128], BF16)
        DgD = pers.get([128, 6, 128], BF16)
        convwh = pers.get([128, 25], F32)
        cbh = pers.get([128, 5], F32)
        cbrow = pers.get([1, 640], BF16)
        cbrow_f = pers.get([1, 640], F32)
        ones_row = pers.get([1, 512], BF16)
        dtb_bc = pers.get([128, 12], F32)
        A_bc = pers.get([128, 12], F32)
        dsk_bc = pers.get([128, 6], F32)
        snw_bc = pers.get([128, 384], F32)
        wfblk = pers.get([128, 128], F32)
        W1b = pers.get([128, 128], BF16)
        W2nb = pers.get([128, 128], BF16)
        HBASE = pers.off
        Hst = pers.get([128, LMAX // 128, 384], BF16)
        PBASE = pers.off

        pb = [es.enter_context(nc.psum_tensor("pb%d" % i, [128, 512], F32)) for i in range(8)]

        def pbf(i, a, b):
            return pb[i][:, a:b]

        def pbb(i, a, b):
            return pb[i][:, a // 2:b // 2].bitcast(BF16)

        block = es.enter_context(nc.Block())

        DMA(masksb, masks_d.rearrange("m k s -> k m s"), [], ["identb", "gtb", "ltb", "masksb", "masks"], "c0")
        DMA(c64, c64_d.rearrange("m k s -> k m s"), [], ["c64"], "c1")
        DMA(normw_col, normw_d.rearrange("(kt k) -> k kt", k=128), [], ["normw_col"], "c2", allow_slow_non_contiguous=True)
        DMA(flag_col, flag_d.partition_broadcast(128), [], ["flag"], "c3")
        V(lambda e: e.memset(ones_row, 1.0), [], ["ones_row"])
        S.barrier()

        def load_uT(seq, L, sc, buf, key, sem):
            t0 = 512 * sc - 2
            lo = max(t0, 0)
            hi = min(t0 + 516, L)
            src = UT[:, :, lo:hi].rearrange("kt k t -> k kt t")
            return DMA(buf[:, :, lo - t0:hi - t0], src, [("UT", sc - 1), ("UT", sc), ("UT", sc + 1)], [key], sem)

        def pass0(seq, L):
            al = Alloc(PBASE)
            xt = [al.get([128, D], F32) for _ in range(2)]
            junk = al.get([128, D], BF16)
            ub = [al.get([128, D], BF16) for _ in range(2)]
            ss = [al.get([128, 1], F32) for _ in range(2)]
            rstd = [al.get([128, 1], F32) for _ in range(2)]
            uts = [al.get([128, 8, 512], BF16) for _ in range(2)]
            NSC = L // 512
            for sc in range(NSC):
                for c in range(4):
                    j = 4 * sc + c
                    b = j % 2
                    DMA(xt[b], x_d[seq][128 * j:128 * (j + 1), :], [], [("xt", b)], "xt%d" % b)
                    act("scalar", junk, xt[b], AF.Square, [("xt", b)], ["junk", ("ss", b)], accum_out=ss[b])
                    ts("vector", rstd[b], ss[b], 1.0 / D, EPS, ALU.mult, ALU.add, [("ss", b)], [("rstd", b)])
                    V(lambda e, b=b: e.reciprocal(out=rstd[b], in_=rstd[b]), [("rstd", b)], [("rstd", b)])
                    act("scalar", rstd[b], rstd[b], AF.Sqrt, [("rstd", b)], [("rstd", b)])
                    act("scalar", ub[b], xt[b], AF.Copy, [("xt", b), ("rstd", b)], [("ub", b)], scale=rstd[b])
                    bank = j % 2
                    for kt in range(8):
                        tr(pbb(bank, 128 * kt, 128 * (kt + 1)), ub[b][:, 128 * kt:128 * (kt + 1)], identb,
                           [("ub", b), "identb"], [("pT", bank)])
                    cp("vector", uts[sc % 2][:, :, 128 * c:128 * (c + 1)],
                       pbb(bank, 0, 1024).rearrange("p (a b) -> p a b", a=8), [("pT", bank)], [("uts", sc % 2, c)])
                DMA(UT[:, :, 512 * sc:512 * (sc + 1)].rearrange("kt k t -> k kt t"), uts[sc % 2],
                    [("uts", sc % 2, c) for c in range(4)], [("UT", sc)], "uts%d" % (sc % 2))

        def load_group(g, L):
            import os
            dbg = os.environ.get("KDEBUG", "")
            al = Alloc(PBASE)
            wst = [al.get([128, GC], F32) for _ in range(2)]
            tmp12 = al.get([128, 12], F32)
            if not dbg or "w" in dbg:
              for kt in range(8):
                b = kt % 2
                DMA(wst[b], win_d[g, 128 * kt:128 * (kt + 1), :], [], [("wst", b)], "wst%d" % b)
                eng_ = "vector" if kt % 2 == 0 else "gpsimd"
                ts(eng_, Wg[:, kt, 512:GC], wst[b][:, 512:GC], normw_col[:, kt:kt + 1], None, ALU.mult, None,
                   [("wst", b), "normw_col"], [("Wg", kt)])
                ts(eng_, Wg[:, kt, 0:512], wst[b][:, 0:512], normw_col[:, kt:kt + 1], 0.5, ALU.mult, ALU.mult,
                   [("wst", b), "normw_col"], [("Wg", kt)])
            if not dbg or "c" in dbg:
              DMA(convw, convw_d[g], [], ["convw"], "c0")
              DMA(convb, convb_d[g], [], ["convb"], "c1")
              for i25 in range(25):
                  ts("vector" if i25 % 2 == 0 else "gpsimd", Dg[:, i25, :], identb, convw[:, i25:i25 + 1], 0.5, ALU.mult, ALU.mult,
                     ["identb", "convw"], [("Dg", i25)])
              ts("vector", convwh, convw, 0.5, None, ALU.mult, None, ["convw"], ["convwh"])
              ts("vector", cbh, convb, 0.5, None, ALU.mult, None, ["convb"], ["cbh"])
              DMA(cbrow_f, cbrow_d[g:g + 1, :], [], ["cbrow_f"], "c8")
              ts("vector", cbrow, cbrow_f, 0.5, None, ALU.mult, None, ["cbrow_f"], ["cbrow"])
              DMA(dtb_bc, dtb_d[g].partition_broadcast(128), [], ["dtb"], "c2")
              DMA(tmp12, alog_d[g].partition_broadcast(128), [], ["tmp12"], "c3")
              DMA(dsk_bc, dsk_d[g].partition_broadcast(128), [], ["dsk"], "c4")
              DMA(snw_bc, snw_d[g].partition_broadcast(128), [], ["snw"], "c5")
              for h_ in range(6):
                  ts("gpsimd", DgD[:, h_, :], identb, dsk_bc[:, h_:h_ + 1], None, ALU.mult, None, ["identb", "dsk"], [("DgD", h_)])
              act("scalar", A_bc, tmp12, AF.Exp, ["tmp12"], ["A_bc"])
              ts("vector", A_bc, A_bc, -1.0, None, ALU.mult, None, ["A_bc"], ["A_bc"])
            if not dbg or "f" in dbg:
              V(lambda e: e.memset(wfblk, 0.0), [], ["wfblk"])
              DMA(wfblk[0:64, 0:64], wfm_d[2 * g], [], ["wfblk"], "c6")
              DMA(wfblk[64:128, 64:128], wfm_d[2 * g + 1], [], ["wfblk"], "c7")
              sc_ = 1.0 / math.sqrt(64.0 * L)
              split3("vector", wfblk, [wf3[:, i, :] for i in range(3)], wftmp, ["wfblk"], "wf3")
              pairs = [(i, j_) for i in range(3) for j_ in range(3) if i + j_ <= 2]
              for m_ in range(2):
                  for n_, (i, j_) in enumerate(pairs):
                      mm(pbf(0, 128 * m_, 128 * (m_ + 1)), c64[:, 3 * m_ + i, :], wf3[:, j_, :], n_ == 0, n_ == len(pairs) - 1,
                         ["c64", "wf30", "wf31", "wf32"], ["w%dp" % (m_ + 1)])
              ts("vector", W1b, pbf(0, 0, 128), sc_, None, ALU.mult, None, ["w1p"], ["W1b"])
              ts("vector", W2nb, pbf(0, 128, 256), -sc_, None, ALU.mult, None, ["w2p"], ["W2nb"])
            S.barrier()

        def passF(seq, L):
            C = L // 128
            NSC = L // 512
            al = Alloc(PBASE)
            UF = al.get([128, C, 128], BF16)
            UTf = al.get([128, 128, 128], BF16)
            Asb = al.get([128, 2, C, 128], BF16)
            csb = al.get([128, 2 * C], BF16)
            uts = [al.get([128, 8, 516], BF16) for _ in range(2)]
            t2b = [al.get([128, 4, 1024], BF16) for _ in range(2)]
            pq = [al.get([128, 256], BF16) for _ in range(2)]
            yst = [al.get([128, 4, 128], F32) for _ in range(2)]
            DMA(csb[0:C, :], cs_d[seq], [], ["csb"], "c0")
            load_uT(seq, L, 0, uts[0], ("uts", 0), "uts0")
            for sc in range(NSC):
                if sc + 1 < NSC:
                    load_uT(seq, L, sc + 1, uts[(sc + 1) % 2], ("uts", (sc + 1) % 2), "uts%d" % ((sc + 1) % 2))
                u = uts[sc % 2]
                for c in range(4):
                    j = 4 * sc + c
                    bank = j % 2
                    for kt in range(8):
                        mm(pbf(bank, 0, 128), u[:, kt, 2 + 128 * c:2 + 128 * (c + 1)], Wg[:, kt, COL_UF:COL_UF + 128],
                           kt == 0, kt == 7, [("uts", sc % 2), ("Wg", kt)], [("pu", bank)])
                    cp("scalar" if j % 2 == 0 else "vector", UF[:, j, :], pbf(bank, 0, 128), [("pu", bank)], ["UF"])
            for cb in range(16):
                bank = 2 + cb % 2
                for i in range(8):
                    ch = 8 * cb + i
                    tr(pbb(bank, 128 * i, 128 * (i + 1))[0:C], UF[:, :, ch], identb, ["UF", "identb"], [("pt", bank)])
                cp("vector" if cb % 2 == 0 else "scalar", UTf[0:C, 8 * cb:8 * cb + 8, :],
                   pbb(bank, 0, 1024)[0:C].rearrange("p (a b) -> p a b", a=8), [("pt", bank)], [("UTf", cb)])
            nper = min(512 // (2 * C), 128)
            nb_ = 128 // nper
            for bi in range(nb_):
                bank = 4 + bi % 2
                for i in range(nper):
                    ch = bi * nper + i
                    mm(pbf(bank, 2 * C * i, 2 * C * (i + 1)), UTf[0:C, ch, :], csb[0:C, :], True, True,
                       [("UTf", ch // 8), "csb"], [("pa", bank)])
                cp("vector" if bi % 2 == 0 else "scalar", Asb[:, :, :, bi * nper:(bi + 1) * nper],
                   pbf(bank, 0, 2 * C * nper).rearrange("p (c r k) -> p r k c", c=nper, r=2),
                   [("pa", bank)], [("Asb", bi)])
            areads = [("Asb", bi) for bi in range(nb_)]
            NP = C // 4
            Ch = C // 2
            DMA(t2b[0], t2_d[seq][:, 0:4, :], [], [("t2b", 0)], "t2b0")
            for pc in range(NP):
                if pc + 1 < NP:
                    DMA(t2b[(pc + 1) % 2], t2_d[seq][:, 4 * (pc + 1):4 * (pc + 2), :], [], [("t2b", (pc + 1) % 2)],
                        "t2b%d" % ((pc + 1) % 2))
                tb = t2b[pc % 2]
                ybank = pc % 2
                for kk in range(4):
                    k2 = 4 * pc + kk
                    kp = (k2 + Ch) % C
                    sl = k2 % 2
                    o = pbf(6 + sl, 0, 256)
                    srcs = ((0, k2, 0), (1, k2, 256), (0, kp, 512), (1, kp, 768))
                    for n_, (ri, kq, off) in enumerate(srcs):
                        mm(o, Asb[:, ri, kq, :], tb[:, kk, off:off + 256], n_ == 0, n_ == 3,
                           areads + [("t2b", pc % 2)], [("ppq", sl)])
                    cp("scalar" if sl == 0 else "vector", pq[sl], o, [("ppq", sl)], [("pq", sl)])
                    yo = pbf(ybank, 128 * kk, 128 * (kk + 1))
                    mm(yo, pq[sl][:, 0:128], W1b, True, False, [("pq", sl), "W1b"], [("py", ybank, kk)])
                    mm(yo, pq[sl][:, 128:256], W2nb, False, True, [("pq", sl), "W2nb"], [("py", ybank, kk)])
                cp("vector" if pc % 2 == 0 else "scalar", yst[pc % 2],
                   pbf(ybank, 0, 512).rearrange("p (a b) -> p a b", a=4),
                   [("py", ybank, qq) for qq in range(4)], [("yst", pc % 2)])
                DMA(YF[0:L].rearrange("(k1 k2) d -> k1 k2 d", k2=C)[:, 4 * pc:4 * pc + 4, :], yst[pc % 2],
                    [("yst", pc % 2)], [("YF", pc)], "yst%d" % (pc % 2))
            S.barrier()

        def ssd_front(u, ukey, WIN, ACC, CV, cvk, sc, NSC, tiles, cbanks, ACCD, WINP=None, qcur=0):
            last = (sc == NSC - 1)
            NM = 510 if last else 512
            for ti, t in enumerate(tiles):
                bank = ti % 2
                col = COL_XS + 128 * t
                for kt in range(8):
                    mm(pbf(bank, 0, NM), Wg[:, kt, col:col + 128], u[:, kt, 4:4 + NM], kt == 0, kt == 7,
                       [ukey, ("Wg", kt)], [("pfm", bank)])
                cp("scalar", WIN[:, t, 4:4 + NM], pbf(bank, 0, NM), [("pfm", bank)], [("WIN", t)])
                if last:
                    G(lambda e, t=t: e.memset(WIN[:, t, 514:516], 0.0), [], [("WIN", t)])
                if sc == NSC // 2 - 1:
                    ts("vector", WIN[:, t, 514:516], WIN[:, t, 514:516], flag_col[:, 0:1], None, ALU.mult, None,
                       [("WIN", t), "flag"], [("WIN", t)])
                rk = (("WIN", t), 1 - qcur)
                if sc == 0 or sc == NSC // 2:
                    if sc == 0:
                        G(lambda e, t=t: e.memset(WIN[:, t, 0:2], 0.0), [], [("WIN", t)])
                    else:
                        ts("vector", WIN[:, t, 0:2], WINP[:, t, 512:514], flag_col[:, 0:1], None, ALU.mult, None,
                           [rk, "flag"], [("WIN", t)])
                    hb = pbf(2, 4 * ti, 4 * ti + 2)
                    for kt in range(8):
                        mm(hb, Wg[:, kt, col:col + 128], u[:, kt, 2:4], kt == 0, kt == 7,
                           [ukey, ("Wg", kt)], [("phalo", ti, 0)])
                    cp("vector", WIN[:, t, 2:4], hb, [("phalo", ti, 0)], [("WIN", t)])
                else:
                    cp("vector", WIN[:, t, 0:4], WINP[:, t, 512:516], [rk], [("WIN", t)])
                if ti in (1, 3):
                    ab = ACCD[(ti // 2) % 2]
                    ak = ("ACCD", (ti // 2) % 2)
                    ts("vector", ab, WIN[:, t, 0:512], convwh[:, 5 * t:5 * t + 1], cbh[:, t:t + 1], ALU.mult, ALU.add,
                       [("WIN", t), "convwh", "cbh"], [ak])
                    for k in range(1, 5):
                        stt("vector", ab, WIN[:, t, k:k + 512], convwh[:, 5 * t + k:5 * t + k + 1], ab, ALU.mult, ALU.add,
                            [("WIN", t), "convwh", ak], [ak])
                    act("scalar", ACC[ti % 2], ab, AF.Tanh, [ak], [("TT", ti % 2)])
                    stt("vector", CV[:, t, :], ACC[ti % 2], 1.0, ab, ALU.add, ALU.mult, [("TT", ti % 2), ak], [(cvk, t)])
                    continue
                cbk = cbanks[(ti // 2) % 2]
                for k in range(5):
                    mm(pbf(cbk, 0, 512), Dg[:, 5 * t + k, :], WIN[:, t, k:k + 512], k == 0, False,
                       [("WIN", t), ("Dg", 5 * t + k)], [("pcv", cbk)])
                mm(pbf(cbk, 0, 512), cbrow[0:1, 128 * t:128 * (t + 1)], ones_row[0:1, 0:512], False, True,
                   ["cbrow", "ones_row"], [("pcv", cbk)])
                act("scalar", ACC[ti % 2], pbf(cbk, 0, 512), AF.Tanh, [("pcv", cbk)], [("TT", ti % 2)])
                stt("vector", CV[:, t, :], ACC[ti % 2], 1.0, pbf(cbk, 0, 512), ALU.add, ALU.mult,
                    [("TT", ti % 2), ("pcv", cbk)], [(cvk, t)])

        def dt_block(u, ukey, DTR, DTV, AV, dk, A3, ATMP):
            for c in range(4):
                for kt in range(8):
                    mm(pbf(2, 20 + 12 * c, 20 + 12 * (c + 1)), u[:, kt, 2 + 128 * c:2 + 128 * (c + 1)],
                       Wg[:, kt, COL_DT:COL_DT + 12], kt == 0, kt == 7, [ukey, ("Wg", kt)], [("pdt", c)])
            tt("vector", DTR, pbf(2, 20, 68).rearrange("p (a b) -> p a b", a=4),
               dtb_bc.unsqueeze(1).broadcast_to([128, 4, 12]), ALU.add, [("pdt", c) for c in range(4)] + ["dtb"], [dk + "r"])
            act("scalar", DTR, DTR, AF.Exp, [dk + "r"], [dk + "r"])
            act("scalar", DTV, DTR, AF.Ln, [dk + "r"], [dk + "v"], bias=1.0)
            tt("vector", AV, DTV, A_bc.unsqueeze(1).broadcast_to([128, 4, 12]), ALU.mult, [dk + "v", "A_bc"], [dk + "a"])
            split3("gpsimd", AV.rearrange("p a b -> p (a b)"), [A3[:, i, :] for i in range(3)], ATMP, [dk + "a"], dk + "a3")

        def to_token_major(CV, cvk, c, XSB, xk):
            for t in range(4):
                tr(pbb(2, 512 + 128 * t, 512 + 128 * (t + 1)), CV[:, t, 128 * c:128 * (c + 1)], identb, [(cvk, t), "identb"], ["ptm"])
            cp("vector", XSB, pbb(2, 512, 1024), ["ptm"], [xk])

        def passA(seq, L):
            NSC = L // 512
            al = Alloc(PBASE)
            two = lambda shape, dt: [al.get(shape, dt) for _ in range(2)]
            uts = two([128, 8, 516], BF16)
            WIN2 = two([128, 5, 516], BF16)
            ACC = two([128, 512], F32)
            ACCD = two([128, 512], F32)
            CV2 = two([128, 5, 512], BF16)
            XSB = two([128, 512], BF16)
            DTR2, DTV2, AV2 = two([128, 4, 12], F32), two([128, 4, 12], F32), two([128, 4, 12], F32)
            A32 = two([128, 3, 48], BF16)
            ATMP2 = two([128, 48], F32)
            E = two([128, 2, 12], F32)
            SCL = two([128, 6], F32)
            XD = two([128, 384], BF16)
            SC_LOCAL = ("WIN", "CV", "dtr", "dtv", "dta", "dta30", "dta31", "dta32", "dta3t")
            H = al.get([128, 384], F32)
            V(lambda e: e.memset(H, 0.0), [], ["H"])
            load_uT(seq, L, 0, uts[0], ("uts", 0), "uts0")
            for sc in range(NSC):
                if sc + 1 < NSC:
                    load_uT(seq, L, sc + 1, uts[(sc + 1) % 2], ("uts", (sc + 1) % 2), "uts%d" % ((sc + 1) % 2))
                u, ukey = uts[sc % 2], ("uts", sc % 2)
                q_ = sc % 2
                WIN, CV, DTR, DTV, AV, A3, ATMP = WIN2[q_], CV2[q_], DTR2[q_], DTV2[q_], AV2[q_], A32[q_], ATMP2[q_]
                S.kmap.update({n_: q_ for n_ in SC_LOCAL})
                ssd_front(u, ukey, WIN, ACC, CV, "CV", sc, NSC, [0, 1, 2, 3, 4], (6, 7), ACCD, WIN2[1 - q_] if sc > 0 else None, q_)
                dt_block(u, ukey, DTR, DTV, AV, "dt", A3, ATMP)
                DMA(BCTs[:, :, 512 * sc:512 * (sc + 1)].rearrange("t k l -> k t l"), CV[:, 3:5, :], [("CV", 3), ("CV", 4)],
                    [("BCTs", sc)], "bct%d" % q_)
                DMA(DTPs[sc, :, 0:48], DTV.rearrange("p a b -> p (a b)"), ["dtv"], [("DTPs", sc, 0)], "dtpa%d" % q_)
                DMA(DTPs[sc, :, 48:96], AV.rearrange("p a b -> p (a b)"), ["dta"], [("DTPs", sc, 1)], "dtpb%d" % q_)
                for c in range(4):
                    j = 4 * sc + c
                    b = j % 2
                    to_token_major(CV, "CV", c, XSB[b], ("XSB", b))
                    DMA(XSBs[128 * j:128 * (j + 1), :], XSB[b], [("XSB", b)], [("XSBs", j)], "xsbw%d" % b)
                    for mi, m_ in enumerate((bGT, bONE)):
                        for i3 in range(3):
                            mm(pbf(2, 68 + 12 * mi, 80 + 12 * mi), m_, A3[:, i3, 12 * c:12 * c + 12], i3 == 0, i3 == 2,
                               ["masksb", "dta30", "dta31", "dta32"], [("pcs", mi)])
                    act("scalar", E[b], pbf(2, 68, 92).rearrange("p (a b) -> p a b", a=2), AF.Exp,
                        [("pcs", 0), ("pcs", 1)], [("E", b)])
                    tt("vector", SCL[b], E[b][:, 0, 0:6], DTV[:, c, 0:6], ALU.mult, [("E", b), "dtv"], [("SCL", b)])
                    tt("gpsimd", XD[b].rearrange("p (h d) -> p h d", h=6), XSB[b][:, 0:384].rearrange("p (h d) -> p h d", h=6),
                       SCL[b].unsqueeze(2).broadcast_to([128, 6, 64]), ALU.mult, [("XSB", b), ("SCL", b)], [("XD", b)])
                    mm(pbf(4 + b, 0, 384), XSB[b][:, 384:512], XD[b], True, True, [("XSB", b), ("XD", b)], [("pst", b)])
                    if j == (L // 128) // 2:
                        ts("vector", H, H, flag_col[:, 0:1], None, ALU.mult, None, ["H", "flag"], ["H"])
                    cp("scalar", Hst[:, j, :], H, ["H"], [("Hst", j)])
                    tt("vector", H.rearrange("p (h d) -> p h d", h=6), H.rearrange("p (h d) -> p h d", h=6),
                       E[b][:, 1, 0:6].unsqueeze(2).broadcast_to([128, 6, 64]), ALU.mult, ["H", ("E", b)], ["H"])
                    tt("vector", H, H, pbf(4 + b, 0, 384), ALU.add, ["H", ("pst", b)], ["H"])
            S.kmap.clear()
            S.barrier()

        def passB(seq, L, g):
            NSC = L // 512
            al = Alloc(PBASE)
            two = lambda shape, dt: [al.get(shape, dt) for _ in range(2)]
            uts = two([128, 8, 516], BF16)
            WIN2 = two([128, 5, 516], BF16)
            ACC = two([128, 512], F32)
            CV2 = two([128, 5, 512], BF16)
            XSB = two([128, 512], BF16)
            SZ = two([128, 512], F32)
            TZ = two([128, 512], F32)
            DTR2, DTV2, AV2 = two([128, 4, 12], F32), two([128, 4, 12], F32), two([128, 4, 12], F32)
            A32 = two([128, 3, 48], BF16)
            ATMP2 = two([128, 48], F32)
            E = two([128, 4, 12], F32)
            SCLB2 = two([128, 6], F32)
            RR2 = [two([128, 2, 6, 128], BF16) for _ in range(2)]
            CBM2 = [two([128, 128], F32) for _ in range(2)]
            DEC = two([128, 256], F32)
            MT2 = two([128, 2, 6, 128], BF16)
            XDT2 = [two([128, 384], BF16) for _ in range(2)]
            XSD2 = two([128, 384], BF16)
            XDB2 = two([128, 384], BF16)
            Gs = al.get([128, 384], F32)
            Gb = al.get([128, 384], BF16)
            T12, T22, YS2 = two([128, 384], F32), two([128, 384], F32), two([128, 384], F32)
            junk2 = two([128, 384], BF16)
            YFR = two([128, 128], F32)
            YG8 = [al.get([128, 384], F32) for _ in range(8)]
            YC8 = [al.get([128, 512], BF16) for _ in range(8)]
            ssq8 = al.get([128, 8], F32)
            rs8 = al.get([128, 8], F32)
            SC_LOCAL = ("WIN", "CV", "dtr", "dtv", "dta", "dta30", "dta31", "dta32", "dta3t")
            CH_LOCAL = ("CBM", "MT", "XDT", "XSD", "XDB", "SCLB", "RR", "T1", "T2a", "T2b", "YS", "junkb")
            V(lambda e: e.memset(Gs, 0.0), [], ["Gs"])
            V(lambda e: e.memset(Gb, 0.0), [], ["Gb"])
            order = list(range(NSC - 1, -1, -1))
            load_uT(seq, L, order[0], uts[0], ("uts", 0), "uts0")
            for oi, sc in enumerate(order):
                if oi + 1 < NSC:
                    load_uT(seq, L, order[oi + 1], uts[(oi + 1) % 2], ("uts", (oi + 1) % 2), "uts%d" % ((oi + 1) % 2))
                u, ukey = uts[oi % 2], ("uts", oi % 2)
                q_ = oi % 2
                WIN, CV, DTR, DTV, AV, A3, ATMP = WIN2[q_], CV2[q_], DTR2[q_], DTV2[q_], AV2[q_], A32[q_], ATMP2[q_]
                S.kmap.update({n_: q_ for n_ in SC_LOCAL})
                BC = CV[:, 3:5, :]
                DMA(BC, BCTs[:, :, 512 * sc:512 * (sc + 1)].rearrange("t k l -> k t l"), [], [("CV", 3), ("CV", 4)], "bcr%d" % q_)
                DMA(DTV.rearrange("p a b -> p (a b)"), DTPs[sc, :, 0:48], [], ["dtv"], "dtra%d" % q_)
                DMA(AV.rearrange("p a b -> p (a b)"), DTPs[sc, :, 48:96], [], ["dta"], "dtrb%d" % q_)
                split3("gpsimd", AV.rearrange("p a b -> p (a b)"), [A3[:, i, :] for i in range(3)], ATMP, ["dta"], "dta3")
                for c in range(3, -1, -1):
                    j = 4 * sc + c
                    b = j % 2
                    S.kmap.update({n_: b for n_ in CH_LOCAL})
                    RR, CBM, MT, XDT, XSD, XDB, SCLB = RR2[b], CBM2[b], MT2[b], XDT2[b], XSD2[b], XDB2[b], SCLB2[b]
                    T1, T2_, YS, junk = T12[b], T22[b], YS2[b], junk2[b]
                    DMA(YFR[b], YF[128 * j:128 * (j + 1), :], [], [("YFR", b)], "yfr%d" % b)
                    for kt in range(8):
                        mm(pbf(b, 0, 512), u[:, kt, 2 + 128 * c:2 + 128 * (c + 1)], Wg[:, kt, COL_Z:COL_Z + 512],
                           kt == 0, kt == 7, [ukey, ("Wg", kt)], [("pfm", b)])
                    act("scalar", TZ[b], pbf(b, 0, 512), AF.Tanh, [("pfm", b)], [("TZ", b)])
                    stt("vector", SZ[b], TZ[b], 1.0, pbf(b, 0, 512), ALU.add, ALU.mult, [("TZ", b), ("pfm", b)], [("SZ", b)])
                    DMA(XSB[b], XSBs[128 * j:128 * (j + 1), :], [], [("XSB", b)], "xsbr%d" % b)
                    for mi, m_ in enumerate((bLE, bGE, bLT, bONE)):
                        for i3 in range(3):
                            mm(pbf(2, 68 + 12 * mi, 80 + 12 * mi), m_, A3[:, i3, 12 * c:12 * c + 12], i3 == 0, i3 == 2,
                               ["masksb", "dta30", "dta31", "dta32"], [("pcs", mi)])
                    act("scalar", E[b], pbf(2, 68, 116).rearrange("p (a b) -> p a b", a=4), AF.Exp,
                        [("pcs", mi) for mi in range(4)], [("E", b)])
                    ef, eb_, dstb, cdb = E[b][:, 0, 0:6], E[b][:, 1, 6:12], E[b][:, 2, 6:12], E[b][:, 3, 6:12]
                    for d_ in range(2):
                        msk = bLE if d_ == 0 else bGE
                        for hl in range(2):
                            tt("gpsimd", RR[d_][:, hl, :, :], msk.unsqueeze(1).broadcast_to([128, 6, 128]),
                               A3[:, hl, 12 * c + 6 * d_:12 * c + 6 * d_ + 6].unsqueeze(2).broadcast_to([128, 6, 128]), ALU.mult,
                               ["masks", "dta30", "dta31"], [("RR", d_, hl)])
                    mm(pbf(2, 116, 244), CV[:, 3, 128 * c:128 * (c + 1)], CV[:, 4, 128 * c:128 * (c + 1)], True, True,
                       [("CV", 3), ("CV", 4)], ["pcb"])
                    tt("vector", CBM[0], pbf(2, 116, 244), bLE, ALU.mult, ["pcb", "masks"], [("CBM", 0)])
                    tt("vector", CBM[1], pbf(2, 116, 244), bGE, ALU.mult, ["pcb", "masks"], [("CBM", 1)])
                    it = 0
                    for d_ in range(2):
                        lt_ = gtb if d_ == 0 else ltb
                        for hp in range(3):
                            slot = it % 2
                            o = pbf(6 + slot, 0, 256)
                            for hl in range(2):
                                mm(o, lt_, RR[d_][:, hl, 2 * hp:2 * hp + 2, :].rearrange("p a b -> p (a b)"),
                                   hl == 0, hl == 1, ["gtb", "ltb", ("RR", d_, hl)], [("pseg", slot)])
                            act("scalar", DEC[slot], o, AF.Exp, [("pseg", slot)], [("DEC", slot)])
                            for hh in range(2):
                                h_ = 2 * hp + hh
                                stt("vector", MT[:, d_, h_, :], DEC[slot][:, 128 * hh:128 * (hh + 1)],
                                    DTV[:, c, 6 * d_ + h_:6 * d_ + h_ + 1], CBM[d_], ALU.mult, ALU.mult,
                                    [("DEC", slot), ("CBM", d_), "dtv"], [("MT", d_, hp)])
                            it += 1
                    for h in range(6):
                        ybk = 4
                        yo = pbf(ybk, 64 * h, 64 * (h + 1))
                        xh = XSB[b][:, 64 * h:64 * (h + 1)]
                        mm(yo, MT[:, 0, h, :], xh, True, False, [("MT", 0, h // 2), ("XSB", b)], [("py", b, h)])
                        mm(yo, MT[:, 1, h, :], xh, False, False, [("MT", 1, h // 2), ("XSB", b)], [("py", b, h)])
                        mm(yo, DgD[:, h, :], xh, False, True, [("DgD", h), ("XSB", b)], [("py", b, h)])
                    mm(pbf(5, 0, 384), CV[:, 4, 128 * c:128 * (c + 1)], Hst[:, j, :], True, True, [("CV", 4), ("Hst", j)], ["pzf"])
                    for h in range(6):
                        act("scalar", T1[:, 64 * h:64 * (h + 1)], pbf(5, 64 * h, 64 * (h + 1)), AF.Copy, ["pzf", ("E", b)], [("T1", h)],
                            scale=E[b][:, 0, h:h + 1])
                    mm(pbf(5, 0, 384), CV[:, 4, 128 * c:128 * (c + 1)], Gb, True, True, [("CV", 4), "Gb"], ["pzf"])
                    for h in range(6):
                        stt("vector", T1[:, 64 * h:64 * (h + 1)], pbf(5, 64 * h, 64 * (h + 1)), E[b][:, 1, 6 + h:7 + h],
                            T1[:, 64 * h:64 * (h + 1)], ALU.mult, ALU.add, ["pzf", ("E", b), ("T1", h)], [("T1", h)])
                    tt("vector", YS, pbf(4, 0, 384), T1, ALU.add, [("py", b, h) for h in range(6)] + [("T1", h) for h in range(6)], ["YS"])
                    tt("vector", SCLB, dstb, DTV[:, c, 6:12], ALU.mult, [("E", b), "dtv"], ["SCLB"])
                    tt("gpsimd", XDB.rearrange("p (h d) -> p h d", h=6), XSB[b][:, 0:384].rearrange("p (h d) -> p h d", h=6),
                       SCLB.unsqueeze(2).broadcast_to([128, 6, 64]), ALU.mult, [("XSB", b), "SCLB"], ["XDB"])
                    mm(pbf(3, 0, 384), XSB[b][:, 384:512], XDB, True, True, [("XSB", b), "XDB"], ["pst3"])
                    tt("vector", Gs.rearrange("p (h d) -> p h d", h=6), Gs.rearrange("p (h d) -> p h d", h=6),
                       cdb.unsqueeze(2).broadcast_to([128, 6, 64]), ALU.mult, ["Gs", ("E", b)], ["Gs"])
                    tt("vector", Gs, Gs, pbf(3, 0, 384), ALU.add, ["Gs", "pst3"], ["Gs"])
                    if j == (L // 128) // 2:
                        ts("vector", Gs, Gs, flag_col[:, 0:1], None, ALU.mult, None, ["Gs", "flag"], ["Gs"])
                    cp("scalar", Gb, Gs, ["Gs"], ["Gb"])
                    sl8 = 4 * q_ + c
                    tt("gpsimd", YG8[sl8], YS, SZ[b][:, 128:512], ALU.mult, ["YS", ("SZ", b)], [("YG8", sl8)])
                    act("scalar", junk, YG8[sl8], AF.Square, [("YG8", sl8)], ["junkb", ("ssq8", sl8)], accum_out=ssq8[:, sl8:sl8 + 1])
                    tt("gpsimd", YC8[sl8][:, 0:128], YFR[b], SZ[b][:, 0:128], ALU.mult, [("YFR", b), ("SZ", b)], [("YC8", sl8, 0)])
                    DMA(YCAT[128 * j:128 * (j + 1), 128 * g:128 * (g + 1)], YC8[sl8][:, 0:128], [("YC8", sl8, 0)], [("YCAT", j, 0)],
                        "yca%d" % sl8)
                rs4 = rs8[:, 4 * q_:4 * q_ + 4]
                ts("vector", rs4, ssq8[:, 4 * q_:4 * q_ + 4], 1.0 / 384, EPS, ALU.mult, ALU.add,
                   [("ssq8", 4 * q_ + c_) for c_ in range(4)], [("rs8", q_)])
                V(lambda e, rs4=rs4: e.reciprocal(out=rs4, in_=rs4), [("rs8", q_)], [("rs8", q_)])
                act("scalar", rs4, rs4, AF.Sqrt, [("rs8", q_)], [("rs8", q_)])
                for c in range(4):
                    j = 4 * sc + c
                    sl8 = 4 * q_ + c
                    stt("vector", YC8[sl8][:, 128:512], YG8[sl8], rs8[:, sl8:sl8 + 1], snw_bc, ALU.mult, ALU.mult,
                        [("YG8", sl8), ("rs8", q_), "snw"], [("YC8", sl8, 1)])
                    DMA(YCAT[128 * j:128 * (j + 1), 512 + 384 * g:512 + 384 * (g + 1)], YC8[sl8][:, 128:512], [("YC8", sl8, 1)],
                        [("YCAT", j, 1)], "ycb%d" % sl8)
            S.kmap.clear()
            S.barrier()

        def load_tail_weights():
            al = Alloc(HBASE)
            wo = al.get([128, 16, D], BF16)
            wg_ = al.get([128, 8, D], BF16)
            wp = al.get([128, 2, D], BF16)
            fnw = al.get([128, D], F32)
            wst = [al.get([128, D], F32) for _ in range(2)]
            i = 0
            for dst, src, n in ((wo, wout_d, 16), (wg_, wpg_d, 8), (wp, wpi_d, 2)):
                for kt in range(n):
                    b = i % 2
                    DMA(wst[b], src[128 * kt:128 * (kt + 1), :], [], [("wst", b)], "wst%d" % b)
                    if dst is wp:
                        ts("vector" if i % 2 == 0 else "gpsimd", dst[:, kt, :], wst[b], 0.5, None, ALU.mult, None, [("wst", b)], [("tw", i)])
                    else:
                        cp("vector" if i % 2 == 0 else "gpsimd", dst[:, kt, :], wst[b], [("wst", b)], [("tw", i)])
                    i += 1
            DMA(fnw, fnw_d.partition_broadcast(128), [], ["fnw"], "c0")
            S.barrier()
            return wo, wg_, wp, fnw, al

        def tail(seq, L, wo, wg_, wp, fnw, al):
            xt = [al.get([128, D], F32) for _ in range(2)]
            yc = [al.get([128, 2048], BF16) for _ in range(2)]
            pt = [al.get([128, 256], F32) for _ in range(2)]
            two = lambda shape, dt: [al.get(shape, dt) for _ in range(2)]
            ptb2, pT2, ycT2 = two([128, 256], BF16), two([128, 2, 128], BF16), two([128, 16, 128], BF16)
            h12, h1b2, h1T2 = two([128, D], F32), two([128, D], BF16), two([128, 8, 128], BF16)
            gate2, junk2 = two([128, D], F32), two([128, D], BF16)
            tq4 = [al.get([128, D], F32) for _ in range(4)]
            ssq4 = al.get([128, 4], F32)
            rs4 = al.get([128, 4], F32)
            T_LOCAL = ("ptb", "pT", "ycT", "h1", "h1b", "h1T", "gate", "junkb")
            ot = [al.get([128, D], F32) for _ in range(2)]
            for j in range(L // 128):
                b = j % 2
                S.kmap.update({n_: b for n_ in T_LOCAL})
                ptb, pT, ycT, h1, h1b, h1T = ptb2[b], pT2[b], ycT2[b], h12[b], h1b2[b], h1T2[b]
                gate, junk = gate2[b], junk2[b]
                s4 = j % 4
                tq = tq4[s4]
                DMA(xt[b], x_d[seq][128 * j:128 * (j + 1), :], [], [("xt", b)], "xt%d" % b)
                DMA(yc[b], YCAT[128 * j:128 * (j + 1), :], [], [("yc", b)], "ycl%d" % b)
                DMA(pt[b], pl_d[seq][128 * j:128 * (j + 1), :], [], [("pt", b)], "pt%d" % b)
                for half in range(2):
                    for i in range(8):
                        kt = 8 * half + i
                        tr(pbb(half, 128 * i, 128 * (i + 1)), yc[b][:, 128 * kt:128 * (kt + 1)], identb,
                           [("yc", b), "identb"], [("ptr", half)])
                    cp("vector" if half == 0 else "scalar", ycT[:, 8 * half:8 * half + 8, :],
                       pbb(half, 0, 1024).rearrange("p (a b) -> p a b", a=8), [("ptr", half)], [("ycT", half)])
                for nh in range(2):
                    for kt in range(16):
                        mm(pbf(2 + nh, 0, 512), ycT[:, kt, :], wo[:, kt, 512 * nh:512 * (nh + 1)], kt == 0, kt == 15,
                           [("ycT", kt // 8)], [("ph", nh)])
                    tt("vector", h1[:, 512 * nh:512 * (nh + 1)], pbf(2 + nh, 0, 512), xt[b][:, 512 * nh:512 * (nh + 1)], ALU.add,
                       [("ph", nh), ("xt", b)], [("h1", nh)])
                cp("scalar", h1b, h1, [("h1", 0), ("h1", 1)], ["h1b"])
                for kt in range(8):
                    tr(pbb(4, 128 * kt, 128 * (kt + 1)), h1b[:, 128 * kt:128 * (kt + 1)], identb, ["h1b", "identb"], ["ph1T"])
                cp("vector", h1T, pbb(4, 0, 1024).rearrange("p (a b) -> p a b", a=8), ["ph1T"], ["h1T"])
                cp("gpsimd", ptb, pt[b], [("pt", b)], ["ptb"])
                for kt in range(2):
                    tr(pbb(5, 128 * kt, 128 * (kt + 1)), ptb[:, 128 * kt:128 * (kt + 1)], identb, ["ptb", "identb"], ["ppT"])
                cp("vector", pT, pbb(5, 0, 256).rearrange("p (a b) -> p a b", a=2), ["ppT"], ["pT"])
                for nh in range(2):
                    for kt in range(8):
                        mm(pbf(6 + nh, 0, 512), h1T[:, kt, :], wg_[:, kt, 512 * nh:512 * (nh + 1)], kt == 0, kt == 7,
                           ["h1T"], [("pg", nh)])
                    act("scalar", gate[:, 512 * nh:512 * (nh + 1)], pbf(6 + nh, 0, 512), AF.Tanh, [("pg", nh)], [("gate", nh)], scale=0.5)
                    for kt in range(2):
                        mm(pbf(2 + nh, 0, 512), pT[:, kt, :], wp[:, kt, 512 * nh:512 * (nh + 1)], kt == 0, kt == 1,
                           ["pT"], [("ph", nh)])
                    stt("vector", tq[:, 512 * nh:512 * (nh + 1)], gate[:, 512 * nh:512 * (nh + 1)], 1.0, pbf(2 + nh, 0, 512),
                        ALU.add, ALU.mult, [("ph", nh), ("gate", nh)], [("tq4", s4, nh)])
                    tt("vector", tq[:, 512 * nh:512 * (nh + 1)], tq[:, 512 * nh:512 * (nh + 1)], h1[:, 512 * nh:512 * (nh + 1)],
                       ALU.add, [("tq4", s4, nh), ("h1", nh)], [("tq4", s4, nh)])
                act("scalar", junk, tq, AF.Square, [("tq4", s4, 0), ("tq4", s4, 1)], ["junkb", ("ssq4", s4)], accum_out=ssq4[:, s4:s4 + 1])
                if j % 2 == 1:
                    pr = (s4 // 2) * 2
                    rsp = rs4[:, pr:pr + 2]
                    ts("vector", rsp, ssq4[:, pr:pr + 2], 1.0 / D, EPS, ALU.mult, ALU.add, [("ssq4", pr), ("ssq4", pr + 1)], [("rs4", pr)])
                    V(lambda e, rsp=rsp: e.reciprocal(out=rsp, in_=rsp), [("rs4", pr)], [("rs4", pr)])
                    act("scalar", rsp, rsp, AF.Sqrt, [("rs4", pr)], [("rs4", pr)])
                    for jj in (j - 1, j):
                        sj = jj % 4
                        bb = jj % 2
                        stt("vector", ot[bb], tq4[sj], rs4[:, sj:sj + 1], fnw, ALU.mult, ALU.mult,
                            [("tq4", sj, 0), ("tq4", sj, 1), ("rs4", pr), "fnw"], [("ot", bb)])
                        DMA(y_d[seq][128 * jj:128 * (jj + 1), :], ot[bb], [("ot", bb)], [], "ot%d" % bb)
            S.kmap.clear()
            S.barrier()

        import os
        dbg = os.environ.get("KDEBUG", "")
        for seq, L in seqs:
            pass0(seq, L)
            S.barrier()
            for g in range(NG):
                if dbg and g > 0 and "4" not in dbg:
                    continue
                if dbg and "g" not in dbg:
                    continue
                load_group(g, L)
                if not dbg or "F" in dbg:
                    passF(seq, L)
                if not dbg or "A" in dbg:
                    passA(seq, L)
                if not dbg or "B" in dbg:
                    passB(seq, L, g)
            if not dbg or "T" in dbg:
                tw = load_tail_weights()
                tail(seq, L, *tw)
        S.emit(eng_sems, block)
    return nc


_CACHE = {}


def _prep_weights(w_in, conv_w, conv_b, a_log_f, a_log_b, dt_bias_f, dt_bias_b, d_skip, ssd_norm_w):
    w = w_in[0]
    cols = []
    for g in range(NG):
        idx = np.concatenate([
            np.arange(128 * g, 128 * g + 128),
            1024 + np.arange(384 * g, 384 * g + 384),
            512 + np.arange(128 * g, 128 * g + 128),
            2560 + np.arange(384 * g, 384 * g + 384),
            2560 + 1536 + np.arange(128 * g, 128 * g + 128),
            2560 + 2048 + np.arange(128 * g, 128 * g + 128),
            5120 + np.arange(6 * g, 6 * g + 6),
            5144 + np.arange(6 * g, 6 * g + 6),
        ])
        cols.append(idx)
    w_in_g = np.ascontiguousarray(np.stack([w[:, c] for c in cols], 0))
    cw, cbias = conv_w[0], conv_b[0]
    convw_g = np.zeros((NG, 128, 25), np.float32)
    convb_g = np.zeros((NG, 128, 5), np.float32)
    for g in range(NG):
        ch = np.concatenate([np.arange(384 * g, 384 * g + 384), 1536 + np.arange(128 * g, 128 * g + 128),
                             2048 + np.arange(128 * g, 128 * g + 128)])
        for t in range(5):
            cht = ch[128 * t:128 * (t + 1)]
            convw_g[g, :, 5 * t:5 * t + 5] = cw[:, cht].T
            convb_g[g, :, t] = cbias[cht]
    cbrow_g = np.ascontiguousarray(convb_g.transpose(0, 2, 1).reshape(NG, 640))
    sl = lambda v, g: v[0][6 * g:6 * g + 6]
    dtb_g = np.stack([np.concatenate([sl(dt_bias_f, g), sl(dt_bias_b, g)]) for g in range(NG)], 0)
    alog_g = np.stack([np.concatenate([sl(a_log_f, g), sl(a_log_b, g)]) for g in range(NG)], 0)
    dsk_g = np.stack([sl(d_skip, g) for g in range(NG)], 0)
    snw_g = np.ascontiguousarray(ssd_norm_w[0].reshape(NG, 384))
    return dict(w_in_g=w_in_g, convw_g=convw_g, convb_g=convb_g, cbrow_g=cbrow_g, dtb_g=np.ascontiguousarray(dtb_g, np.float32),
                alog_g=np.ascontiguousarray(alog_g, np.float32), dsk_g=np.ascontiguousarray(dsk_g, np.float32), snw_g=snw_g)


def run(x_prompt, x_sample, p_prompt, p_sample, norm_w, w_in, w_fmix, conv_w, conv_b, a_log_f, a_log_b,
        dt_bias_f, dt_bias_b, d_skip, ssd_norm_w, w_out, w_ple_in, w_ple_gate, final_norm_w):
    f = lambda a: np.ascontiguousarray(np.asarray(a, dtype=np.float32))
    x_prompt, x_sample, p_prompt, p_sample = f(x_prompt), f(x_sample), f(p_prompt), f(p_sample)
    Bp, Lp, _ = x_prompt.shape
    Bs, Ls, _ = x_sample.shape
    assert Ls == 2 * Lp
    if Ls not in _CACHE:
        _CACHE[Ls] = build(Ls)
    nc = _CACHE[Ls]
    common = _prep_weights(f(w_in), f(conv_w), f(conv_b), f(a_log_f), f(a_log_b), f(dt_bias_f), f(dt_bias_b),
                           f(d_skip), f(ssd_norm_w))
    masks, c64 = _masks()
    cs1, t21 = _consts(Ls, False)
    cs2, t22 = _consts(Ls, True)
    common.update(wfm=f(w_fmix)[0], w_out=f(w_out)[0], w_ple_in=f(w_ple_in)[0], w_ple_gate=f(w_ple_gate)[0],
                  fnw=f(final_norm_w), normw=f(norm_w)[0], masks=masks, c64=c64)
    import os
    n_dual = int(os.environ.get("KNDUAL", 8 - Bs))
    slots = []
    for c in range(Bs):
        slots.append(("s", c, None))
    pr = list(range(Bp))
    pairs = [[] for _ in range(n_dual)]
    for i, b in enumerate(pr):
        pairs[i % n_dual].append(b)
    assert all(len(p_) <= 2 for p_ in pairs), "prompt batch does not fit the slots"
    for p_ in pairs:
        a = p_[0] if len(p_) > 0 else 0
        b = p_[1] if len(p_) > 1 else a
        slots.append(("p", a, b if len(p_) > 1 else None, b))
    in_maps = []
    for sl in slots:
        m = dict(common)
        if sl[0] == "s":
            m["x_s"] = x_sample[sl[1]]
            m["pl_s"] = p_sample[0, sl[1]]
            m["cs_s"], m["t2_s"] = cs1, t21
            m["flag"] = np.ones((1,), np.float32)
        else:
            a, b = sl[1], sl[3]
            m["x_s"] = np.ascontiguousarray(np.concatenate([x_prompt[a], x_prompt[b]], 0))
            m["pl_s"] = np.ascontiguousarray(np.concatenate([p_prompt[0, a], p_prompt[0, b]], 0))
            m["cs_s"], m["t2_s"] = cs2, t22
            m["flag"] = np.zeros((1,), np.float32)
        in_maps.append(m)
    while len(in_maps) < 8:
        in_maps.append(in_maps[-1])
    res = run_bass_kernel_spmd(nc, in_maps, core_ids=list(range(8)))
    ys = np.stack([res.results[c]["y_s"] for c in range(Bs)], 0)
    yp = np.zeros((Bp, Lp, D), np.float32)
    for c, sl in enumerate(slots):
        if sl[0] == "p":
            o = res.results[c]["y_s"]
            yp[sl[1]] = o[:Lp]
            if sl[2] is not None:
                yp[sl[2]] = o[Lp:]
    return yp, ys.astype(np.float32)


def kernel(**inputs):
    return run(**inputs)
```

```python
import math
from contextlib import ExitStack
import numpy as np
import ml_dtypes
import concourse.bass as bass
import concourse.mybir as mybir
from concourse.bass_utils import run_bass_kernel_spmd

F32 = mybir.dt.float32
BF16 = mybir.dt.bfloat16
ALU = mybir.AluOpType
AF = mybir.ActivationFunctionType

D = 1024
NG = 4
GC = 1292
COL_Z, COL_UF, COL_XS, COL_B, COL_C, COL_DT = 0, 512, 640, 1024, 1152, 1280
EPS = 1e-6
SAME_ENGINE_RAW_SYNC = True
LIST_SCHEDULE = True
PRIO_CRITICAL = True
FILL_MIN_GAP = 700.0
FILL_SLACK = 250.0
FILL_COST = 240.0


class Op:
    __slots__ = ("eng", "fn", "deps", "odeps", "inc", "mile", "sem", "is_dma", "idx", "cost", "users", "npend", "ready", "fin", "blev")

    def __init__(self, eng, fn, is_dma=False, sem=None, cost=100.0):
        self.eng, self.fn, self.is_dma, self.sem, self.cost = eng, fn, is_dma, sem, cost
        self.deps = set()
        self.odeps = set()
        self.inc = False
        self.mile = None


class Sched:
    ENGS = ("sync", "scalar", "vector", "gpsimd", "tensor")

    def __init__(self, nc):
        self.nc = nc
        self.final = []
        self.seg = []
        self.last_w = {}
        self.readers = {}
        self.last_on = {}
        self.dmas_since = []
        self.excl_last = {}
        self.nops = 0
        self.kmap = {}
        self.filler = None
        self.nfill = 0

    def _xk(self, k):
        n = k if isinstance(k, str) else k[0]
        sfx = self.kmap.get(n)
        return k if sfx is None else (k, sfx)

    def _edge(self, op, d, raw):
        if d is op:
            return
        if d.is_dma or op.is_dma or d.eng != op.eng:
            op.deps.add(d)
        elif d.eng != "tensor" and SAME_ENGINE_RAW_SYNC:
            op.deps.add(d)
        else:
            op.odeps.add(d)

    def add(self, eng, fn, reads=(), writes=(), dma_sem=None, extra_deps=(), excl=(), cost=100.0):
        op = Op(eng, fn, dma_sem is not None, dma_sem, cost)
        op.idx = self.nops
        self.nops += 1
        if self.kmap:
            reads = [self._xk(k) for k in reads]
            writes = [self._xk(k) for k in writes]
        for k in excl:
            d = self.excl_last.setdefault(k, {})
            for e2, o2 in d.items():
                self._edge(op, o2, False)
            d[eng] = op
        for b in reads:
            w = self.last_w.get(b)
            if w is not None:
                self._edge(op, w, True)
        for b in writes:
            w = self.last_w.get(b)
            if w is not None:
                self._edge(op, w, False)
            for r in self.readers.get(b, ()):
                self._edge(op, r, False)
        for d in extra_deps:
            op.deps.add(d)
        for b in reads:
            self.readers.setdefault(b, []).append(op)
        for b in writes:
            self.last_w[b] = op
            self.readers[b] = []
        self.seg.append(op)
        if op.is_dma:
            self.dmas_since.append(op)
        else:
            self.last_on[eng] = op
        return op

    def _schedule_segment(self):
        import heapq
        seg = self.seg
        self.seg = []
        if not LIST_SCHEDULE:
            self.final.extend(seg)
            return
        inseg = set(id(o) for o in seg)
        for o in seg:
            o.users = []
            o.npend = 0
            o.ready = 0.0
        for o in seg:
            for d in list(o.deps) + list(o.odeps):
                if id(d) in inseg:
                    d.users.append(o)
                    o.npend += 1
        for o in reversed(seg):
            bl = 0.0
            for u in o.users:
                if u.blev > bl:
                    bl = u.blev
            o.blev = bl + o.cost
        if PRIO_CRITICAL:
            for o in seg:
                o.idx = (-o.blev, o.idx)
        future = {e: [] for e in self.ENGS}
        avail = {e: [] for e in self.ENGS}
        free = {e: 0.0 for e in self.ENGS}
        for o in seg:
            if o.npend == 0:
                heapq.heappush(future[o.eng], (0.0, o.idx, o))
        out = []
        n = len(seg)
        while len(out) < n:
            best = None
            for e in self.ENGS:
                fq, aq = future[e], avail[e]
                t0 = free[e]
                while fq and fq[0][0] <= t0:
                    r_, i_, o_ = heapq.heappop(fq)
                    heapq.heappush(aq, (i_, r_, o_))
                if aq:
                    st, key = t0, aq[0][0]
                elif fq:
                    st, key = fq[0][0], fq[0][1]
                else:
                    continue
                if best is None or (st, key) < (best[0], best[1]):
                    best = (st, key, e)
            st, key, e = best
            if avail[e]:
                o = heapq.heappop(avail[e])[2]
            else:
                o = heapq.heappop(future[e])[2]
            if e == "tensor" and self.filler is not None and free[e] > 0.0:
                gap = st - free[e]
                if gap > FILL_MIN_GAP:
                    for _ in range(min(int((gap - FILL_SLACK) / FILL_COST), 24)):
                        fo = Op("tensor", self.filler, False, None, FILL_COST)
                        fo.idx = -1
                        out.append(fo)
                        n += 1
                        self.nfill += 1
            if o.is_dma:
                free[e] = st + 60.0
                o.fin = st + o.cost
            else:
                free[e] = st + o.cost
                o.fin = free[e] + 40.0
            out.append(o)
            for u in o.users:
                u.npend -= 1
                if o.fin > u.ready:
                    u.ready = o.fin
                if u.npend == 0:
                    heapq.heappush(future[u.eng], (u.ready, u.idx, u))
        self.final.extend(out)

    def barrier(self):
        dmas = list(self.dmas_since)
        self.dmas_since = []
        self._schedule_segment()
        self.filler = None
        last = {}
        for o in self.final[::-1]:
            if not o.is_dma and o.eng not in last:
                last[o.eng] = o
                if len(last) == 4:
                    break
        deps = list(last.values()) + dmas
        for e in self.ENGS:
            self.add(e, lambda eng: eng.nop(), extra_deps=deps, cost=30.0)
        self._schedule_segment()
        self.last_w.clear()
        self.readers.clear()
        self.excl_last.clear()

    def emit(self, eng_sems, block):
        self._schedule_segment()
        ops = self.final
        for op in ops:
            for d in op.deps:
                if not d.is_dma:
                    d.inc = True
        cnt = {e: 0 for e in self.ENGS}
        dcnt = {}
        for op in ops:
            if op.is_dma:
                dcnt[op.sem] = dcnt.get(op.sem, 0) + 16
                op.mile = dcnt[op.sem]
            elif op.inc:
                cnt[op.eng] += 1
                op.mile = cnt[op.eng]
        by_eng = {e: [o for o in ops if o.eng == e] for e in self.ENGS}

        def run(eng_name, eng):
            waited = {}
            for op in by_eng[eng_name]:
                need = {}
                for d in op.deps:
                    s = d.sem if d.is_dma else eng_sems[d.eng]
                    if need.get(s, 0) < d.mile:
                        need[s] = d.mile
                for s, v in need.items():
                    if waited.get(s, 0) < v:
                        eng.wait_ge(s, v)
                        waited[s] = v
                ins = op.fn(eng)
                if op.is_dma:
                    ins.then_inc(op.sem, 16)
                elif op.inc:
                    ins.then_inc(eng_sems[eng_name], 1)

        @block.sync
        def _(e):
            run("sync", e)

        @block.scalar
        def _(e):
            run("scalar", e)

        @block.vector
        def _(e):
            run("vector", e)

        @block.gpsimd
        def _(e):
            run("gpsimd", e)

        @block.tensor
        def _(e):
            run("tensor", e)


def _consts(L, dual):
    C = L // 128
    Ch = C // 2
    j = np.arange(C)[:, None].astype(np.float64)
    k2 = np.arange(C)[None, :].astype(np.float64)
    if not dual:
        ang = 2 * np.pi * j * k2 / C
        cs = np.concatenate([np.cos(ang), np.sin(ang)], axis=1)
    else:
        cs = np.zeros((C, 2 * C))
        jj = np.arange(Ch)[:, None].astype(np.float64)
        rr = np.arange(Ch)[None, :].astype(np.float64)
        ang = 2 * np.pi * jj * rr / Ch
        cs[0:Ch, 0:Ch] = np.cos(ang)
        cs[0:Ch, C:C + Ch] = np.sin(ang)
        cs[Ch:C, Ch:C] = np.cos(ang)
        cs[Ch:C, C + Ch:2 * C] = np.sin(ang)
    p = np.arange(128)[:, None, None].astype(np.float64)
    kk2 = np.arange(C)[None, :, None].astype(np.float64)
    k1 = np.arange(128)[None, None, :].astype(np.float64)
    t2 = np.zeros((128, C, 2, 4, 128))
    if not dual:
        a2 = 2 * np.pi * ((p * (C * k1 + kk2)) % L) / L
        mc, ms = np.cos(a2), np.sin(a2)
        t2[:, :, 0] = np.stack([mc, ms, -ms, mc], axis=2)
    else:
        Lh = L // 2
        k1a = np.arange(128)[None, None, :]
        isA = (k1a < 64)
        freq = np.where(isA, C * k1 + kk2, C * (k1 - 64) + kk2)
        a2 = 2 * np.pi * ((p * freq) % Lh) / Lh
        mc, ms = np.cos(a2), np.sin(a2)
        full = np.stack([mc, ms, -ms, mc], axis=2) * math.sqrt(2.0)
        slotA = (np.arange(C) < Ch)[None, :, None, None]
        colA = isA[:, :, None, :] if isA.ndim == 3 else isA
        colA = np.broadcast_to((np.arange(128) < 64)[None, None, None, :], full.shape)
        own_mask = np.where(slotA, colA, ~colA)
        t2[:, :, 0] = np.where(own_mask, full, 0.0)
        t2[:, :, 1] = np.where(own_mask, 0.0, full)
    return cs.astype(ml_dtypes.bfloat16), np.ascontiguousarray(t2.reshape(128, C, 1024)).astype(ml_dtypes.bfloat16)


def _masks():
    k = np.arange(128)[:, None]
    s = np.arange(128)[None, :]
    m = np.stack([(k <= s), (k >= s), (k < s), (k > s), np.ones((128, 128), bool), np.eye(128, dtype=bool)], 0)
    c = np.arange(64)[:, None] * np.arange(64)[None, :]
    c64 = np.cos(2 * np.pi * c / 64)
    s64 = np.sin(2 * np.pi * c / 64)
    z = np.zeros((64, 64))
    cb = np.block([[c64, z], [z, c64]])
    sb = np.block([[s64, z], [z, s64]])
    out = []
    for t in (cb, sb):
        r = t.astype(np.float32)
        for _ in range(3):
            h = r.astype(ml_dtypes.bfloat16)
            out.append(h)
            r = (r - h.astype(np.float32)).astype(np.float32)
    return m.astype(ml_dtypes.bfloat16), np.stack(out, 0)


def build(Ls):
    nc = bass.Bass("TRN2", target_bir_lowering=False)
    LMAX = Ls
    seqs = [("s", Ls)]

    def din(name, shape, dt=F32):
        return nc.dram_tensor(name, shape, dt, kind="ExternalInput").ap()

    x_d = {"s": din("x_s", [Ls, D])}
    pl_d = {"s": din("pl_s", [Ls, 256])}
    y_d = {"s": nc.dram_tensor("y_s", [Ls, D], F32, kind="ExternalOutput").ap()}
    flag_d = din("flag", [1])
    win_d = din("w_in_g", [NG, D, GC])
    convw_d = din("convw_g", [NG, 128, 25])
    convb_d = din("convb_g", [NG, 128, 5])
    cbrow_d = din("cbrow_g", [NG, 640])
    dtb_d = din("dtb_g", [NG, 12])
    alog_d = din("alog_g", [NG, 12])
    dsk_d = din("dsk_g", [NG, 6])
    snw_d = din("snw_g", [NG, 384])
    wfm_d = din("wfm", [8, 64, 64])
    wout_d = din("w_out", [2048, D])
    wpi_d = din("w_ple_in", [256, D])
    wpg_d = din("w_ple_gate", [D, D])
    fnw_d = din("fnw", [D])
    normw_d = din("normw", [D])
    masks_d = din("masks", [6, 128, 128], BF16)
    c64_d = din("c64", [6, 128, 128], BF16)
    cs_d = {"s": din("cs_s", [Ls // 128, 2 * (Ls // 128)], BF16)}
    t2_d = {"s": din("t2_s", [128, Ls // 128, 1024], BF16)}
    UT = nc.dram_tensor("UT", [8, 128, LMAX], BF16).ap()
    YCAT = nc.dram_tensor("YCAT", [LMAX, 2048], BF16).ap()
    YF = nc.dram_tensor("YFs", [LMAX, 128], F32).ap()
    XSBs = nc.dram_tensor("XSBs", [LMAX, 512], BF16).ap()
    BCTs = nc.dram_tensor("BCTs", [2, 128, LMAX], BF16).ap()
    DTPs = nc.dram_tensor("DTPs", [LMAX // 512, 128, 96], F32).ap()

    es = ExitStack()
    with es:
        S = Sched(nc)
        eng_sems = {e: es.enter_context(nc.semaphore("s_" + e)) for e in Sched.ENGS}
        dsems = {}

        def dsem(name):
            if name not in dsems:
                dsems[name] = es.enter_context(nc.semaphore("d_" + name))
            return dsems[name]

        def banks(*aps):
            out = set()
            for a in aps:
                try:
                    nm = a.tensor.name
                except Exception:
                    continue
                if nm.startswith("pb"):
                    out.add(nm)
            return out

        def fsz(ap):
            n = 1
            for d_ in ap.shape[1:]:
                n *= d_
            return n

        def ecost(eng, ap, *ins):
            n = fsz(ap)
            slow = 1.0
            for a_ in ins:
                try:
                    if a_.ap[-1][0] == 0 and a_.ap[-1][1] > 1:
                        slow = 1.0
                except Exception:
                    pass
            if eng == "scalar":
                return 220.0 + n / 1.4
            if eng == "vector":
                return 70.0 + slow * n / 0.96
            return 120.0 + slow * n / 0.7

        def V(fn, r=(), w=()):
            return S.add("vector", fn, r, w, cost=100.0)

        def G(fn, r=(), w=()):
            return S.add("gpsimd", fn, r, w, cost=150.0)

        def DMA(out, in_, r, w, sem, **kw):
            nbytes = out.shape[0] * fsz(out) * (4 if out.dtype == F32 else 2)
            return S.add("sync", lambda e: e.dma_start(out=out, in_=in_, **kw), r, w, dma_sem=dsem(sem),
                         cost=2500.0 + nbytes / 60.0)

        def mm(out, lhsT, rhs, start, stop, r, w):
            bk = banks(out)
            return S.add("tensor", lambda e: e.matmul(out, lhsT=lhsT, rhs=rhs, start=start, stop=stop), r,
                         list(w) + [("accb", b_) for b_ in bk], excl=bk, cost=28.0 + fsz(rhs) * 0.45)

        def tr(out, in_, ident, r, w):
            bk = banks(out)
            return S.add("tensor", lambda e: e.transpose(out=out, in_=in_, identity=ident), r,
                         list(w) + [("accb", b_) for b_ in bk], excl=bk, cost=110.0)

        def act(eng, out, in_, func, r, w, **kw):
            return S.add(eng, lambda e: e.activation(out=out, in_=in_, func=func, **kw), r, w, excl=banks(out, in_),
                         cost=ecost(eng, out))

        def cp(eng, out, in_, r, w):
            if eng == "scalar":
                return S.add(eng, lambda e: e.activation(out=out, in_=in_, func=AF.Copy), r, w, excl=banks(out, in_),
                             cost=ecost(eng, out))
            return S.add(eng, lambda e: e.tensor_copy(out=out, in_=in_), r, w, excl=banks(out, in_), cost=ecost(eng, out))

        def tt(eng, out, in0, in1, op, r, w):
            return S.add(eng, lambda e: e.tensor_tensor(out=out, in0=in0, in1=in1, op=op), r, w, excl=banks(out, in0, in1),
                         cost=ecost(eng, out, in0, in1))

        def ts(eng, out, in0, s1, s2, op0, op1, r, w, **kw):
            if s2 is None:
                return S.add(eng, lambda e: e.tensor_scalar(out=out, in0=in0, scalar1=s1, scalar2=None, op0=op0, **kw), r, w,
                             excl=banks(out, in0), cost=ecost(eng, out))
            return S.add(eng, lambda e: e.tensor_scalar(out=out, in0=in0, scalar1=s1, scalar2=s2, op0=op0, op1=op1, **kw), r, w,
                         excl=banks(out, in0), cost=ecost(eng, out))

        def split3(eng, src, dst3, tmp, r, w):
            cp(eng, dst3[0], src, r, [w + "0"])
            tt(eng, tmp, src, dst3[0], ALU.subtract, list(r) + [w + "0"], [w + "t"])
            cp(eng, dst3[1], tmp, [w + "t"], [w + "1"])
            tt(eng, tmp, tmp, dst3[1], ALU.subtract, [w + "t", w + "1"], [w + "t"])
            cp(eng, dst3[2], tmp, [w + "t"], [w + "2"])

        def stt(eng, out, in0, scalar, in1, op0, op1, r, w):
            return S.add(eng, lambda e: e.scalar_tensor_tensor(out=out, in0=in0, scalar=scalar, in1=in1, op0=op0, op1=op1), r, w,
                         excl=banks(out, in0, in1), cost=ecost(eng, out))

        ARENA = 207 * 1024
        arena = es.enter_context(nc.sbuf_tensor("arena", [128, ARENA // 2], BF16))

        class Alloc:
            def __init__(self, base=0):
                self.off = base

            def get(self, shape, dt):
                n = 1
                for s_ in shape[1:]:
                    n *= s_
                nb = n * (4 if dt == F32 else 2)
                nb = (nb + 63) // 64 * 64
                assert self.off + nb <= ARENA, (self.off, nb)
                v = arena[:, self.off // 2:(self.off + nb) // 2]
                self.off += nb
                if dt == F32:
                    v = v.bitcast(F32)
                v = v[:, 0:n]
                if len(shape) == 3:
                    v = v.rearrange("p (a b) -> p a b", a=shape[1])
                elif len(shape) == 4:
                    v = v.rearrange("p (a b c) -> p a b c", a=shape[1], b=shape[2])
                if shape[0] < 128:
                    v = v[0:shape[0]]
                return v

        pers = Alloc(0)
        masksb = pers.get([128, 6, 128], BF16)
        bLE, bGE, bLT, bGT, bONE, identb = (masksb[:, i, :] for i in range(6))
        gtb, ltb = bGT, bLT
        c64 = pers.get([128, 6, 128], BF16)
        wf3 = pers.get([128, 3, 128], BF16)
        wftmp = pers.get([128, 128], F32)
        normw_col = pers.get([128, 8], F32)
        flag_col = pers.get([128, 1], F32)
        Wg = pers.get([128, 8, GC], BF16)
        convw = pers.get([128, 25], F32)
        convb = pers.get([128, 5], F32)
        Dg = pers.get([128, 25, 128], BF16)
        DgD = pers.get([128, 6, 128], BF16)
        convwh = pers.get([128, 25], F32)
        cbh = pers.get([128, 5], F32)
        cbrow = pers.get([1, 640], BF16)
        cbrow_f = pers.get([1, 640], F32)
        ones_row = pers.get([1, 512], BF16)
        dtb_bc = pers.get([128, 12], F32)
        A_bc = pers.get([128, 12], F32)
        dsk_bc = pers.get([128, 6], F32)
        snw_bc = pers.get([128, 384], F32)
        wfblk = pers.get([128, 128], F32)
        W1b = pers.get([128, 128], BF16)
        W2nb = pers.get([128, 128], BF16)
        HBASE = pers.off
        Hst = pers.get([128, LMAX // 128, 384], BF16)
        PBASE = pers.off

        pb = [es.enter_context(nc.psum_tensor("pb%d" % i, [128, 512], F32)) for i in range(8)]

        def pbf(i, a, b):
            return pb[i][:, a:b]

        def pbb(i, a, b):
            return pb[i][:, a // 2:b // 2].bitcast(BF16)

        block = es.enter_context(nc.Block())

        DMA(masksb, masks_d.rearrange("m k s -> k m s"), [], ["identb", "gtb", "ltb", "masksb", "masks"], "c0")
        DMA(c64, c64_d.rearrange("m k s -> k m s"), [], ["c64"], "c1")
        DMA(normw_col, normw_d.rearrange("(kt k) -> k kt", k=128), [], ["normw_col"], "c2", allow_slow_non_contiguous=True)
        DMA(flag_col, flag_d.partition_broadcast(128), [], ["flag"], "c3")
        V(lambda e: e.memset(ones_row, 1.0), [], ["ones_row"])
        S.barrier()

        def load_uT(seq, L, sc, buf, key, sem):
            t0 = 512 * sc - 2
            lo = max(t0, 0)
            hi = min(t0 + 516, L)
            src = UT[:, :, lo:hi].rearrange("kt k t -> k kt t")
            return DMA(buf[:, :, lo - t0:hi - t0], src, [("UT", sc - 1), ("UT", sc), ("UT", sc + 1)], [key], sem)

        def pass0(seq, L):
            al = Alloc(PBASE)
            xt = [al.get([128, D], F32) for _ in range(2)]
            junk = al.get([128, D], BF16)
            ub = [al.get([128, D], BF16) for _ in range(2)]
            ss = [al.get([128, 1], F32) for _ in range(2)]
            rstd = [al.get([128, 1], F32) for _ in range(2)]
            uts = [al.get([128, 8, 512], BF16) for _ in range(2)]
            NSC = L // 512
            for sc in range(NSC):
                for c in range(4):
                    j = 4 * sc + c
                    b = j % 2
                    DMA(xt[b], x_d[seq][128 * j:128 * (j + 1), :], [], [("xt", b)], "xt%d" % b)
                    act("scalar", junk, xt[b], AF.Square, [("xt", b)], ["junk", ("ss", b)], accum_out=ss[b])
                    ts("vector", rstd[b], ss[b], 1.0 / D, EPS, ALU.mult, ALU.add, [("ss", b)], [("rstd", b)])
                    V(lambda e, b=b: e.reciprocal(out=rstd[b], in_=rstd[b]), [("rstd", b)], [("rstd", b)])
                    act("scalar", rstd[b], rstd[b], AF.Sqrt, [("rstd", b)], [("rstd", b)])
                    act("scalar", ub[b], xt[b], AF.Copy, [("xt", b), ("rstd", b)], [("ub", b)], scale=rstd[b])
                    bank = j % 2
                    for kt in range(8):
                        tr(pbb(bank, 128 * kt, 128 * (kt + 1)), ub[b][:, 128 * kt:128 * (kt + 1)], identb,
                           [("ub", b), "identb"], [("pT", bank)])
                    cp("vector", uts[sc % 2][:, :, 128 * c:128 * (c + 1)],
                       pbb(bank, 0, 1024).rearrange("p (a b) -> p a b", a=8), [("pT", bank)], [("uts", sc % 2, c)])
                DMA(UT[:, :, 512 * sc:512 * (sc + 1)].rearrange("kt k t -> k kt t"), uts[sc % 2],
                    [("uts", sc % 2, c) for c in range(4)], [("UT", sc)], "uts%d" % (sc % 2))

        def load_group(g, L):
            import os
            dbg = os.environ.get("KDEBUG", "")
            al = Alloc(PBASE)
            wst = [al.get([128, GC], F32) for _ in range(2)]
            tmp12 = al.get([128, 12], F32)
            if not dbg or "w" in dbg:
              for kt in range(8):
                b = kt % 2
                DMA(wst[b], win_d[g, 128 * kt:128 * (kt + 1), :], [], [("wst", b)], "wst%d" % b)
                eng_ = "vector" if kt % 2 == 0 else "gpsimd"
                ts(eng_, Wg[:, kt, 512:GC], wst[b][:, 512:GC], normw_col[:, kt:kt + 1], None, ALU.mult, None,
                   [("wst", b), "normw_col"], [("Wg", kt)])
                ts(eng_, Wg[:, kt, 0:512], wst[b][:, 0:512], normw_col[:, kt:kt + 1], 0.5, ALU.mult, ALU.mult,
                   [("wst", b), "normw_col"], [("Wg", kt)])
            if not dbg or "c" in dbg:
              DMA(convw, convw_d[g], [], ["convw"], "c0")
              DMA(convb, convb_d[g], [], ["convb"], "c1")
              for i25 in range(25):
                  ts("vector" if i25 % 2 == 0 else "gpsimd", Dg[:, i25, :], identb, convw[:, i25:i25 + 1], 0.5, ALU.mult, ALU.mult,
                     ["identb", "convw"], [("Dg", i25)])
              ts("vector", convwh, convw, 0.5, None, ALU.mult, None, ["convw"], ["convwh"])
              ts("vector", cbh, convb, 0.5, None, ALU.mult, None, ["convb"], ["cbh"])
              DMA(cbrow_f, cbrow_d[g:g + 1, :], [], ["cbrow_f"], "c8")
              ts("vector", cbrow, cbrow_f, 0.5, None, ALU.mult, None, ["cbrow_f"], ["cbrow"])
              DMA(dtb_bc, dtb_d[g].partition_broadcast(128), [], ["dtb"], "c2")
              DMA(tmp12, alog_d[g].partition_broadcast(128), [], ["tmp12"], "c3")
              DMA(dsk_bc, dsk_d[g].partition_broadcast(128), [], ["dsk"], "c4")
              DMA(snw_bc, snw_d[g].partition_broadcast(128), [], ["snw"], "c5")
              for h_ in range(6):
                  ts("gpsimd", DgD[:, h_, :], identb, dsk_bc[:, h_:h_ + 1], None, ALU.mult, None, ["identb", "dsk"], [("DgD", h_)])
              act("scalar", A_bc, tmp12, AF.Exp, ["tmp12"], ["A_bc"])
              ts("vector", A_bc, A_bc, -1.0, None, ALU.mult, None, ["A_bc"], ["A_bc"])
            if not dbg or "f" in dbg:
              V(lambda e: e.memset(wfblk, 0.0), [], ["wfblk"])
              DMA(wfblk[0:64, 0:64], wfm_d[2 * g], [], ["wfblk"], "c6")
              DMA(wfblk[64:128, 64:128], wfm_d[2 * g + 1], [], ["wfblk"], "c7")
              sc_ = 1.0 / math.sqrt(64.0 * L)
              split3("vector", wfblk, [wf3[:, i, :] for i in range(3)], wftmp, ["wfblk"], "wf3")
              pairs = [(i, j_) for i in range(3) for j_ in range(3) if i + j_ <= 2]
              for m_ in range(2):
                  for n_, (i, j_) in enumerate(pairs):
                      mm(pbf(0, 128 * m_, 128 * (m_ + 1)), c64[:, 3 * m_ + i, :], wf3[:, j_, :], n_ == 0, n_ == len(pairs) - 1,
                         ["c64", "wf30", "wf31", "wf32"], ["w%dp" % (m_ + 1)])
              ts("vector", W1b, pbf(0, 0, 128), sc_, None, ALU.mult, None, ["w1p"], ["W1b"])
              ts("vector", W2nb, pbf(0, 128, 256), -sc_, None, ALU.mult, None, ["w2p"], ["W2nb"])
            S.barrier()

        def passF(seq, L):
            C = L // 128
            NSC = L // 512
            al = Alloc(PBASE)
            UF = al.get([128, C, 128], BF16)
            UTf = al.get([128, 128, 128], BF16)
            Asb = al.get([128, 2, C, 128], BF16)
            csb = al.get([128, 2 * C], BF16)
            uts = [al.get([128, 8, 516], BF16) for _ in range(2)]
            t2b = [al.get([128, 4, 1024], BF16) for _ in range(2)]
            pq = [al.get([128, 256], BF16) for _ in range(2)]
            yst = [al.get([128, 4, 128], F32) for _ in range(2)]
            DMA(csb[0:C, :], cs_d[seq], [], ["csb"], "c0")
            load_uT(seq, L, 0, uts[0], ("uts", 0), "uts0")
            for sc in range(NSC):
                if sc + 1 < NSC:
                    load_uT(seq, L, sc + 1, uts[(sc + 1) % 2], ("uts", (sc + 1) % 2), "uts%d" % ((sc + 1) % 2))
                u = uts[sc % 2]
                for c in range(4):
                    j = 4 * sc + c
                    bank = j % 2
                    for kt in range(8):
                        mm(pbf(bank, 0, 128), u[:, kt, 2 + 128 * c:2 + 128 * (c + 1)], Wg[:, kt, COL_UF:COL_UF + 128],
                           kt == 0, kt == 7, [("uts", sc % 2), ("Wg", kt)], [("pu", bank)])
                    cp("scalar" if j % 2 == 0 else "vector", UF[:, j, :], pbf(bank, 0, 128), [("pu", bank)], ["UF"])
            for cb in range(16):
                bank = 2 + cb % 2
                for i in range(8):
                    ch = 8 * cb + i
                    tr(pbb(bank, 128 * i, 128 * (i + 1))[0:C], UF[:, :, ch], identb, ["UF", "identb"], [("pt", bank)])
                cp("vector" if cb % 2 == 0 else "scalar", UTf[0:C, 8 * cb:8 * cb + 8, :],
                   pbb(bank, 0, 1024)[0:C].rearrange("p (a b) -> p a b", a=8), [("pt", bank)], [("UTf", cb)])
            nper = min(512 // (2 * C), 128)
            nb_ = 128 // nper
            for bi in range(nb_):
                bank = 4 + bi % 2
                for i in range(nper):
                    ch = bi * nper + i
                    mm(pbf(bank, 2 * C * i, 2 * C * (i + 1)), UTf[0:C, ch, :], csb[0:C, :], True, True,
                       [("UTf", ch // 8), "csb"], [("pa", bank)])
                cp("vector" if bi % 2 == 0 else "scalar", Asb[:, :, :, bi * nper:(bi + 1) * nper],
                   pbf(bank, 0, 2 * C * nper).rearrange("p (c r k) -> p r k c", c=nper, r=2),
                   [("pa", bank)], [("Asb", bi)])
            areads = [("Asb", bi) for bi in range(nb_)]
            NP = C // 4
            Ch = C // 2
            DMA(t2b[0], t2_d[seq][:, 0:4, :], [], [("t2b", 0)], "t2b0")
            for pc in range(NP):
                if pc + 1 < NP:
                    DMA(t2b[(pc + 1) % 2], t2_d[seq][:, 4 * (pc + 1):4 * (pc + 2), :], [], [("t2b", (pc + 1) % 2)],
                        "t2b%d" % ((pc + 1) % 2))
                tb = t2b[pc % 2]
                ybank = pc % 2
                for kk in range(4):
                    k2 = 4 * pc + kk
                    kp = (k2 + Ch) % C
                    sl = k2 % 2
                    o = pbf(6 + sl, 0, 256)
                    srcs = ((0, k2, 0), (1, k2, 256), (0, kp, 512), (1, kp, 768))
                    for n_, (ri, kq, off) in enumerate(srcs):
                        mm(o, Asb[:, ri, kq, :], tb[:, kk, off:off + 256], n_ == 0, n_ == 3,
                           areads + [("t2b", pc % 2)], [("ppq", sl)])
                    cp("scalar" if sl == 0 else "vector", pq[sl], o, [("ppq", sl)], [("pq", sl)])
                    yo = pbf(ybank, 128 * kk, 128 * (kk + 1))
                    mm(yo, pq[sl][:, 0:128], W1b, True, False, [("pq", sl), "W1b"], [("py", ybank, kk)])
                    mm(yo, pq[sl][:, 128:256], W2nb, False, True, [("pq", sl), "W2nb"], [("py", ybank, kk)])
                cp("vector" if pc % 2 == 0 else "scalar", yst[pc % 2],
                   pbf(ybank, 0, 512).rearrange("p (a b) -> p a b", a=4),
                   [("py", ybank, qq) for qq in range(4)], [("yst", pc % 2)])
                DMA(YF[0:L].rearrange("(k1 k2) d -> k1 k2 d", k2=C)[:, 4 * pc:4 * pc + 4, :], yst[pc % 2],
                    [("yst", pc % 2)], [("YF", pc)], "yst%d" % (pc % 2))
            S.barrier()

        def ssd_front(u, ukey, WIN, ACC, CV, cvk, sc, NSC, tiles, cbanks, ACCD, WINP=None, qcur=0):
            last = (sc == NSC - 1)
            NM = 510 if last else 512
            for ti, t in enumerate(tiles):
                bank = ti % 2
                col = COL_XS + 128 * t
                for kt in range(8):
                    mm(pbf(bank, 0, NM), Wg[:, kt, col:col + 128], u[:, kt, 4:4 + NM], kt == 0, kt == 7,
                       [ukey, ("Wg", kt)], [("pfm", bank)])
                cp("scalar", WIN[:, t, 4:4 + NM], pbf(bank, 0, NM), [("pfm", bank)], [("WIN", t)])
                if last:
                    G(lambda e, t=t: e.memset(WIN[:, t, 514:516], 0.0), [], [("WIN", t)])
                if sc == NSC // 2 - 1:
                    ts("vector", WIN[:, t, 514:516], WIN[:, t, 514:516], flag_col[:, 0:1], None, ALU.mult, None,
                       [("WIN", t), "flag"], [("WIN", t)])
                rk = (("WIN", t), 1 - qcur)
                if sc == 0 or sc == NSC // 2:
                    if sc == 0:
                        G(lambda e, t=t: e.memset(WIN[:, t, 0:2], 0.0), [], [("WIN", t)])
                    else:
                        ts("vector", WIN[:, t, 0:2], WINP[:, t, 512:514], flag_col[:, 0:1], None, ALU.mult, None,
                           [rk, "flag"], [("WIN", t)])
                    hb = pbf(2, 4 * ti, 4 * ti + 2)
                    for kt in range(8):
                        mm(hb, Wg[:, kt, col:col + 128], u[:, kt, 2:4], kt == 0, kt == 7,
                           [ukey, ("Wg", kt)], [("phalo", ti, 0)])
                    cp("vector", WIN[:, t, 2:4], hb, [("phalo", ti, 0)], [("WIN", t)])
                else:
                    cp("vector", WIN[:, t, 0:4], WINP[:, t, 512:516], [rk], [("WIN", t)])
                if ti in (1, 3):
                    ab = ACCD[(ti // 2) % 2]
                    ak = ("ACCD", (ti // 2) % 2)
                    ts("vector", ab, WIN[:, t, 0:512], convwh[:, 5 * t:5 * t + 1], cbh[:, t:t + 1], ALU.mult, ALU.add,
                       [("WIN", t), "convwh", "cbh"], [ak])
                    for k in range(1, 5):
                        stt("vector", ab, WIN[:, t, k:k + 512], convwh[:, 5 * t + k:5 * t + k + 1], ab, ALU.mult, ALU.add,
                            [("WIN", t), "convwh", ak], [ak])
                    act("scalar", ACC[ti % 2], ab, AF.Tanh, [ak], [("TT", ti % 2)])
                    stt("vector", CV[:, t, :], ACC[ti % 2], 1.0, ab, ALU.add, ALU.mult, [("TT", ti % 2), ak], [(cvk, t)])
                    continue
                cbk = cbanks[(ti // 2) % 2]
                for k in range(5):
                    mm(pbf(cbk, 0, 512), Dg[:, 5 * t + k, :], WIN[:, t, k:k + 512], k == 0, False,
                       [("WIN", t), ("Dg", 5 * t + k)], [("pcv", cbk)])
                mm(pbf(cbk, 0, 512), cbrow[0:1, 128 * t:128 * (t + 1)], ones_row[0:1, 0:512], False, True,
                   ["cbrow", "ones_row"], [("pcv", cbk)])
                act("scalar", ACC[ti % 2], pbf(cbk, 0, 512), AF.Tanh, [("pcv", cbk)], [("TT", ti % 2)])
                stt("vector", CV[:, t, :], ACC[ti % 2], 1.0, pbf(cbk, 0, 512), ALU.add, ALU.mult,
                    [("TT", ti % 2), ("pcv", cbk)], [(cvk, t)])

        def dt_block(u, ukey, DTR, DTV, AV, dk, A3, ATMP):
            for c in range(4):
                for kt in range(8):
                    mm(pbf(3, 20 + 12 * c, 20 + 12 * (c + 1)), u[:, kt, 2 + 128 * c:2 + 128 * (c + 1)],
                       Wg[:, kt, COL_DT:COL_DT + 12], kt == 0, kt == 7, [ukey, ("Wg", kt)], [("pdt", c)])
            tt("vector", DTR, pbf(3, 20, 68).rearrange("p (a b) -> p a b", a=4),
               dtb_bc.unsqueeze(1).broadcast_to([128, 4, 12]), ALU.add, [("pdt", c) for c in range(4)] + ["dtb"], [dk + "r"])
            act("scalar", DTR, DTR, AF.Exp, [dk + "r"], [dk + "r"])
            act("scalar", DTV, DTR, AF.Ln, [dk + "r"], [dk + "v"], bias=1.0)
            tt("vector", AV, DTV, A_bc.unsqueeze(1).broadcast_to([128, 4, 12]), ALU.mult, [dk + "v", "A_bc"], [dk + "a"])
            split3("gpsimd", AV.rearrange("p a b -> p (a b)"), [A3[:, i, :] for i in range(3)], ATMP, [dk + "a"], dk + "a3")

        def to_token_major(CV, cvk, c, XSB, xk):
            for t in range(4):
                tr(pbb(2, 512 + 128 * t, 512 + 128 * (t + 1)), CV[:, t, 128 * c:128 * (c + 1)], identb, [(cvk, t), "identb"], ["ptm"])
            cp("vector", XSB, pbb(2, 512, 1024), ["ptm"], [xk])

        def passA(seq, L):
            NSC = L // 512
            al = Alloc(PBASE)
            two = lambda shape, dt: [al.get(shape, dt) for _ in range(2)]
            uts = two([128, 8, 516], BF16)
            WIN2 = two([128, 5, 516], BF16)
            ACC = two([128, 512], F32)
            ACCD = two([128, 512], F32)
            CV2 = two([128, 5, 512], BF16)
            XSB = two([128, 512], BF16)
            DTR2, DTV2, AV2 = two([128, 4, 12], F32), two([128, 4, 12], F32), two([128, 4, 12], F32)
            A32 = two([128, 3, 48], BF16)
            ATMP2 = two([128, 48], F32)
            E = two([128, 2, 12], F32)
            SCL = two([128, 6], F32)
            XD = two([128, 384], BF16)
            SC_LOCAL = ("WIN", "CV", "dtr", "dtv", "dta", "dta30", "dta31", "dta32", "dta3t")
            H = al.get([128, 384], F32)
            V(lambda e: e.memset(H, 0.0), [], ["H"])
            load_uT(seq, L, 0, uts[0], ("uts", 0), "uts0")
            for sc in range(NSC):
                if sc + 1 < NSC:
                    load_uT(seq, L, sc + 1, uts[(sc + 1) % 2], ("uts", (sc + 1) % 2), "uts%d" % ((sc + 1) % 2))
                u, ukey = uts[sc % 2], ("uts", sc % 2)
                q_ = sc % 2
                WIN, CV, DTR, DTV, AV, A3, ATMP = WIN2[q_], CV2[q_], DTR2[q_], DTV2[q_], AV2[q_], A32[q_], ATMP2[q_]
                S.kmap.update({n_: q_ for n_ in SC_LOCAL})
                ssd_front(u, ukey, WIN, ACC, CV, "CV", sc, NSC, [0, 1, 2, 3, 4], (6, 7), ACCD, WIN2[1 - q_] if sc > 0 else None, q_)
                dt_block(u, ukey, DTR, DTV, AV, "dt", A3, ATMP)
                DMA(BCTs[:, :, 512 * sc:512 * (sc + 1)].rearrange("t k l -> k t l"), CV[:, 3:5, :], [("CV", 3), ("CV", 4)],
                    [("BCTs", sc)], "bct%d" % q_)
                DMA(DTPs[sc, :, 0:48], DTV.rearrange("p a b -> p (a b)"), ["dtv"], [("DTPs", sc, 0)], "dtpa%d" % q_)
                DMA(DTPs[sc, :, 48:96], AV.rearrange("p a b -> p (a b)"), ["dta"], [("DTPs", sc, 1)], "dtpb%d" % q_)
                for c in range(4):
                    j = 4 * sc + c
                    b = j % 2
                    to_token_major(CV, "CV", c, XSB[b], ("XSB", b))
                    DMA(XSBs[128 * j:128 * (j + 1), :], XSB[b], [("XSB", b)], [("XSBs", j)], "xsbw%d" % b)
                    for mi, m_ in enumerate((bGT, bONE)):
                        for i3 in range(3):
                            mm(pbf(2, 68 + 12 * mi, 80 + 12 * mi), m_, A3[:, i3, 12 * c:12 * c + 12], i3 == 0, i3 == 2,
                               ["masksb", "dta30", "dta31", "dta32"], [("pcs", mi)])
                    act("scalar", E[b], pbf(2, 68, 92).rearrange("p (a b) -> p a b", a=2), AF.Exp,
                        [("pcs", 0), ("pcs", 1)], [("E", b)])
                    tt("vector", SCL[b], E[b][:, 0, 0:6], DTV[:, c, 0:6], ALU.mult, [("E", b), "dtv"], [("SCL", b)])
                    tt("gpsimd", XD[b].rearrange("p (h d) -> p h d", h=6), XSB[b][:, 0:384].rearrange("p (h d) -> p h d", h=6),
                       SCL[b].unsqueeze(2).broadcast_to([128, 6, 64]), ALU.mult, [("XSB", b), ("SCL", b)], [("XD", b)])
                    mm(pbf(4 + b, 0, 384), XSB[b][:, 384:512], XD[b], True, True, [("XSB", b), ("XD", b)], [("pst", b)])
                    if j == (L // 128) // 2:
                        ts("vector", H, H, flag_col[:, 0:1], None, ALU.mult, None, ["H", "flag"], ["H"])
                    cp("scalar", Hst[:, j, :], H, ["H"], [("Hst", j)])
                    tt("vector", H.rearrange("p (h d) -> p h d", h=6), H.rearrange("p (h d) -> p h d", h=6),
                       E[b][:, 1, 0:6].unsqueeze(2).broadcast_to([128, 6, 64]), ALU.mult, ["H", ("E", b)], ["H"])
                    tt("vector", H, H, pbf(4 + b, 0, 384), ALU.add, ["H", ("pst", b)], ["H"])
            S.kmap.clear()
            S.barrier()

        def passB(seq, L, g):
            NSC = L // 512
            al = Alloc(PBASE)
            two = lambda shape, dt: [al.get(shape, dt) for _ in range(2)]
            uts = two([128, 8, 516], BF16)
            WIN2 = two([128, 5, 516], BF16)
            ACC = two([128, 512], F32)
            CV2 = two([128, 5, 512], BF16)
            XSB = two([128, 512], BF16)
            SZ = two([128, 512], F32)
            TZ = two([128, 512], F32)
            DTR2, DTV2, AV2 = two([128, 4, 12], F32), two([128, 4, 12], F32), two([128, 4, 12], F32)
            A32 = two([128, 3, 48], BF16)
            ATMP2 = two([128, 48], F32)
            E = two([128, 4, 12], F32)
            SCLB2 = two([128, 6], F32)
            RR2 = [two([128, 2, 6, 128], BF16) for _ in range(2)]
            CBM2 = [two([128, 128], F32) for _ in range(2)]
            DEC = two([128, 256], F32)
            MT2 = two([128, 2, 6, 128], BF16)
            XDT2 = [two([128, 384], BF16) for _ in range(2)]
            XSD2 = two([128, 384], BF16)
            XDB2 = two([128, 384], BF16)
            Gs = al.get([128, 384], F32)
            Gb = al.get([128, 384], BF16)
            T12, T22, YS2 = two([128, 384], F32), two([128, 384], F32), two([128, 384], F32)
            junk2 = two([128, 384], BF16)
            YFR = two([128, 128], F32)
            YG8 = [al.get([128, 384], F32) for _ in range(8)]
            YC8 = [al.get([128, 512], BF16) for _ in range(8)]
            ssq8 = al.get([128, 8], F32)
            rs8 = al.get([128, 8], F32)
            SC_LOCAL = ("WIN", "CV", "dtr", "dtv", "dta", "dta30", "dta31", "dta32", "dta3t")
            CH_LOCAL = ("CBM", "MT", "XDT", "XSD", "XDB", "SCLB", "RR", "T1", "T2a", "T2b", "YS", "junkb")
            V(lambda e: e.memset(Gs, 0.0), [], ["Gs"])
            V(lambda e: e.memset(Gb, 0.0), [], ["Gb"])
            order = list(range(NSC - 1, -1, -1))
            load_uT(seq, L, order[0], uts[0], ("uts", 0), "uts0")
            for oi, sc in enumerate(order):
                if oi + 1 < NSC:
                    load_uT(seq, L, order[oi + 1], uts[(oi + 1) % 2], ("uts", (oi + 1) % 2), "uts%d" % ((oi + 1) % 2))
                u, ukey = uts[oi % 2], ("uts", oi % 2)
                q_ = oi % 2
                WIN, CV, DTR, DTV, AV, A3, ATMP = WIN2[q_], CV2[q_], DTR2[q_], DTV2[q_], AV2[q_], A32[q_], ATMP2[q_]
                S.kmap.update({n_: q_ for n_ in SC_LOCAL})
                BC = CV[:, 3:5, :]
                DMA(BC, BCTs[:, :, 512 * sc:512 * (sc + 1)].rearrange("t k l -> k t l"), [], [("CV", 3), ("CV", 4)], "bcr%d" % q_)
                DMA(DTV.rearrange("p a b -> p (a b)"), DTPs[sc, :, 0:48], [], ["dtv"], "dtra%d" % q_)
                DMA(AV.rearrange("p a b -> p (a b)"), DTPs[sc, :, 48:96], [], ["dta"], "dtrb%d" % q_)
                split3("gpsimd", AV.rearrange("p a b -> p (a b)"), [A3[:, i, :] for i in range(3)], ATMP, ["dta"], "dta3")
                for c in range(3, -1, -1):
                    j = 4 * sc + c
                    b = j % 2
                    S.kmap.update({n_: b for n_ in CH_LOCAL})
                    RR, CBM, MT, XDT, XSD, XDB, SCLB = RR2[b], CBM2[b], MT2[b], XDT2[b], XSD2[b], XDB2[b], SCLB2[b]
                    T1, T2_, YS, junk = T12[b], T22[b], YS2[b], junk2[b]
                    DMA(YFR[b], YF[128 * j:128 * (j + 1), :], [], [("YFR", b)], "yfr%d" % b)
                    for kt in range(8):
                        mm(pbf(b, 0, 512), u[:, kt, 2 + 128 * c:2 + 128 * (c + 1)], Wg[:, kt, COL_Z:COL_Z + 512],
                           kt == 0, kt == 7, [ukey, ("Wg", kt)], [("pfm", b)])
                    act("scalar", TZ[b], pbf(b, 0, 512), AF.Tanh, [("pfm", b)], [("TZ", b)])
                    stt("vector", SZ[b], TZ[b], 1.0, pbf(b, 0, 512), ALU.add, ALU.mult, [("TZ", b), ("pfm", b)], [("SZ", b)])
                    DMA(XSB[b], XSBs[128 * j:128 * (j + 1), :], [], [("XSB", b)], "xsbr%d" % b)
                    for mi, m_ in enumerate((bLE, bGE, bLT, bONE)):
                        for i3 in range(3):
                            mm(pbf(2, 68 + 12 * mi, 80 + 12 * mi), m_, A3[:, i3, 12 * c:12 * c + 12], i3 == 0, i3 == 2,
                               ["masksb", "dta30", "dta31", "dta32"], [("pcs", mi)])
                    act("scalar", E[b], pbf(2, 68, 116).rearrange("p (a b) -> p a b", a=4), AF.Exp,
                        [("pcs", mi) for mi in range(4)], [("E", b)])
                    ef, eb_, dstb, cdb = E[b][:, 0, 0:6], E[b][:, 1, 6:12], E[b][:, 2, 6:12], E[b][:, 3, 6:12]
                    for d_ in range(2):
                        msk = bLE if d_ == 0 else bGE
                        for hl in range(2):
                            tt("gpsimd", RR[d_][:, hl, :, :], msk.unsqueeze(1).broadcast_to([128, 6, 128]),
                               A3[:, hl, 12 * c + 6 * d_:12 * c + 6 * d_ + 6].unsqueeze(2).broadcast_to([128, 6, 128]), ALU.mult,
                               ["masks", "dta30", "dta31"], [("RR", d_, hl)])
                    mm(pbf(2, 116, 244), CV[:, 3, 128 * c:128 * (c + 1)], CV[:, 4, 128 * c:128 * (c + 1)], True, True,
                       [("CV", 3), ("CV", 4)], ["pcb"])
                    tt("vector", CBM[0], pbf(2, 116, 244), bLE, ALU.mult, ["pcb", "masks"], [("CBM", 0)])
                    tt("vector", CBM[1], pbf(2, 116, 244), bGE, ALU.mult, ["pcb", "masks"], [("CBM", 1)])
                    it = 0
                    for d_ in range(2):
                        lt_ = gtb if d_ == 0 else ltb
                        for hp in range(3):
                            slot = it % 2
                            o = pbf(6 + slot, 0, 256)
                            for hl in range(2):
                                mm(o, lt_, RR[d_][:, hl, 2 * hp:2 * hp + 2, :].rearrange("p a b -> p (a b)"),
                                   hl == 0, hl == 1, ["gtb", "ltb", ("RR", d_, hl)], [("pseg", slot)])
                            act("scalar", DEC[slot], o, AF.Exp, [("pseg", slot)], [("DEC", slot)])
                            for hh in range(2):
                                h_ = 2 * hp + hh
                                stt("vector", MT[:, d_, h_, :], DEC[slot][:, 128 * hh:128 * (hh + 1)],
                                    DTV[:, c, 6 * d_ + h_:6 * d_ + h_ + 1], CBM[d_], ALU.mult, ALU.mult,
                                    [("DEC", slot), ("CBM", d_), "dtv"], [("MT", d_, hp)])
                            it += 1
                    for h in range(6):
                        ybk = 4
                        yo = pbf(ybk, 64 * h, 64 * (h + 1))
                        xh = XSB[b][:, 64 * h:64 * (h + 1)]
                        mm(yo, MT[:, 0, h, :], xh, True, False, [("MT", 0, h // 2), ("XSB", b)], [("py", b, h)])
                        mm(yo, MT[:, 1, h, :], xh, False, False, [("MT", 1, h // 2), ("XSB", b)], [("py", b, h)])
                        mm(yo, DgD[:, h, :], xh, False, True, [("DgD", h), ("XSB", b)], [("py", b, h)])
                    mm(pbf(5, 0, 384), CV[:, 4, 128 * c:128 * (c + 1)], Hst[:, j, :], True, True, [("CV", 4), ("Hst", j)], ["pzf"])
                    for h in range(6):
                        act("scalar", T1[:, 64 * h:64 * (h + 1)], pbf(5, 64 * h, 64 * (h + 1)), AF.Copy, ["pzf", ("E", b)], [("T1", h)],
                            scale=E[b][:, 0, h:h + 1])
                    mm(pbf(5, 0, 384), CV[:, 4, 128 * c:128 * (c + 1)], Gb, True, True, [("CV", 4), "Gb"], ["pzf"])
                    for h in range(6):
                        stt("vector", T1[:, 64 * h:64 * (h + 1)], pbf(5, 64 * h, 64 * (h + 1)), E[b][:, 1, 6 + h:7 + h],
                            T1[:, 64 * h:64 * (h + 1)], ALU.mult, ALU.add, ["pzf", ("E", b), ("T1", h)], [("T1", h)])
                    tt("vector", YS, pbf(4, 0, 384), T1, ALU.add, [("py", b, h) for h in range(6)] + [("T1", h) for h in range(6)], ["YS"])
                    tt("vector", SCLB, dstb, DTV[:, c, 6:12], ALU.mult, [("E", b), "dtv"], ["SCLB"])
                    tt("gpsimd", XDB.rearrange("p (h d) -> p h d", h=6), XSB[b][:, 0:384].rearrange("p (h d) -> p h d", h=6),
                       SCLB.unsqueeze(2).broadcast_to([128, 6, 64]), ALU.mult, [("XSB", b), "SCLB"], ["XDB"])
                    mm(pbf(3, 0, 384), XSB[b][:, 384:512], XDB, True, True, [("XSB", b), "XDB"], ["pst3"])
                    tt("vector", Gs.rearrange("p (h d) -> p h d", h=6), Gs.rearrange("p (h d) -> p h d", h=6),
                       cdb.unsqueeze(2).broadcast_to([128, 6, 64]), ALU.mult, ["Gs", ("E", b)], ["Gs"])
                    tt("vector", Gs, Gs, pbf(3, 0, 384), ALU.add, ["Gs", "pst3"], ["Gs"])
                    if j == (L // 128) // 2:
                        ts("vector", Gs, Gs, flag_col[:, 0:1], None, ALU.mult, None, ["Gs", "flag"], ["Gs"])
                    cp("scalar", Gb, Gs, ["Gs"], ["Gb"])
                    sl8 = 4 * q_ + c
                    tt("gpsimd", YG8[sl8], YS, SZ[b][:, 128:512], ALU.mult, ["YS", ("SZ", b)], [("YG8", sl8)])
                    act("scalar", junk, YG8[sl8], AF.Square, [("YG8", sl8)], ["junkb", ("ssq8", sl8)], accum_out=ssq8[:, sl8:sl8 + 1])
                    tt("gpsimd", YC8[sl8][:, 0:128], YFR[b], SZ[b][:, 0:128], ALU.mult, [("YFR", b), ("SZ", b)], [("YC8", sl8, 0)])
                    DMA(YCAT[128 * j:128 * (j + 1), 128 * g:128 * (g + 1)], YC8[sl8][:, 0:128], [("YC8", sl8, 0)], [("YCAT", j, 0)],
                        "yca%d" % sl8)
                rs4 = rs8[:, 4 * q_:4 * q_ + 4]
                ts("vector", rs4, ssq8[:, 4 * q_:4 * q_ + 4], 1.0 / 384, EPS, ALU.mult, ALU.add,
                   [("ssq8", 4 * q_ + c_) for c_ in range(4)], [("rs8", q_)])
                V(lambda e, rs4=rs4: e.reciprocal(out=rs4, in_=rs4), [("rs8", q_)], [("rs8", q_)])
                act("scalar", rs4, rs4, AF.Sqrt, [("rs8", q_)], [("rs8", q_)])
                for c in range(4):
                    j = 4 * sc + c
                    sl8 = 4 * q_ + c
                    stt("vector", YC8[sl8][:, 128:512], YG8[sl8], rs8[:, sl8:sl8 + 1], snw_bc, ALU.mult, ALU.mult,
                        [("YG8", sl8), ("rs8", q_), "snw"], [("YC8", sl8, 1)])
                    DMA(YCAT[128 * j:128 * (j + 1), 512 + 384 * g:512 + 384 * (g + 1)], YC8[sl8][:, 128:512], [("YC8", sl8, 1)],
                        [("YCAT", j, 1)], "ycb%d" % sl8)
            S.kmap.clear()
            S.barrier()

        def load_tail_weights():
            al = Alloc(HBASE)
            wo = al.get([128, 16, D], BF16)
            wg_ = al.get([128, 8, D], BF16)
            wp = al.get([128, 2, D], BF16)
            fnw = al.get([128, D], F32)
            wst = [al.get([128, D], F32) for _ in range(2)]
            i = 0
            for dst, src, n in ((wo, wout_d, 16), (wg_, wpg_d, 8), (wp, wpi_d, 2)):
                for kt in range(n):
                    b = i % 2
                    DMA(wst[b], src[128 * kt:128 * (kt + 1), :], [], [("wst", b)], "wst%d" % b)
                    if dst is wp:
                        ts("vector" if i % 2 == 0 else "gpsimd", dst[:, kt, :], wst[b], 0.5, None, ALU.mult, None, [("wst", b)], [("tw", i)])
                    else:
                        cp("vector" if i % 2 == 0 else "gpsimd", dst[:, kt, :], wst[b], [("wst", b)], [("tw", i)])
                    i += 1
            DMA(fnw, fnw_d.partition_broadcast(128), [], ["fnw"], "c0")
            S.barrier()
            return wo, wg_, wp, fnw, al

        def tail(seq, L, wo, wg_, wp, fnw, al):
            xt = [al.get([128, D], F32) for _ in range(2)]
            yc = [al.get([128, 2048], BF16) for _ in range(2)]
            pt = [al.get([128, 256], F32) for _ in range(2)]
            two = lambda shape, dt: [al.get(shape, dt) for _ in range(2)]
            ptb2, pT2, ycT2 = two([128, 256], BF16), two([128, 2, 128], BF16), two([128, 16, 128], BF16)
            h12, h1b2, h1T2 = two([128, D], F32), two([128, D], BF16), two([128, 8, 128], BF16)
            gate2, junk2 = two([128, D], F32), two([128, D], BF16)
            tq4 = [al.get([128, D], F32) for _ in range(4)]
            ssq4 = al.get([128, 4], F32)
            rs4 = al.get([128, 4], F32)
            T_LOCAL = ("ptb", "pT", "ycT", "h1", "h1b", "h1T", "gate", "junkb")
            ot = [al.get([128, D], F32) for _ in range(2)]
            for j in range(L // 128):
                b = j % 2
                S.kmap.update({n_: b for n_ in T_LOCAL})
                ptb, pT, ycT, h1, h1b, h1T = ptb2[b], pT2[b], ycT2[b], h12[b], h1b2[b], h1T2[b]
                gate, junk = gate2[b], junk2[b]
                s4 = j % 4
                tq = tq4[s4]
                DMA(xt[b], x_d[seq][128 * j:128 * (j + 1), :], [], [("xt", b)], "xt%d" % b)
                DMA(yc[b], YCAT[128 * j:128 * (j + 1), :], [], [("yc", b)], "ycl%d" % b)
                DMA(pt[b], pl_d[seq][128 * j:128 * (j + 1), :], [], [("pt", b)], "pt%d" % b)
                for half in range(2):
                    for i in range(8):
                        kt = 8 * half + i
                        tr(pbb(half, 128 * i, 128 * (i + 1)), yc[b][:, 128 * kt:128 * (kt + 1)], identb,
                           [("yc", b), "identb"], [("ptr", half)])
                    cp("vector" if half == 0 else "scalar", ycT[:, 8 * half:8 * half + 8, :],
                       pbb(half, 0, 1024).rearrange("p (a b) -> p a b", a=8), [("ptr", half)], [("ycT", half)])
                for nh in range(2):
                    for kt in range(16):
                        mm(pbf(2 + nh, 0, 512), ycT[:, kt, :], wo[:, kt, 512 * nh:512 * (nh + 1)], kt == 0, kt == 15,
                           [("ycT", kt // 8)], [("ph", nh)])
                    tt("vector", h1[:, 512 * nh:512 * (nh + 1)], pbf(2 + nh, 0, 512), xt[b][:, 512 * nh:512 * (nh + 1)], ALU.add,
                       [("ph", nh), ("xt", b)], [("h1", nh)])
                cp("scalar", h1b, h1, [("h1", 0), ("h1", 1)], ["h1b"])
                for kt in range(8):
                    tr(pbb(4, 128 * kt, 128 * (kt + 1)), h1b[:, 128 * kt:128 * (kt + 1)], identb, ["h1b", "identb"], ["ph1T"])
                cp("vector", h1T, pbb(4, 0, 1024).rearrange("p (a b) -> p a b", a=8), ["ph1T"], ["h1T"])
                cp("gpsimd", ptb, pt[b], [("pt", b)], ["ptb"])
                for kt in range(2):
                    tr(pbb(5, 128 * kt, 128 * (kt + 1)), ptb[:, 128 * kt:128 * (kt + 1)], identb, ["ptb", "identb"], ["ppT"])
                cp("vector", pT, pbb(5, 0, 256).rearrange("p (a b) -> p a b", a=2), ["ppT"], ["pT"])
                for nh in range(2):
                    for kt in range(8):
                        mm(pbf(6 + nh, 0, 512), h1T[:, kt, :], wg_[:, kt, 512 * nh:512 * (nh + 1)], kt == 0, kt == 7,
                           ["h1T"], [("pg", nh)])
                    act("scalar", gate[:, 512 * nh:512 * (nh + 1)], pbf(6 + nh, 0, 512), AF.Tanh, [("pg", nh)], [("gate", nh)], scale=0.5)
                    for kt in range(2):
                        mm(pbf(2 + nh, 0, 512), pT[:, kt, :], wp[:, kt, 512 * nh:512 * (nh + 1)], kt == 0, kt == 1,
                           ["pT"], [("ph", nh)])
                    stt("vector", tq[:, 512 * nh:512 * (nh + 1)], gate[:, 512 * nh:512 * (nh + 1)], 1.0, pbf(2 + nh, 0, 512),
                        ALU.add, ALU.mult, [("ph", nh), ("gate", nh)], [("tq4", s4, nh)])
                    tt("vector", tq[:, 512 * nh:512 * (nh + 1)], tq[:, 512 * nh:512 * (nh + 1)], h1[:, 512 * nh:512 * (nh + 1)],
                       ALU.add, [("tq4", s4, nh), ("h1", nh)], [("tq4", s4, nh)])
                act("scalar", junk, tq, AF.Square, [("tq4", s4, 0), ("tq4", s4, 1)], ["junkb", ("ssq4", s4)], accum_out=ssq4[:, s4:s4 + 1])
                if j % 2 == 1:
                    pr = (s4 // 2) * 2
                    rsp = rs4[:, pr:pr + 2]
                    ts("vector", rsp, ssq4[:, pr:pr + 2], 1.0 / D, EPS, ALU.mult, ALU.add, [("ssq4", pr), ("ssq4", pr + 1)], [("rs4", pr)])
                    V(lambda e, rsp=rsp: e.reciprocal(out=rsp, in_=rsp), [("rs4", pr)], [("rs4", pr)])
                    act("scalar", rsp, rsp, AF.Sqrt, [("rs4", pr)], [("rs4", pr)])
                    for jj in (j - 1, j):
                        sj = jj % 4
                        bb = jj % 2
                        stt("vector", ot[bb], tq4[sj], rs4[:, sj:sj + 1], fnw, ALU.mult, ALU.mult,
                            [("tq4", sj, 0), ("tq4", sj, 1), ("rs4", pr), "fnw"], [("ot", bb)])
                        DMA(y_d[seq][128 * jj:128 * (jj + 1), :], ot[bb], [("ot", bb)], [], "ot%d" % bb)
            S.kmap.clear()
            S.barrier()

        import os
        dbg = os.environ.get("KDEBUG", "")
        for seq, L in seqs:
            pass0(seq, L)
            S.barrier()
            for g in range(NG):
                if dbg and g > 0 and "4" not in dbg:
                    continue
                if dbg and "g" not in dbg:
                    continue
                load_group(g, L)
                if not dbg or "F" in dbg:
                    passF(seq, L)
                if not dbg or "A" in dbg:
                    passA(seq, L)
                if not dbg or "B" in dbg:
                    passB(seq, L, g)
            if not dbg or "T" in dbg:
                tw = load_tail_weights()
                tail(seq, L, *tw)
        S.emit(eng_sems, block)
    return nc


_CACHE = {}


def _prep_weights(w_in, conv_w, conv_b, a_log_f, a_log_b, dt_bias_f, dt_bias_b, d_skip, ssd_norm_w):
    w = w_in[0]
    cols = []
    for g in range(NG):
        idx = np.concatenate([
            np.arange(128 * g, 128 * g + 128),
            1024 + np.arange(384 * g, 384 * g + 384),
            512 + np.arange(128 * g, 128 * g + 128),
            2560 + np.arange(384 * g, 384 * g + 384),
            2560 + 1536 + np.arange(128 * g, 128 * g + 128),
            2560 + 2048 + np.arange(128 * g, 128 * g + 128),
            5120 + np.arange(6 * g, 6 * g + 6),
            5144 + np.arange(6 * g, 6 * g + 6),
        ])
        cols.append(idx)
    w_in_g = np.ascontiguousarray(np.stack([w[:, c] for c in cols], 0))
    cw, cbias = conv_w[0], conv_b[0]
    convw_g = np.zeros((NG, 128, 25), np.float32)
    convb_g = np.zeros((NG, 128, 5), np.float32)
    for g in range(NG):
        ch = np.concatenate([np.arange(384 * g, 384 * g + 384), 1536 + np.arange(128 * g, 128 * g + 128),
                             2048 + np.arange(128 * g, 128 * g + 128)])
        for t in range(5):
            cht = ch[128 * t:128 * (t + 1)]
            convw_g[g, :, 5 * t:5 * t + 5] = cw[:, cht].T
            convb_g[g, :, t] = cbias[cht]
    cbrow_g = np.ascontiguousarray(convb_g.transpose(0, 2, 1).reshape(NG, 640))
    sl = lambda v, g: v[0][6 * g:6 * g + 6]
    dtb_g = np.stack([np.concatenate([sl(dt_bias_f, g), sl(dt_bias_b, g)]) for g in range(NG)], 0)
    alog_g = np.stack([np.concatenate([sl(a_log_f, g), sl(a_log_b, g)]) for g in range(NG)], 0)
    dsk_g = np.stack([sl(d_skip, g) for g in range(NG)], 0)
    snw_g = np.ascontiguousarray(ssd_norm_w[0].reshape(NG, 384))
    return dict(w_in_g=w_in_g, convw_g=convw_g, convb_g=convb_g, cbrow_g=cbrow_g, dtb_g=np.ascontiguousarray(dtb_g, np.float32),
                alog_g=np.ascontiguousarray(alog_g, np.float32), dsk_g=np.ascontiguousarray(dsk_g, np.float32), snw_g=snw_g)


def run(x_prompt, x_sample, p_prompt, p_sample, norm_w, w_in, w_fmix, conv_w, conv_b, a_log_f, a_log_b,
        dt_bias_f, dt_bias_b, d_skip, ssd_norm_w, w_out, w_ple_in, w_ple_gate, final_norm_w):
    f = lambda a: np.ascontiguousarray(np.asarray(a, dtype=np.float32))
    x_prompt, x_sample, p_prompt, p_sample = f(x_prompt), f(x_sample), f(p_prompt), f(p_sample)
    Bp, Lp, _ = x_prompt.shape
    Bs, Ls, _ = x_sample.shape
    assert Ls == 2 * Lp
    if Ls not in _CACHE:
        _CACHE[Ls] = build(Ls)
    nc = _CACHE[Ls]
    common = _prep_weights(f(w_in), f(conv_w), f(conv_b), f(a_log_f), f(a_log_b), f(dt_bias_f), f(dt_bias_b),
                           f(d_skip), f(ssd_norm_w))
    masks, c64 = _masks()
    cs1, t21 = _consts(Ls, False)
    cs2, t22 = _consts(Ls, True)
    common.update(wfm=f(w_fmix)[0], w_out=f(w_out)[0], w_ple_in=f(w_ple_in)[0], w_ple_gate=f(w_ple_gate)[0],
                  fnw=f(final_norm_w), normw=f(norm_w)[0], masks=masks, c64=c64)
    import os
    n_dual = int(os.environ.get("KNDUAL", 8 - Bs))
    slots = []
    for c in range(Bs):
        slots.append(("s", c, None))
    pr = list(range(Bp))
    pairs = [[] for _ in range(n_dual)]
    for i, b in enumerate(pr):
        pairs[i % n_dual].append(b)
    assert all(len(p_) <= 2 for p_ in pairs), "prompt batch does not fit the slots"
    for p_ in pairs:
        a = p_[0] if len(p_) > 0 else 0
        b = p_[1] if len(p_) > 1 else a
        slots.append(("p", a, b if len(p_) > 1 else None, b))
    in_maps = []
    for sl in slots:
        m = dict(common)
        if sl[0] == "s":
            m["x_s"] = x_sample[sl[1]]
            m["pl_s"] = p_sample[0, sl[1]]
            m["cs_s"], m["t2_s"] = cs1, t21
            m["flag"] = np.ones((1,), np.float32)
        else:
            a, b = sl[1], sl[3]
            m["x_s"] = np.ascontiguousarray(np.concatenate([x_prompt[a], x_prompt[b]], 0))
            m["pl_s"] = np.ascontiguousarray(np.concatenate([p_prompt[0, a], p_prompt[0, b]], 0))
            m["cs_s"], m["t2_s"] = cs2, t22
            m["flag"] = np.zeros((1,), np.float32)
        in_maps.append(m)
    while len(in_maps) < 8:
        in_maps.append(in_maps[-1])
    res = run_bass_kernel_spmd(nc, in_maps, core_ids=list(range(8)))
    ys = np.stack([res.results[c]["y_s"] for c in range(Bs)], 0)
    yp = np.zeros((Bp, Lp, D), np.float32)
    for c, sl in enumerate(slots):
        if sl[0] == "p":
            o = res.results[c]["y_s"]
            yp[sl[1]] = o[:Lp]
            if sl[2] is not None:
                yp[sl[2]] = o[Lp:]
    return yp, ys.astype(np.float32)


def kernel(**inputs):
    return run(**inputs)
```
